# Optimizing a Trainium2 kernel written in Bass

```python
import jax, jax.numpy as jnp
from jax import lax
import numpy as np

D_MODEL = 1024
BATCH = 16
SEQ = 2048
DEPTH = 2

N_META = 16
EPS = 1e-6
SSM_D_INNER = 2 * D_MODEL
SSM_HEAD_DIM = 64
SSM_HEADS = SSM_D_INNER // SSM_HEAD_DIM
SSM_GROUPS = 4
SSM_STATE = 128
SSM_CONV = 4
SSM_CHUNK = 128
SSM_CONV_DIM = SSM_D_INNER + 2 * SSM_GROUPS * SSM_STATE
MLA_HEADS = 8
MLA_Q_LORA = D_MODEL // 2
MLA_KV_LORA = D_MODEL // 4
MLA_NOPE = 128
MLA_ROPE = 64
MLA_V = 128
ROPE_THETA = 10000.0
Q_BLOCK = 128
D_FF = 4 * D_MODEL
IN_SPLITS = [SSM_D_INNER, SSM_CONV_DIM, SSM_HEADS, MLA_Q_LORA, MLA_KV_LORA, MLA_ROPE, D_MODEL, D_MODEL]
IN_DIM = sum(IN_SPLITS)

kernel_name = "hybrid_ssd_mla_meta_block"


def rms_norm(x, w):
    xf = x.astype(jnp.float32)
    y = xf * lax.rsqrt(jnp.mean(xf * xf, axis=-1, keepdims=True) + EPS)
    return (y * w.astype(jnp.float32)).astype(x.dtype)


def rope_tables(n_pos, dim):
    inv = ROPE_THETA ** (-jnp.arange(0, dim, 2, dtype=jnp.float32) / dim)
    ang = jnp.arange(n_pos, dtype=jnp.float32)[:, None] * inv[None, :]
    return jnp.cos(ang), jnp.sin(ang)


def apply_rope(x, cos, sin):
    x1, x2 = jnp.split(x.astype(jnp.float32), 2, axis=-1)
    return jnp.concatenate([x1 * cos - x2 * sin, x2 * cos + x1 * sin], axis=-1).astype(x.dtype)


def causal_dwconv(x, w, b):
    k, c = w.shape
    y = lax.conv_general_dilated(x, w[:, None, :].astype(x.dtype), window_strides=(1,),
                                 padding=[(k - 1, 0)], dimension_numbers=("NWC", "WIO", "NWC"),
                                 feature_group_count=c)
    return y + b.astype(x.dtype)


def ssd_chunked(xdt, adt, bm, cm):
    b, t, h, p = xdt.shape
    g, n = bm.shape[-2:]
    e = h // g
    q = SSM_CHUNK
    c = t // q
    xc = xdt.reshape(b, c, q, g, e, p)
    a = adt.astype(jnp.float32).reshape(b, c, q, g, e).transpose(0, 3, 4, 1, 2)
    bc = bm.reshape(b, c, q, g, n)
    cc = cm.reshape(b, c, q, g, n)
    a_cs = jnp.cumsum(a, axis=-1)
    causal = np.tril(np.ones((q, q), dtype=bool))
    l_dec = jnp.exp(jnp.where(causal, a_cs[..., :, None] - a_cs[..., None, :], -jnp.inf))
    cb = jnp.einsum("bclgn,bcsgn->bcgls", cc, bc)
    y_diag = jnp.einsum("bcgls,bgecls,bcsgep->bclgep", cb, l_dec, xc)
    decay_states = jnp.exp(a_cs[..., -1:] - a_cs)
    states = jnp.einsum("bcsgn,bgecs,bcsgep->bcgepn", bc, decay_states, xc)
    chunk_decay = jnp.exp(a_cs[..., -1])

    def step(hs, inp):
        s_c, d_c = inp
        return hs * d_c[..., None, None] + s_c, hs

    h0 = jnp.zeros((b, g, e, p, n), jnp.float32)
    _, prev = lax.scan(step, h0, (states.astype(jnp.float32).transpose(1, 0, 2, 3, 4, 5),
                                  chunk_decay.transpose(3, 0, 1, 2)))
    prev = prev.transpose(1, 0, 2, 3, 4, 5)
    y_off = jnp.einsum("bclgn,bcgepn,bgecl->bclgep", cc, prev, jnp.exp(a_cs))
    return (y_diag + y_off).reshape(b, t, h, p).astype(xdt.dtype)


def ssd_mixer(z, xbc, dt, conv_w, conv_b, dt_bias, a_log, d_skip, norm_w):
    bsz, L, _ = xbc.shape
    xbc = jax.nn.silu(causal_dwconv(xbc, conv_w, conv_b))
    xs, bm, cm = jnp.split(xbc, [SSM_D_INNER, SSM_D_INNER + SSM_GROUPS * SSM_STATE], axis=-1)
    xs = xs.reshape(bsz, L, SSM_HEADS, SSM_HEAD_DIM)
    bm = bm.reshape(bsz, L, SSM_GROUPS, SSM_STATE)
    cm = cm.reshape(bsz, L, SSM_GROUPS, SSM_STATE)
    dt = jax.nn.softplus(dt.astype(jnp.float32) + dt_bias.astype(jnp.float32))
    a = -jnp.exp(a_log.astype(jnp.float32))
    pad = SSM_CHUNK - N_META

    def lpad(t):
        return jnp.pad(t, [(0, 0), (pad, 0)] + [(0, 0)] * (t.ndim - 2))

    y = ssd_chunked(lpad(xs * dt[..., None]), lpad(dt * a), lpad(bm), lpad(cm))[:, pad:]
    y = y + xs * d_skip[:, None].astype(xs.dtype)
    y = y.reshape(bsz, L, SSM_D_INNER) * jax.nn.silu(z)
    gsz = SSM_D_INNER // SSM_GROUPS
    y = rms_norm(y.reshape(bsz, L, SSM_GROUPS, gsz), norm_w.reshape(SSM_GROUPS, gsz))
    return y.reshape(bsz, L, SSM_D_INNER)


def mla_mixer(c_q, c_kv, k_rope, q_norm_w, kv_norm_w, w_uq, w_ukv, cos, sin):
    bsz, L, _ = c_q.shape
    qf = (rms_norm(c_q, q_norm_w) @ w_uq).reshape(bsz, L, MLA_HEADS, MLA_NOPE + MLA_ROPE)
    q_nope, q_pe = jnp.split(qf, [MLA_NOPE], axis=-1)
    q_pe = apply_rope(q_pe, cos[:, None, :], sin[:, None, :])
    kv = (rms_norm(c_kv, kv_norm_w) @ w_ukv).reshape(bsz, L, MLA_HEADS, MLA_NOPE + MLA_V)
    k_nope, v = jnp.split(kv, [MLA_NOPE], axis=-1)
    k_pe = apply_rope(k_rope, cos, sin)
    scale = (MLA_NOPE + MLA_ROPE) ** -0.5
    bounds = [0, N_META] + list(range(N_META + Q_BLOCK, L + 1, Q_BLOCK))
    outs = []
    for qs, qe in zip(bounds[:-1], bounds[1:]):
        s = (jnp.einsum("bqhd,bkhd->bhqk", q_nope[:, qs:qe], k_nope[:, :qe])
             + jnp.einsum("bqhr,bkr->bhqk", q_pe[:, qs:qe], k_pe[:, :qe])).astype(jnp.float32) * scale
        mask = np.arange(qs, qe)[:, None] >= np.arange(qe)[None, :]
        p = jax.nn.softmax(jnp.where(mask, s, -jnp.inf), axis=-1).astype(v.dtype)
        outs.append(jnp.einsum("bhqk,bkhv->bqhv", p, v[:, :qe]))
    o = jnp.concatenate(outs, axis=1)
    return o.reshape(bsz, L, MLA_HEADS * MLA_V)


def setup_inputs(seed: int = 0) -> dict:
    key = jax.random.key(seed)
    ks = jax.random.split(key, 24)
    f32 = jnp.float32
    nrm = lambda k, shape, s: jax.random.normal(k, shape, f32) * s
    gain = lambda k, shape: 1.0 + 0.02 * jax.random.normal(k, shape, f32)
    res_scale = (2 * DEPTH) ** -0.5
    dt0 = jnp.exp(jax.random.uniform(ks[5], (DEPTH, SSM_HEADS), f32, np.log(1e-3), np.log(1e-1)))
    dt_bias = dt0 + jnp.log(-jnp.expm1(-dt0))
    return {
        "x": nrm(ks[0], (BATCH, SEQ, D_MODEL), 1.0),
        "meta_tokens": nrm(ks[1], (N_META, D_MODEL), 1.0),
        "norm_mix_w": gain(ks[2], (DEPTH, D_MODEL)),
        "w_in": nrm(ks[3], (DEPTH, D_MODEL, IN_DIM), D_MODEL ** -0.5),
        "conv_w": nrm(ks[4], (DEPTH, SSM_CONV, SSM_CONV_DIM), SSM_CONV ** -0.5),
        "conv_b": nrm(ks[6], (DEPTH, SSM_CONV_DIM), 0.01),
        "dt_bias": dt_bias,
        "a_log": jnp.log(jax.random.uniform(ks[7], (DEPTH, SSM_HEADS), f32, 1.0, 16.0)),
        "d_skip": 1.0 + 0.1 * jax.random.normal(ks[8], (DEPTH, SSM_HEADS), f32),
        "ssm_norm_w": gain(ks[9], (DEPTH, SSM_D_INNER)),
        "q_norm_w": gain(ks[10], (DEPTH, MLA_Q_LORA)),
        "kv_norm_w": gain(ks[11], (DEPTH, MLA_KV_LORA)),
        "w_uq": nrm(ks[12], (DEPTH, MLA_Q_LORA, MLA_HEADS * (MLA_NOPE + MLA_ROPE)), MLA_Q_LORA ** -0.5),
        "w_ukv": nrm(ks[13], (DEPTH, MLA_KV_LORA, MLA_HEADS * (MLA_NOPE + MLA_V)), MLA_KV_LORA ** -0.5),
        "w_branch_ssm": nrm(ks[14], (DEPTH, SSM_D_INNER, D_MODEL), SSM_D_INNER ** -0.5),
        "w_branch_mla": nrm(ks[15], (DEPTH, MLA_HEADS * MLA_V, D_MODEL), (MLA_HEADS * MLA_V) ** -0.5),
        "w_out": nrm(ks[16], (DEPTH, D_MODEL, D_MODEL), D_MODEL ** -0.5 * res_scale),
        "norm_mlp_w": gain(ks[17], (DEPTH, D_MODEL)),
        "w_mlp_up": nrm(ks[18], (DEPTH, D_MODEL, D_FF), D_MODEL ** -0.5),
        "w_mlp_down": nrm(ks[19], (DEPTH, D_FF, D_MODEL), D_FF ** -0.5 * res_scale),
        "final_norm_w": gain(ks[20], (D_MODEL,)),
    }


def reference(x, meta_tokens, norm_mix_w, w_in, conv_w, conv_b, dt_bias, a_log, d_skip, ssm_norm_w,
              q_norm_w, kv_norm_w, w_uq, w_ukv, w_branch_ssm, w_branch_mla, w_out, norm_mlp_w,
              w_mlp_up, w_mlp_down, final_norm_w):
    bsz = x.shape[0]
    meta = jnp.broadcast_to(meta_tokens.astype(x.dtype)[None], (bsz, N_META, D_MODEL))
    h = jnp.concatenate([meta, x], axis=1)
    L = h.shape[1]
    cos, sin = rope_tables(L, MLA_ROPE)
    split_idx = np.cumsum(IN_SPLITS)[:-1].tolist()
    for i in range(DEPTH):
        u = rms_norm(h, norm_mix_w[i])
        z, xbc, dt, c_q, c_kv, k_rope, g_ssm, g_mla = jnp.split(u @ w_in[i], split_idx, axis=-1)
        y_ssm = ssd_mixer(z, xbc, dt, conv_w[i], conv_b[i], dt_bias[i], a_log[i], d_skip[i], ssm_norm_w[i])
        y_mla = mla_mixer(c_q, c_kv, k_rope, q_norm_w[i], kv_norm_w[i], w_uq[i], w_ukv[i], cos, sin)
        mixed = (jax.nn.sigmoid(g_ssm) * (y_ssm @ w_branch_ssm[i])
                 + jax.nn.sigmoid(g_mla) * (y_mla @ w_branch_mla[i]))
        h = h + mixed @ w_out[i]
        v = rms_norm(h, norm_mlp_w[i])
        h = h + jnp.square(jax.nn.relu(v @ w_mlp_up[i])) @ w_mlp_down[i]
    return rms_norm(h, final_norm_w)[:, N_META:]
```

```python
import contextlib
import numpy as np
import concourse.bass as bass
import concourse.mybir as mybir
from concourse.bass_utils import run_bass_kernel_spmd

F32 = mybir.dt.float32
BF16 = mybir.dt.bfloat16
AF = mybir.ActivationFunctionType
ALU = mybir.AluOpType
AX = mybir.AxisListType

COMPUTE = ("pe", "act", "dve", "pool")
SAME_ENGINE_SYNC = True


class T:
    def __init__(self, name, handle):
        self.name = name
        self.h = handle
        self.writers = []
        self.readers = []
        self.dsem = None
        self.dcount = 0

    def __getitem__(self, k):
        return self.h[k]


class TG:
    def __init__(self, name, handle, n):
        self.name = name
        self.h = handle
        self.subs = [T(f"{name}_s{j}", handle) for j in range(n)]

    def __getitem__(self, k):
        return self.h[k]

    def sub(self, j):
        return self.subs[j]


def _flat(lst):
    out = []
    for t in lst:
        if isinstance(t, TG):
            out += t.subs
        elif isinstance(t, (list, tuple)):
            out += _flat(t)
        else:
            out.append(t)
    return out


def _flatp(lst):
    out = []
    for (t, k) in lst:
        if isinstance(t, TG):
            out += [(x, k) for x in t.subs]
        else:
            out.append((t, k))
    return out


class Op:
    __slots__ = ("eng", "fn", "deps", "idx", "signal", "dma", "dtile", "dval", "waits", "semval")

    def __init__(self, eng, fn):
        self.eng = eng
        self.fn = fn
        self.deps = []
        self.idx = -1
        self.signal = False
        self.dma = False
        self.dtile = None
        self.dval = 0
        self.waits = []
        self.semval = 0


class Prog:
    def __init__(self, nc):
        self.nc = nc
        self.es = contextlib.ExitStack()
        self.streams = {e: [] for e in ("pe", "act", "dve", "pool", "sp")}
        self.known = {e: {} for e in self.streams}
        self.sems = {}

    def sb(self, name, shape, dt):
        h = self.es.enter_context(self.nc.sbuf_tensor(name, list(shape), dt))
        return T(name, h)

    def sbg(self, name, shape, dt, n):
        h = self.es.enter_context(self.nc.sbuf_tensor(name, list(shape), dt))
        return TG(name, h, n)

    def ps(self, name, shape, dt=F32):
        h = self.es.enter_context(self.nc.psum_tensor(name, list(shape), dt))
        return T(name, h)

    def dram(self, name, shape, dt, kind="Internal"):
        h = self.nc.dram_tensor(name, list(shape), dt, kind=kind)
        return T(name, h.ap())

    def sem(self, name):
        if name not in self.sems:
            self.sems[name] = self.es.enter_context(self.nc.semaphore(name))
        return self.sems[name]

    def _add(self, eng, fn, r=(), w=(), wp=(), dma=False):
        op = Op(eng, fn)
        op.dma = dma
        r, w, wp = _flat(r), _flat(w), _flatp(wp)
        deps = []
        raw = set()
        for t in r:
            deps += [o for (o, k) in t.writers]
            raw.update(id(o) for (o, k) in t.writers)
        for t in w:
            deps += [o for (o, k) in t.writers]
            deps += t.readers
        for (t, key) in wp:
            deps += t.readers
            deps += [o for (o, k) in t.writers if k is None or k == key]
        for t in r:
            t.readers.append(op)
        for t in w:
            t.writers = [(op, None)]
            t.readers = []
        for (t, key) in wp:
            if t.readers:
                t.writers = [(op, key)]
                t.readers = []
            else:
                t.writers = [(o, k) for (o, k) in t.writers if k != key] + [(op, key)]
        if dma:
            dts = [t for t in w] + [t for (t, k) in wp]
            assert len(dts) == 1
            t = dts[0]
            if t.dsem is None:
                t.dsem = self.sem("d_" + t.name)
            t.dcount += 16
            op.dtile = t
            op.dval = t.dcount
        op.idx = len(self.streams[eng])
        kn = self.known[eng]
        best = {}
        for d in deps:
            if d is op:
                continue
            if d.dma:
                key = ("dma", d.dtile.name)
                val = d.dval
            else:
                if d.eng == eng and (eng == "pe" or not SAME_ENGINE_SYNC or id(d) not in raw
                                     or (eng in ("dve", "act") and d.idx <= op.idx - 2)):
                    continue
                key = d.eng
                val = d.idx
            if kn.get(key, -1) >= val:
                continue
            if key not in best or best[key][0] < val:
                best[key] = (val, d)
        for key, (val, d) in best.items():
            kn[key] = val
            if not d.dma:
                d.signal = True
            op.deps.append(d)
        self.streams[eng].append(op)
        return op

    def pe(self, fn, r=(), w=(), wp=()):
        return self._add("pe", fn, r, w, wp)

    def act(self, fn, r=(), w=(), wp=()):
        return self._add("act", fn, r, w, wp)

    def dve(self, fn, r=(), w=(), wp=()):
        return self._add("dve", fn, r, w, wp)

    def pool(self, fn, r=(), w=(), wp=()):
        return self._add("pool", fn, r, w, wp)

    def dma(self, out_t, out_ap, in_t, in_ap, q="sp", partial=None, **kw):
        fn = lambda e: e.dma_start(out=out_ap, in_=in_ap, **kw)
        r = [in_t] if in_t is not None else []
        if partial is None:
            return self._add(q, fn, r=r, w=[out_t], dma=True)
        return self._add(q, fn, r=r, wp=[(out_t, partial)], dma=True)

    def finish(self, final_waits=()):
        nc = self.nc
        esem = {e: self.sem("e_" + e) for e in COMPUTE}
        self.maxsem = 0
        for e in COMPUTE:
            c = 0
            for op in self.streams[e]:
                if op.signal:
                    c += 1
                    op.semval = c
            self.maxsem = max(self.maxsem, c)
        for e, ops in self.streams.items():
            for op in ops:
                for d in op.deps:
                    if d.dma:
                        op.waits.append((d.dtile.dsem, d.dval))
                    else:
                        op.waits.append((esem[d.eng], d.semval))
        engmap = {"pe": "tensor", "act": "scalar", "dve": "vector", "pool": "gpsimd", "sp": "sync"}
        with nc.Block() as block:
            for e, ops in self.streams.items():
                def body(eng, e=e, ops=ops):
                    for op in ops:
                        for (s, v) in op.waits:
                            eng.wait_ge(s, v)
                        ins = op.fn(eng)
                        if op.dma:
                            ins.then_inc(op.dtile.dsem, 16)
                        elif op.signal:
                            ins.then_inc(esem[e], 1)
                    if e == "sp":
                        for t in final_waits:
                            eng.wait_ge(t.dsem, t.dcount)
                getattr(block, engmap[e])(body)
        self.es.close()


D = 1024
NMETA = 16
EPS = 1e-6
SCALE = 192 ** -0.5
WSLOT = 4352
NEGBIG = -30000.0

C_Z, C_XBC, C_DT, C_CQ, C_CKV, C_KR, C_GS, C_GM = 0, 2048, 5120, 5152, 5664, 5920, 5984, 7008


def vec_layout(depth):
    off = {}
    n = 0

    def add(name, w):
        nonlocal n
        off[name] = n
        n += w
    for l in range(depth):
        for nm, w in (("nmix", 8), ("nmlp", 8), ("convw", 96), ("convb", 24), ("ssmw", 16), ("dskip", 16),
                      ("qnw", 4), ("kvnw", 2), ("dtb", 1), ("alog", 1)):
            add(f"{nm}{l}", w)
    add("fnw", 8)
    add("eps", 1)
    add("one", 1)
    return off, n


class MK:
    def __init__(self, nseq=2, depth=2, nblk=4):
        self.nseq, self.depth, self.nblk = nseq, depth, nblk
        self.Lt = NMETA + 512 * nblk
        self.ntile = 1 + 4 * nblk
        nc = bass.Bass("TRN2", target_bir_lowering=False)
        self.nc = nc
        P = Prog(nc)
        self.P = P
        Lt = self.Lt
        self.voff, self.NV = vec_layout(depth)
        di = lambda n, s: P.dram(n, s, F32, kind="ExternalInput")
        self.x = di("x", [nseq, 512 * nblk, D])
        self.meta = di("meta", [NMETA, D])
        self.w_in = di("w_in", [depth, D, 8032])
        self.w_kr2 = di("w_kr2", [depth, D, 128])
        self.w_uq2 = di("w_uq2", [depth, 512, 2048])
        self.w_uk = di("w_uk", [depth, 256, 1024])
        self.w_uv = di("w_uv", [depth, 256, 1024])
        self.w_bs = di("w_bs", [depth, 2048, D])
        self.w_bm = di("w_bm", [depth, D, D])
        self.w_out = di("w_out", [depth, D, D])
        self.w_up = di("w_up", [depth, D, 4096])
        self.w_dn = di("w_dn", [depth, 4096, D])
        self.vecs_d = di("vecs", [128, self.NV])
        self.cos_d = di("cos2", [64, Lt])
        self.sin_d = di("sin2", [64, Lt])
        self.wsrc = {"w_in": self.w_in, "w_kr2": self.w_kr2, "w_uq2": self.w_uq2, "w_uk": self.w_uk, "w_uv": self.w_uv,
                     "w_bs": self.w_bs, "w_bm": self.w_bm, "w_out": self.w_out, "w_up": self.w_up, "w_dn": self.w_dn}
        self.wb = {}
        for nm, t in self.wsrc.items():
            shp = list(t.h.shape)
            self.wb[nm] = [P.dram(f"{nm}_b{l}", shp[1:], BF16, kind="Internal") for l in range(depth)]
        self.out = P.dram("out", [nseq, 512 * nblk, D], F32, kind="ExternalOutput")
        self.h1 = P.dram("h1s", [128, 8, Lt], F32, kind="Internal")
        self.mk = [P.dram(f"mk{l}", [128, 8, NMETA], BF16) for l in range(depth)]
        self.mv = [P.dram(f"mv{l}", [NMETA, 1024], BF16) for l in range(depth)]
        self.mp = [P.dram(f"mp{l}", [64, NMETA], BF16) for l in range(depth)]
        self.ms = [P.dram(f"ms{l}", [128, 2048], F32) for l in range(depth)]
        self.mh = [P.dram(f"mh{l}", [128, 72], F32) for l in range(depth)]
        self.mm = [P.dram(f"mm{l}", [65, 1], F32) for l in range(depth)]

        sb = P.sb
        self.ident_f = sb("ident_f", [128, 128], F32)
        self.ident_b = sb("ident_b", [128, 128], BF16)
        self.ones_b = sb("ones_b", [128, 128], BF16)
        self.ones_f = sb("ones_f", [128, 128], F32)
        self.tri_b = sb("tri_b", [128, 128], BF16)
        self.tri_f = sb("tri_f", [128, 128], F32)
        self.triU_f = sb("triU_f", [128, 128], F32)
        self.neg_f = sb("neg_f", [128, 128], F32)
        self.vecs = sb("vecs_s", [128, self.NV], F32)
        self.h = sb("h", [128, 8, 512], F32)
        self.u = P.sbg("u", [128, 8, 512], BF16, 8)
        self.ws = [sb(f"ws{i}", [128, WSLOT], BF16) for i in range(3)]
        self.wi = 0
        self.kc = sb("kc", [128, 8, Lt], BF16)
        self.vc = sb("vc", [128, self.ntile, 1024], BF16)
        self.kpe = sb("kpe", [65, Lt], BF16)
        self.state_f = sb("state_f", [128, 2048], F32)
        self.state_b = sb("state_b", [128, 2048], BF16)
        self.kmax2 = sb("kmax2", [65, 1], F32)
        self.kmtmp = sb("kmtmp", [65, 1], F32)
        self.big = [P.sbg(f"big{i}", [128, 8, 512], BF16, 4) for i in range(3)]
        self.sz = [sb(f"sz{i}", [128, 8, 512], BF16) for i in range(2)]
        self.tok = [sb(f"tok{i}", [128, 1024], F32) for i in range(2)]
        self.sm = [sb(f"sm{i}", [128, 4, 128], BF16) for i in range(6)]
        self.tmp = [sb(f"tmp{i}", [128, 512], F32) for i in range(2)]
        self.ti = 0
        self.dta = sb("dta", [128, 4, 64], F32)
        self.acsE2 = [sb(f"acsE{i}", [128, 96], F32) for i in range(2)]
        self.cbT2 = [self.sm[4], sb("cbTb", [128, 4, 128], BF16)]
        self.negA = sb("negA", [32, 1], F32)
        self.a42 = [sb(f"a4{i}", [128, 4, 32], BF16) for i in range(2)]
        self.neg_b = sb("neg_b", [128, 128], BF16)
        self.xpre = sb("xpre", [128, 515], F32)
        self.acc = sb("acc", [128, 512], F32)
        self.halo = sb("halo", [128, 24, 3], F32)
        self.halo_c = [T(f"halo_c{c}", self.halo.h) for c in range(24)]
        self.PT = [sb(f"PT{i}", [128, 512], BF16) for i in range(4)]
        self.pti = 0
        self.bank = [P.ps(f"bk{i}", [128, 512], F32) for i in range(8)]
        self.bi = 0
        self.dbg_outs = []
        self.chunk_ctr = 0
        self.castq = []

    def dump(self, name, t, ap, dt):
        if not getattr(self, "debug", False):
            return
        shape = list(ap.shape)
        d = self.P.dram("dbg_" + name, shape, dt, kind="ExternalOutput")
        self.P.dma(d, d[:], t, ap)
        self.dbg_outs.append(d)

    def nb(self):
        b = self.bank[self.bi % 4]
        self.bi += 1
        return b

    def nt_(self):
        t = self.tmp[self.ti % 2]
        self.ti += 1
        return t

    def vcol(self, name, c=0, rows=128):
        o = self.voff[name] + c
        return self.vecs[0:rows, o:o + 1]

    def consts(self):
        P = self
        Pr = self.P
        Pr.dma(self.vecs, self.vecs[:], self.vecs_d, self.vecs_d[:])
        for t, dt in ((self.ident_f, F32), (self.ident_b, BF16)):
            Pr.pool(lambda e, t=t: e.memset(t[:], 1.0), w=[t])
            Pr.pool(lambda e, t=t: e.affine_select(out=t[:], in_=t[:], pattern=[[-1, 128]], compare_op=ALU.is_equal,
                                                   fill=0.0, base=0, channel_multiplier=1), r=[t], w=[t])
        for t in (self.ones_b, self.ones_f):
            Pr.pool(lambda e, t=t: e.memset(t[:], 1.0), w=[t])
        for t in (self.tri_b, self.tri_f):
            Pr.pool(lambda e, t=t: e.memset(t[:], 1.0), w=[t])
            Pr.pool(lambda e, t=t: e.affine_select(out=t[:], in_=t[:], pattern=[[1, 128]], compare_op=ALU.is_ge,
                                                   fill=0.0, base=0, channel_multiplier=-1), r=[t], w=[t])
        t = self.triU_f
        Pr.pool(lambda e, t=t: e.memset(t[:], 1.0), w=[t])
        Pr.pool(lambda e, t=t: e.affine_select(out=t[:], in_=t[:], pattern=[[-1, 128]], compare_op=ALU.is_gt,
                                               fill=0.0, base=0, channel_multiplier=1), r=[t], w=[t])
        t = self.neg_f
        Pr.pool(lambda e, t=t: e.memset(t[:], 0.0), w=[t])
        Pr.pool(lambda e, t=t: e.affine_select(out=t[:], in_=t[:], pattern=[[1, 128]], compare_op=ALU.is_ge,
                                               fill=NEGBIG, base=0, channel_multiplier=-1), r=[t], w=[t])
        Pr.pool(lambda e: e.tensor_copy(self.neg_b[:], self.neg_f[:]), r=[self.neg_f], w=[self.neg_b])
        self.cast_weights(0, ["w_in", "w_kr2", "w_bs", "w_uk", "w_uv", "w_uq2", "w_bm", "w_out", "w_up", "w_dn"])
        Pr.pool(lambda e: e.memset(self.kpe[64:65, :], 1.0), wp=[(self.kpe, "ones")])

    def cast_queue(self, l, names):
        for nm in names:
            src = self.wsrc[nm]
            dst = self.wb[nm][l]
            R_, C_ = dst.h.shape
            for r0 in range(0, R_, 1024):
                r1 = min(R_, r0 + 1024)
                for c0 in range(0, C_, 1024):
                    c1 = min(C_, c0 + 1024)
                    self.castq.append((dst, dst[r0:r1, c0:c1], src, src[l][r0:r1, c0:c1], (r0, c0)))

    def cast_some(self, n):
        while n > 0 and self.castq:
            dst, dap, src, sap, key = self.castq.pop(0)
            self.P.dma(dst, dap, src, sap, q="pool", partial=key)
            n -= 1

    def cast_weights(self, l, names):
        for nm in names:
            src = self.wsrc[nm]
            dst = self.wb[nm][l]
            R_, C_ = dst.h.shape
            for c0 in range(0, C_, 2048):
                c1 = min(C_, c0 + 2048)
                self.P.dma(dst, dst[:, c0:c1], src, src[l][:, c0:c1], q="pool", partial=c0)

    def wload(self, wT, wap, Kc, ncols):
        assert Kc * ncols <= WSLOT
        s = self.ws[self.wi % 3]
        self.wi += 1
        view = s[:, 0:Kc * ncols].rearrange("p (k n) -> p k n", n=ncols)
        self.P.dma(s, view, wT, wap.rearrange("(k p) n -> p k n", p=128), q="sp")
        return s, view

    def dense(self, wT, tiles, Kc, rhs, nt, evac):
        P = self.P
        loaded = [None] * len(tiles)
        loaded[0] = self.wload(wT, tiles[0][0], Kc, tiles[0][1])
        deferred = []
        for i, (wap, ncols, chunks) in enumerate(tiles):
            if i + 1 < len(tiles):
                loaded[i + 1] = self.wload(wT, tiles[i + 1][0], Kc, tiles[i + 1][1])
            s, view = loaded[i]
            for (off, width, tag) in chunks:
                bk = self.nb()
                for k in range(Kc):
                    rt, rap = rhs(k)
                    P.pe(lambda e, bk=bk, view=view, k=k, off=off, width=width, rap=rap:
                         e.matmul(bk[0:width, 0:nt], lhsT=view[:, k, off:off + width], rhs=rap,
                                  start=(k == 0), stop=(k == Kc - 1)),
                         r=[s, rt], w=[bk])
                while len(deferred) > 1:
                    deferred.pop(0)()
                d = evac(tag, bk, bk[0:width, 0:nt])
                if d is not None:
                    deferred.append(d)
        while deferred:
            deferred.pop(0)()

    def wtiles(self, wbase, c0, ncols_total, Kc, per=None, width=128, tag0=0):
        if per is None:
            per = (WSLOT // Kc) // width * width
            per = min(per, 512)
        tiles = []
        c = 0
        tag = tag0
        while c < ncols_total:
            n = min(per, ncols_total - c)
            chunks = []
            for o in range(0, n, width):
                chunks.append((o, min(width, n - o), tag))
                tag += 1
            tiles.append((wbase[:, c0 + c:c0 + c + n], n, chunks))
            c += n
        return tiles

    def rmsnorm(self, src_fn, nch, nfeat, wcol, out_fn, nt, sq_fn):
        P = self.P
        for c in range(nch):
            st, sap = src_fn(c)
            qt, qap = sq_fn(c)
            P.act(lambda e, sap=sap, qap=qap: e.activation(qap, sap, AF.Square), r=[st], wp=[(qt, ("sq", c))])
        bk = self.nb()
        for c in range(nch):
            qt, qap = sq_fn(c)
            P.pe(lambda e, bk=bk, qap=qap, c=c: e.matmul(bk[:, 0:nt], lhsT=self.ones_b[:], rhs=qap,
                                                        start=(c == 0), stop=(c == nch - 1)),
                 r=[qt, self.ones_b], w=[bk])
        rstd = self.nt_()
        P.act(lambda e, bk=bk, rstd=rstd: e.activation(rstd[:, 0:nt], bk[:, 0:nt], AF.Sqrt,
                                                       bias=self.vcol("eps"), scale=1.0 / nfeat),
              r=[bk, self.vecs], w=[rstd])
        P.dve(lambda e, rstd=rstd: e.reciprocal(rstd[:, 0:nt], rstd[:, 0:nt]), r=[rstd], w=[rstd])
        for c in range(nch):
            st, sap = src_fn(c)
            ot, oap = out_fn(c)
            P.dve(lambda e, sap=sap, oap=oap, c=c, rstd=rstd:
                  e.scalar_tensor_tensor(out=oap, in0=sap, scalar=wcol(c), in1=rstd[:, 0:nt],
                                         op0=ALU.mult, op1=ALU.mult),
                  r=[st, rstd, self.vecs], wp=[(ot, ("n", c))])

    def load_block(self, l, s, t0, nt):
        P = self.P
        if l > 0:
            P.dma(self.h, self.h[:, :, 0:nt], self.h1, self.h1[:, :, t0:t0 + nt])
            return
        nch = max(1, nt // 128)
        for j in range(nch):
            q = min(128, nt)
            xin = self.tok[j % 2]
            if t0 == 0:
                src_t, src = self.meta, self.meta[:, :]
            else:
                a = t0 - NMETA + j * 128
                src_t, src = self.x, self.x[s, a:a + q, :]
            P.dma(xin, xin[0:q, :], src_t, src)
            for half in range(2):
                bk = self.nb()
                for cc in range(4):
                    c = half * 4 + cc
                    P.pe(lambda e, bk=bk, xin=xin, c=c, cc=cc, q=q:
                         e.transpose(bk[:, cc * 128:cc * 128 + q], xin[0:q, c * 128:(c + 1) * 128],
                                     self.ident_f[0:q, 0:q]),
                         r=[xin, self.ident_f], w=[bk])
                P.act(lambda e, bk=bk, half=half, j=j, q=q:
                      e.copy(self.h[:, half * 4:half * 4 + 4, j * 128:j * 128 + q],
                             bk[:, :].rearrange("p (c t) -> p c t", t=128)[:, :, 0:q]),
                      r=[bk], wp=[(self.h, ("ld", j, half))])

    def xa(self, c):
        return self.big[c // 8], c % 8

    def ssd_inproj(self, l, nt, first):
        P = self.P
        W = self.wb["w_in"][l].h
        rhs = lambda k: (self.u.sub(k), self.u[:, k, 0:nt])
        def ev_z(tag, bk, ps):
            t = self.sz[tag // 8]
            P.act(lambda e, t=t, tag=tag, ps=ps: e.activation(t[:, tag % 8, 0:nt], ps, AF.Silu),
                  r=[bk], wp=[(t, ("z", tag))])
        self.dense(self.wb["w_in"][l], self.wtiles(W, C_Z, 2048, 8), 8, rhs, nt, ev_z)
        if first:
            P.dve(lambda e: e.memset(self.halo[:], 0.0), w=list(self.halo_c))

        xsets = [self.xpre, self.tok[0], self.tok[1]]
        asets = [self.acc, self.tmp[0], self.tmp[1]]

        pendx = []

        def emit_x(items):
            info = []
            for (c, bk, ps) in items:
                si = c % 3
                info.append((c, bk, ps, xsets[si], asets[si], self.halo_c[c]))
            cw = lambda c, k: self.vcol(f"convw{l}", k * 24 + c)
            for (c, bk, ps, xp, acc, hc) in info:
                P.act(lambda e, ps=ps, xp=xp: e.copy(xp[:, 3:3 + nt], ps), r=[bk], wp=[(xp, "main")])
            for (c, bk, ps, xp, acc, hc) in info:
                P.dve(lambda e, c=c, xp=xp: e.tensor_copy(xp[:, 0:3], self.halo[:, c, :]), r=[hc], wp=[(xp, "halo")])
            for (c, bk, ps, xp, acc, hc) in info:
                P.dve(lambda e, c=c, xp=xp, acc=acc: e.tensor_scalar(out=acc[:, 0:nt], in0=xp[:, 0:nt], scalar1=cw(c, 0),
                                                                     scalar2=None, op0=ALU.mult),
                      r=[xp, self.vecs], w=[acc])
            for k in (1, 2, 3):
                for (c, bk, ps, xp, acc, hc) in info:
                    P.dve(lambda e, k=k, c=c, xp=xp, acc=acc:
                          e.scalar_tensor_tensor(out=acc[:, 0:nt], in0=xp[:, k:k + nt], scalar=cw(c, k), in1=acc[:, 0:nt],
                                                 op0=ALU.mult, op1=ALU.add), r=[xp, acc, self.vecs], w=[acc])
            for (c, bk, ps, xp, acc, hc) in info:
                P.dve(lambda e, c=c, xp=xp: e.tensor_copy(self.halo[:, c, :], xp[:, nt:nt + 3]), r=[xp], w=[hc])
            for (c, bk, ps, xp, acc, hc) in info:
                bt, bc = self.xa(c)
                P.act(lambda e, bt=bt, bc=bc, c=c, acc=acc: e.activation(bt[:, bc, 0:nt], acc[:, 0:nt], AF.Silu,
                                                                        bias=self.vcol(f"convb{l}", c)),
                      r=[acc, self.vecs], wp=[(bt, ("x", bc))])

        def ev_x(tag, bk, ps):
            pendx.append((tag, bk, ps))
            if len(pendx) == 2:
                emit_x(list(pendx))
                pendx.clear()
        self.dense(self.wb["w_in"][l], self.wtiles(W, C_XBC, 3072, 8), 8, rhs, nt, ev_x)

        def ev_dt(tag, bk, ps):
            t1 = self.nt_()
            P.act(lambda e, ps=ps, t1=t1: e.activation(t1[0:32, 0:nt], ps, AF.Exp, bias=self.vcol(f"dtb{l}", 0, 32)),
                  r=[bk, self.vecs], w=[t1])
            P.act(lambda e, t1=t1: e.activation(t1[0:32, 0:nt], t1[0:32, 0:nt], AF.Ln, bias=self.vcol("one", 0, 32)),
                  r=[t1, self.vecs], w=[t1])
            t2 = self.nt_()
            P.dve(lambda e, t1=t1, t2=t2: e.tensor_scalar(out=t2[0:32, 0:nt], in0=t1[0:32, 0:nt],
                                                          scalar1=self.negA[:, 0:1], scalar2=None, op0=ALU.mult),
                  r=[t1, self.negA], w=[t2])
            nch = max(1, nt // 128)
            q = min(128, nt)
            bk2 = self.nb()
            for j in range(nch):
                for (ii, tt) in ((0, t1), (1, t2)):
                    P.pe(lambda e, j=j, ii=ii, tt=tt, bk2=bk2:
                         e.transpose(bk2[0:q, j * 64 + ii * 32:j * 64 + ii * 32 + 32],
                                     tt[0:32, j * 128:j * 128 + q], self.ident_f[0:32, 0:32]),
                         r=[tt, self.ident_f], w=[bk2])
            P.act(lambda e, bk2=bk2: e.copy(self.dta[0:q, 0:nch, :],
                                           bk2[0:q, 0:nch * 64].rearrange("p (j x) -> p j x", x=64)),
                  r=[bk2], w=[self.dta])
        self.dense(self.wb["w_in"][l], [(W[:, C_DT:C_DT + 32], 32, [(0, 32, 0)])], 8, rhs, nt, ev_dt)

    def ssd_chunk(self, l, j, q, c0, nt):
        P = self.P
        LT, MT, Btok = self.sm[0:2], self.sm[2:4], self.sm[5]
        par = self.chunk_ctr % 2
        self.chunk_ctr += 1
        cbT = self.cbT2[par]
        acsE = self.acsE2[par]
        bsub = [self.big[i].sub(j) for i in range(3)]
        tokA, tokB = self.tok
        tokA_b = tokA[:].bitcast(BF16)
        tokB_b = tokB[:].bitcast(BF16)
        bfv = lambda bk: bk[:].bitcast(BF16)
        dt_tok = self.dta[0:q, j, 0:32]
        a_tok = self.dta[0:q, j, 32:64]
        for half in range(2):
            bk = self.nb()
            for cc in range(8):
                c = half * 8 + cc
                bt, bc = self.xa(c)
                P.pe(lambda e, bk=bk, bt=bt, bc=bc, cc=cc:
                     e.transpose(bfv(bk)[0:q, cc * 128:(cc + 1) * 128], bt[:, bc, c0:c0 + q], self.ident_b[:]),
                     r=[bsub[half], self.ident_b], w=[bk])
            P.dve(lambda e, bk=bk, half=half:
                  e.tensor_tensor(out=tokA_b[0:q, half * 1024:(half + 1) * 1024].rearrange("p (h x) -> p h x", x=64),
                                  in0=bfv(bk)[0:q, :].rearrange("p (h x) -> p h x", x=64),
                                  in1=dt_tok[:, half * 16:(half + 1) * 16].unsqueeze(2).to_broadcast([q, 16, 64]),
                                  op=ALU.mult),
                  r=[bk, self.dta], wp=[(tokA, half)])
        bk = self.nb()
        bt = self.big[2]
        for g in range(4):
            P.pe(lambda e, bk=bk, g=g: e.transpose(bfv(bk)[0:q, g * 128:(g + 1) * 128], bt[:, g, c0:c0 + q],
                                                  self.ident_b[:]), r=[bsub[2], self.ident_b], w=[bk])
        P.act(lambda e, bk=bk: e.copy(Btok[0:q, :, :], bfv(bk)[0:q, 0:512].rearrange("p (g x) -> p g x", x=128)),
              r=[bk], w=[Btok])
        bk = self.nb()
        P.pe(lambda e, bk=bk: e.matmul(bk[0:q, 0:32], lhsT=self.tri_f[0:q, 0:q], rhs=a_tok, start=True, stop=True),
             r=[self.tri_f, self.dta], w=[bk])
        P.pe(lambda e, bk=bk: e.matmul(bk[0:q, 32:64], lhsT=self.triU_f[0:q, 0:q], rhs=a_tok, start=True, stop=True),
             r=[self.triU_f, self.dta], w=[bk])
        P.pe(lambda e, bk=bk: e.matmul(bk[0:128, 64:96], lhsT=self.ones_f[0:q, 0:128], rhs=a_tok, start=True, stop=True),
             r=[self.ones_f, self.dta], w=[bk])
        P.act(lambda e, bk=bk: e.activation(acsE[0:q, 0:64], bk[0:q, 0:64], AF.Exp), r=[bk], wp=[(acsE, 0)])
        P.act(lambda e, bk=bk: e.activation(acsE[:, 64:96], bk[:, 64:96], AF.Exp), r=[bk], wp=[(acsE, 1)])
        bk = self.nb()
        for g in range(4):
            P.pe(lambda e, bk=bk, g=g: e.matmul(bk[0:q, g * 128:g * 128 + q], lhsT=bt[:, g, c0:c0 + q],
                                               rhs=bt[:, 4 + g, c0:c0 + q], start=True, stop=True),
                 r=[bsub[2]], w=[bk])
        P.act(lambda e, bk=bk: e.copy(cbT[0:q, :, 0:q], bk[0:q, :].rearrange("p (g x) -> p g x", x=128)[:, :, 0:q]),
              r=[bk], w=[cbT])
        a4 = self.a42[par]
        P.dve(lambda e: e.tensor_copy(a4[0:q, 0, :], a_tok), r=[self.dta], wp=[(a4, 0)])
        P.dve(lambda e: e.tensor_tensor(out=a4[0:q, 1, :], in0=a_tok, in1=a4[0:q, 0, :], op=ALU.subtract),
              r=[self.dta, a4], wp=[(a4, 1)])
        P.dve(lambda e: e.tensor_scalar(out=a4[0:q, 2:4, :], in0=a4[0:q, 0:2, :], scalar1=-1.0, scalar2=None,
                                        op0=ALU.mult), r=[a4], wp=[(a4, 2)])
        v3 = lambda ap: ap.rearrange("p (h x) -> p h x", x=128)[:, :, 0:q]

        def emit_decay(g, hb):
            h0 = g * 8 + hb * 4
            bkd = self.nb()
            for i, part in enumerate((2, 3)):
                P.pe(lambda e, part=part, i=i:
                     e.matmul(v3(bkd[0:q, :]), lhsT=self.tri_b[0:q, 0:q],
                              rhs=a4[0:q, part, h0:h0 + 4].unsqueeze(2).to_broadcast([q, 4, q]),
                              start=(i == 0), stop=False), r=[self.tri_b, a4], w=[bkd])
            P.pe(lambda e: e.matmul(v3(bkd[0:q, :]), lhsT=self.ident_b[0:q, 0:q],
                                    rhs=self.neg_b[0:q, 0:q].unsqueeze(1).to_broadcast([q, 4, q]),
                                    start=False, stop=False), r=[self.ident_b, self.neg_b], w=[bkd])
            for hh in range(4):
                for part in (0, 1):
                    P.pe(lambda e, hh=hh, part=part:
                         e.matmul(bkd[0:q, hh * 128:hh * 128 + q],
                                  lhsT=a4[0:q, part, h0 + hh:h0 + hh + 1].to_broadcast([q, q]),
                                  rhs=self.tri_b[0:q, 0:q], start=False, stop=(hh == 3 and part == 1)),
                         r=[self.tri_b, a4], w=[bkd])
            lt = LT[hb]
            mt = MT[hb]
            P.act(lambda e: e.activation(lt[0:q, :, 0:q], v3(bkd[0:q, :]), AF.Exp), r=[bkd], w=[lt])
            P.dve(lambda e: e.tensor_tensor(out=mt[0:q, :, 0:q], in0=lt[0:q, :, 0:q],
                                            in1=cbT[0:q, g:g + 1, 0:q].to_broadcast([q, 4, q]), op=ALU.mult),
                  r=[lt, cbT], w=[mt])
            return mt

        def emit_yoff(g):
            bko = self.bank[6 + g % 2]
            P.pe(lambda e: e.matmul(bko[0:q, :], lhsT=bt[:, 4 + g, c0:c0 + q],
                                    rhs=self.state_b[:, g * 512:(g + 1) * 512], start=True, stop=True),
                 r=[bsub[2], self.state_b], w=[bko])
            ty = self.nt_()
            P.dve(lambda e: e.tensor_tensor(out=ty[0:q, :].rearrange("p (h x) -> p h x", x=64),
                                            in0=bko[0:q, :].rearrange("p (h x) -> p h x", x=64),
                                            in1=acsE[0:q, g * 8:(g + 1) * 8].unsqueeze(2).to_broadcast([q, 8, 64]),
                                            op=ALU.mult), r=[bko, acsE], w=[ty])
            return ty

        def emit_ydiag(g, hb, mt, ty):
            bky = self.bank[4 + g % 2]
            h0 = g * 8 + hb * 4
            for hh in range(4):
                hd = h0 + hh
                P.pe(lambda e, hh=hh, hd=hd:
                     e.matmul(bky[0:q, (hb * 4 + hh) * 64:(hb * 4 + hh + 1) * 64], lhsT=mt[0:q, hh, 0:q],
                              rhs=tokA_b[0:q, hd * 64:(hd + 1) * 64], start=True, stop=True),
                     r=[mt, tokA], w=[bky])
            if hb == 1:
                P.dve(lambda e: e.tensor_tensor(out=tokB_b[0:q, g * 512:(g + 1) * 512], in0=bky[0:q, :], in1=ty[0:q, :],
                                                op=ALU.add), r=[bky, ty], wp=[(tokB, g)])

        batches = [(g, hb) for g in range(4) for hb in range(2)]
        tys = {}
        tys[0] = emit_yoff(0)
        mts = {0: emit_decay(*batches[0])}
        for bi, (g, hb) in enumerate(batches):
            if bi + 1 < len(batches):
                g2, hb2 = batches[bi + 1]
                if hb2 == 0:
                    tys[g2] = emit_yoff(g2)
                mts[bi + 1] = emit_decay(g2, hb2)
            emit_ydiag(g, hb, mts.pop(bi), tys[g])
        if j == 1 and nt == 512:
            self.dump("ytok", tokB, tokB_b[0:q, :], BF16)
            self.dump("xdt", tokA, tokA_b[0:q, :], BF16)
            self.dump("cbT", cbT, cbT[0:q, :, :], BF16)
            self.dump("acsE", acsE, acsE[:, :], F32)
            self.dump("mt", MT[1], MT[1][0:q, :, :], BF16)
            self.dump("lt", LT[1], LT[1][0:q, :, :], BF16)
            self.dump("dta", self.dta, self.dta[:, :, :], F32)
        for half in range(2):
            bk = self.nb()
            for cc in range(8):
                c = half * 8 + cc
                P.pe(lambda e, bk=bk, c=c, cc=cc:
                     e.transpose(bfv(bk)[:, cc * 128:cc * 128 + q], tokB_b[0:q, c * 128:(c + 1) * 128],
                                 self.ident_b[0:q, 0:q]),
                     r=[tokB, self.ident_b], w=[bk])
            xt = self.big[half]
            for cc in range(8):
                c = half * 8 + cc
                P.dve(lambda e, bk=bk, xt=xt, c=c, cc=cc:
                      e.scalar_tensor_tensor(out=xt[:, cc, c0:c0 + q], in0=xt[:, cc, c0:c0 + q],
                                             scalar=self.vcol(f"dskip{l}", c), in1=bfv(bk)[:, cc * 128:cc * 128 + q],
                                             op0=ALU.mult, op1=ALU.add),
                      r=[bk, bsub[half], self.vecs], wp=[(bsub[half], ("y", cc))])
        P.pool(lambda e: e.tensor_tensor(out=tokB_b[0:q, :].rearrange("p (h x) -> p h x", x=64),
                                         in0=tokA_b[0:q, :].rearrange("p (h x) -> p h x", x=64),
                                         in1=acsE[0:q, 32:64].unsqueeze(2).to_broadcast([q, 32, 64]), op=ALU.mult),
               r=[tokA, acsE], w=[tokB])
        P.pool(lambda e: e.tensor_tensor(out=self.state_f[:, :].rearrange("p (h x) -> p h x", x=64),
                                         in0=self.state_f[:, :].rearrange("p (h x) -> p h x", x=64),
                                         in1=acsE[:, 64:96].unsqueeze(2).to_broadcast([128, 32, 64]), op=ALU.mult),
               r=[self.state_f, acsE], w=[self.state_f])
        for g in range(4):
            bk = self.nb()
            P.pe(lambda e, bk=bk, g=g: e.matmul(bk[:, :], lhsT=Btok[0:q, g, :], rhs=tokB_b[0:q, g * 512:(g + 1) * 512],
                                               start=True, stop=True), r=[Btok, tokB], w=[bk])
            P.dve(lambda e, bk=bk, g=g: e.tensor_tensor(out=self.state_f[:, g * 512:(g + 1) * 512],
                                                       in0=self.state_f[:, g * 512:(g + 1) * 512], in1=bk[:, :],
                                                       op=ALU.add),
                  r=[bk, self.state_f], wp=[(self.state_f, g)])
        P.act(lambda e: e.copy(self.state_b[:, :], self.state_f[:, :]), r=[self.state_f], w=[self.state_b])

    def ssd_finish(self, l, nt):
        P = self.P
        for g in range(4):
            i, o = g // 2, (g % 2) * 4
            eng = P.pool if g % 2 == 0 else P.dve
            eng(lambda e, i=i, o=o: e.tensor_tensor(out=self.sz[i][:, o:o + 4, 0:nt], in0=self.sz[i][:, o:o + 4, 0:nt],
                                                    in1=self.big[i][:, o:o + 4, 0:nt], op=ALU.mult),
                r=[self.sz[i], self.big[i]], wp=[(self.sz[i], ("gate", g))])
        tokB = self.tok[1]
        sqv = tokB[:].bitcast(BF16).rearrange("p (c t) -> p c t", t=512)
        for g in range(4):
            st = self.sz[g // 2]
            ot = self.big[g // 2]
            self.rmsnorm(lambda c, g=g, st=st: (st, st[:, (g % 2) * 4 + c, 0:nt]), 4, 512,
                         lambda c, g=g: self.vcol(f"ssmw{l}", g * 4 + c),
                         lambda c, g=g, ot=ot: (ot, ot[:, (g % 2) * 4 + c, 0:nt]), nt,
                         lambda c: (tokB, sqv[:, c, 0:nt]))

    def mla(self, l, b, t0, nt, chunks, first):
        P = self.P
        W = self.wb["w_in"][l].h
        rhs_u = lambda k: (self.u.sub(k), self.u[:, k, 0:nt])
        tokA, tokB = self.tok
        cqf = self.sz[0][:].bitcast(F32).rearrange("p (a b) c -> p a (b c)", b=2)
        ckvf = tokA[:].rearrange("p (a t) -> p a t", t=512)
        sqv = tokB[:].bitcast(BF16).rearrange("p (c t) -> p c t", t=512)
        cqn = [self.sm[i][:].rearrange("p a b -> p (a b)") for i in range(4)]
        ckvn = [self.sm[4 + i][:].rearrange("p a b -> p (a b)") for i in range(2)]
        cos_t, sin_t = self.acc, self.xpre
        P.dma(cos_t, cos_t[0:64, 0:nt], self.cos_d, self.cos_d[:, t0:t0 + nt])
        P.dma(sin_t, sin_t[0:64, 0:nt], self.sin_d, self.sin_d[:, t0:t0 + nt])
        if first:
            P.dve(lambda e: e.memset(self.kmax2[:], 0.0), w=[self.kmax2])

        def ev_cq(tag, bk, ps):
            P.act(lambda e, ps=ps, tag=tag: e.copy(cqf[:, tag, 0:nt], ps), r=[bk], wp=[(self.sz[0], ("cq", tag))])
        self.dense(self.wb["w_in"][l], self.wtiles(W, C_CQ, 512, 8), 8, rhs_u, nt, ev_cq)

        def ev_ckv(tag, bk, ps):
            P.act(lambda e, ps=ps, tag=tag: e.copy(ckvf[:, tag, 0:nt], ps), r=[bk], wp=[(tokA, ("ckv", tag))])
        self.dense(self.wb["w_in"][l], self.wtiles(W, C_CKV, 256, 8), 8, rhs_u, nt, ev_ckv)
        self.rmsnorm(lambda c: (self.sz[0], cqf[:, c, 0:nt]), 4, 512, lambda c: self.vcol(f"qnw{l}", c),
                     lambda c: (self.sm[c], cqn[c][:, 0:nt]), nt, lambda c: (tokB, sqv[:, c, 0:nt]))
        self.rmsnorm(lambda c: (tokA, ckvf[:, c, 0:nt]), 2, 256, lambda c: self.vcol(f"kvnw{l}", c),
                     lambda c: (self.sm[4 + c], ckvn[c][:, 0:nt]), nt, lambda c: (tokB, sqv[:, c, 0:nt]))

        ropebuf = {}

        def rope_ev(kind, bk, ps, out_t, out_ap, key):
            if kind == "pe":
                t1 = self.nt_()
                P.dve(lambda e, t1=t1, ps=ps: e.tensor_tensor(out=t1[0:64, 0:nt], in0=ps, in1=cos_t[0:64, 0:nt],
                                                              op=ALU.mult), r=[bk, cos_t], w=[t1])
                ropebuf["t1"] = t1
            else:
                t1 = ropebuf["t1"]
                t2 = self.nt_()
                P.dve(lambda e, t2=t2, ps=ps: e.tensor_tensor(out=t2[0:64, 0:nt], in0=ps, in1=sin_t[0:64, 0:nt],
                                                              op=ALU.mult), r=[bk, sin_t], w=[t2])
                P.pool(lambda e, t1=t1, t2=t2: e.tensor_tensor(out=out_ap, in0=t1[0:64, 0:nt], in1=t2[0:64, 0:nt],
                                                               op=ALU.add), r=[t1, t2], wp=[(out_t, key)])

        def ev_kr(tag, bk, ps):
            rope_ev("pe" if tag == 0 else "sw", bk, ps, self.kpe, self.kpe[0:64, t0:t0 + nt], ("kpe", b))
        self.dense(self.wb["w_kr2"][l], [(self.wb["w_kr2"][l].h, 128, [(0, 64, 0), (64, 64, 1)])], 8, rhs_u, nt, ev_kr)
        bkk = self.bank[4]

        def ev_k(tag, bk, ps):
            P.act(lambda e, ps=ps, tag=tag: e.copy(self.kc[:, tag, t0:t0 + nt], ps), r=[bk],
                  wp=[(self.kc, ("k", b, tag))])
            P.pool(lambda e, tag=tag: e.tensor_tensor(out=sqv[:, tag % 4, 0:nt], in0=self.kc[:, tag, t0:t0 + nt],
                                                      in1=self.kc[:, tag, t0:t0 + nt], op=ALU.mult),
                   r=[self.kc], wp=[(tokB, ("ksq", tag % 4))])
            def later(tag=tag):
                P.pe(lambda e: e.matmul(bkk[0:65, 0:nt], lhsT=self.ones_b[:, 0:65], rhs=sqv[:, tag % 4, 0:nt],
                                        start=(tag == 0), stop=False), r=[tokB, self.ones_b], w=[bkk])
            return later
        self.dense(self.wb["w_uk"][l], self.wtiles(self.wb["w_uk"][l].h, 0, 1024, 2), 2, lambda k: (self.sm[4 + k], ckvn[k][:, 0:nt]),
                   nt, ev_k)
        P.pool(lambda e: e.tensor_tensor(out=sqv[0:64, 0, 0:nt], in0=self.kpe[0:64, t0:t0 + nt],
                                         in1=self.kpe[0:64, t0:t0 + nt], op=ALU.mult),
               r=[self.kpe], wp=[(tokB, ("ksq", 0))])
        P.pe(lambda e: e.matmul(bkk[0:65, 0:nt], lhsT=self.ones_b[0:64, 0:65], rhs=sqv[0:64, 0, 0:nt],
                                start=False, stop=True), r=[tokB, self.ones_b], w=[bkk])
        P.dve(lambda e: e.tensor_reduce(out=self.kmtmp[:, 0:1], in_=bkk[0:65, 0:nt], axis=AX.X, op=ALU.max),
              r=[bkk], w=[self.kmtmp])
        P.dve(lambda e: e.tensor_tensor(out=self.kmax2[:, :], in0=self.kmax2[:, :], in1=self.kmtmp[:, :], op=ALU.max),
              r=[self.kmtmp, self.kmax2], w=[self.kmax2])
        ws, wview = self.wload(self.wb["w_uv"][l], self.wb["w_uv"][l].h, 2, 1024)
        for (j, q, c0, ti) in chunks:
            for half in range(2):
                bk = self.nb()
                for k in range(2):
                    P.pe(lambda e, bk=bk, k=k, half=half, q=q, c0=c0:
                         e.matmul(bk[0:q, :], lhsT=ckvn[k][:, c0:c0 + q], rhs=wview[:, k, half * 512:(half + 1) * 512],
                                  start=(k == 0), stop=(k == 1)), r=[self.sm[4 + k], ws], w=[bk])
                P.act(lambda e, bk=bk, half=half, q=q, ti=ti: e.copy(self.vc[0:q, ti, half * 512:(half + 1) * 512],
                                                                    bk[0:q, :]),
                      r=[bk], wp=[(self.vc, ("v", ti, half))])
        qn = self.big[0]
        qpe = self.big[1]
        bkq = self.bank[5]
        tiles = []
        for hp in range(4):
            ch = []
            for hh in range(2):
                h = hp * 2 + hh
                ch += [(hh * 256, 128, ("n", h)), (hh * 256 + 128, 64, ("pe", h)), (hh * 256 + 192, 64, ("sw", h))]
            tiles.append((self.wb["w_uq2"][l].h[:, hp * 512:(hp + 1) * 512], 512, ch))

        def ev_q(tag, bk, ps):
            kind, h = tag
            if kind == "n":
                P.act(lambda e, ps=ps, h=h: e.copy(qn[:, h, 0:nt], ps), r=[bk], wp=[(qn, ("q", h))])
                P.pool(lambda e, h=h: e.tensor_tensor(out=sqv[:, h % 4, 0:nt], in0=qn[:, h, 0:nt], in1=qn[:, h, 0:nt],
                                                      op=ALU.mult), r=[qn], wp=[(tokB, ("qsq", h % 4))])
                def later(h=h):
                    P.pe(lambda e: e.matmul(bkq[0:65, 0:nt], lhsT=self.ones_b[:, 0:65], rhs=sqv[:, h % 4, 0:nt],
                                            start=True, stop=False), r=[tokB, self.ones_b], w=[bkq])
                return later
            else:
                rope_ev(kind, bk, ps, qpe, qpe[0:64, h, 0:nt], ("qpe", h))
                if kind == "sw":
                    def later2(h=h):
                        P.pool(lambda e: e.tensor_tensor(out=sqv[0:64, h % 4, 0:nt], in0=qpe[0:64, h, 0:nt],
                                                         in1=qpe[0:64, h, 0:nt], op=ALU.mult),
                               r=[qpe], wp=[(tokB, ("qsq", h % 4))])
                        P.pe(lambda e: e.matmul(bkq[0:65, 0:nt], lhsT=self.ones_b[0:64, 0:65], rhs=sqv[0:64, h % 4, 0:nt],
                                                start=False, stop=True), r=[tokB, self.ones_b], w=[bkq])
                        tq = self.nt_()
                        P.act(lambda e: e.activation(tq[64:65, 0:nt], bkq[64:65, 0:nt], AF.Sqrt,
                                                     scale=self.kmax2[64:65, 0:1]), r=[bkq, self.kmax2], w=[tq])
                        P.dve(lambda e: e.tensor_scalar(out=qpe[64:65, h, 0:nt], in0=tq[64:65, 0:nt], scalar1=-1.0,
                                                        scalar2=None, op0=ALU.mult), r=[tq], wp=[(qpe, ("qb", h))])
                    return later2
            return None
        self.dense(self.wb["w_uq2"][l], tiles, 4, lambda k: (self.sm[k], cqn[k][:, 0:nt]), nt, ev_q)

        ymla = self.big[2]
        ktiles = [(0, 0, NMETA)] + [(i, NMETA + 128 * (i - 1), 128) for i in range(1, self.ntile)]
        ktiles = [kt for kt in ktiles if kt[1] < t0 + nt]
        items = [(h, idx) for h in range(8) for idx in range(len(ktiles))]
        nk = len(ktiles)

        def emit_S(h, idx):
            ti, k0, kn = ktiles[idx]
            qs = max(0, k0 - t0)
            ncol = nt - qs
            bks = self.nb()
            P.pe(lambda e: e.matmul(bks[0:kn, 0:ncol], lhsT=self.kc[:, h, k0:k0 + kn], rhs=qn[:, h, qs:nt],
                                    start=True, stop=False), r=[self.kc, qn], w=[bks])
            P.pe(lambda e: e.matmul(bks[0:kn, 0:ncol], lhsT=self.kpe[0:65, k0:k0 + kn], rhs=qpe[0:65, h, qs:nt],
                                    start=False, stop=True), r=[self.kpe, qpe], w=[bks])
            pt = self.PT[self.pti % len(self.PT)]
            self.pti += 1
            P.act(lambda e: e.activation(pt[0:kn, 0:ncol], bks[0:kn, 0:ncol], AF.Exp, scale=SCALE), r=[bks], w=[pt])
            if k0 >= t0:
                P.pool(lambda e: e.tensor_tensor(out=pt[0:kn, 0:kn], in0=pt[0:kn, 0:kn], in1=self.tri_b[0:kn, 0:kn],
                                                 op=ALU.mult), r=[pt, self.tri_b], w=[pt])
            return pt

        def emit_PV(h, idx, pt):
            ti, k0, kn = ktiles[idx]
            qs = max(0, k0 - t0)
            ncol = nt - qs
            bo = self.bank[4 + 2 * (h % 2)]
            bd = self.bank[5 + 2 * (h % 2)]
            last = idx == nk - 1
            P.pe(lambda e: e.matmul(bo[:, qs:nt], lhsT=self.vc[0:kn, ti, h * 128:(h + 1) * 128], rhs=pt[0:kn, 0:ncol],
                                    start=(idx == 0), stop=last), r=[self.vc, pt], w=[bo])
            P.pe(lambda e: e.matmul(bd[:, qs:nt], lhsT=self.ones_b[0:kn, :], rhs=pt[0:kn, 0:ncol],
                                    start=(idx == 0), stop=last), r=[self.ones_b, pt], w=[bd])
            if last:
                rd = self.nt_()
                P.dve(lambda e: e.reciprocal(rd[:, 0:nt], bd[:, 0:nt]), r=[bd], w=[rd])
                P.dve(lambda e: e.tensor_tensor(out=ymla[:, h, 0:nt], in0=bo[:, 0:nt], in1=rd[:, 0:nt], op=ALU.mult),
                      r=[bo, rd], wp=[(ymla, ("y", h))])

        pend = []
        for (h, idx) in items:
            pend.append((h, idx, emit_S(h, idx)))
            if len(pend) > 2:
                emit_PV(*pend.pop(0))
        while pend:
            emit_PV(*pend.pop(0))

    def gate(self, l, c0, nt, dst):
        P = self.P
        rhs_u = lambda k: (self.u.sub(k), self.u[:, k, 0:nt])

        def ev(tag, bk, ps):
            P.act(lambda e, ps=ps, tag=tag: e.activation(dst[:, tag, 0:nt], ps, AF.Sigmoid), r=[bk],
                  wp=[(dst, ("g", tag))])
        self.dense(self.wb["w_in"][l], self.wtiles(self.wb["w_in"][l].h, c0, 1024, 8), 8, rhs_u, nt, ev)

    def block(self, l, s, b, t0, nt):
        P = self.P
        first = (b == 0)
        if t0 == 0:
            chunks = [(0, NMETA, 0, 0)]
        else:
            chunks = [(j, 128, 128 * j, 1 + 4 * (b - 1) + j) for j in range(4)]
        self.load_block(l, s, t0, nt)
        hsrc = lambda c: (self.h, self.h[:, c, 0:nt])
        usrc = lambda c: (self.u.sub(c), self.u[:, c, 0:nt])
        self.rmsnorm(hsrc, 8, 1024, lambda c: self.vcol(f"nmix{l}", c), usrc, nt, usrc)
        if first:
            P.dve(lambda e: e.memset(self.state_f[:], 0.0), w=[self.state_f])
            P.pool(lambda e: e.memset(self.state_b[:], 0.0), w=[self.state_b])
        if first or b == 1:
            t1 = self.nt_()
            P.act(lambda e, t1=t1: e.activation(t1[0:32, 0:1], self.vcol(f"alog{l}", 0, 32), AF.Exp),
                  r=[self.vecs], w=[t1])
            P.dve(lambda e, t1=t1: e.tensor_scalar(out=self.negA[:, 0:1], in0=t1[0:32, 0:1], scalar1=-1.0, scalar2=None,
                                                   op0=ALU.mult), r=[t1], w=[self.negA])
        self.ssd_inproj(l, nt, first)
        for (j, q, c0, ti) in chunks:
            if nt == 512:
                self.cast_some(2)
            self.ssd_chunk(l, j, q, c0, nt)
        self.ssd_finish(l, nt)
        gs = self.sz[0]
        mixed = self.sz[1]
        self.gate(l, C_GS, nt, gs)

        def ev_bs(tag, bk, ps):
            P.dve(lambda e, ps=ps, tag=tag: e.tensor_tensor(out=mixed[:, tag, 0:nt], in0=ps, in1=gs[:, tag, 0:nt],
                                                           op=ALU.mult), r=[bk, gs], wp=[(mixed, ("m", tag))])
        self.dense(self.wb["w_bs"][l], self.wtiles(self.wb["w_bs"][l].h, 0, 1024, 16), 16,
                   lambda k: (self.big[k // 8], self.big[k // 8][:, k % 8, 0:nt]), nt, ev_bs)
        self.mla(l, b, t0, nt, chunks, first)
        if nt == 512:
            self.dump('ymla', self.big[2], self.big[2][:, :, :], BF16)
            self.dump('yn0', self.big[0], self.big[0][:, :, :], BF16)
        gm = self.sz[0]
        self.gate(l, C_GM, nt, gm)
        ymla = self.big[2]

        def ev_bm(tag, bk, ps):
            t1 = self.nt_()
            P.dve(lambda e, ps=ps, tag=tag, t1=t1: e.tensor_tensor(out=t1[:, 0:nt], in0=ps, in1=gm[:, tag, 0:nt],
                                                                  op=ALU.mult), r=[bk, gm], w=[t1])
            P.pool(lambda e, tag=tag, t1=t1: e.tensor_tensor(out=mixed[:, tag, 0:nt], in0=mixed[:, tag, 0:nt],
                                                            in1=t1[:, 0:nt], op=ALU.add),
                   r=[t1, mixed], wp=[(mixed, ("m2", tag))])
        self.dense(self.wb["w_bm"][l], self.wtiles(self.wb["w_bm"][l].h, 0, 1024, 8), 8, lambda k: (ymla, ymla[:, k, 0:nt]), nt, ev_bm)

        def ev_res(tag, bk, ps):
            P.dve(lambda e, ps=ps, tag=tag: e.tensor_tensor(out=self.h[:, tag, 0:nt], in0=self.h[:, tag, 0:nt], in1=ps,
                                                           op=ALU.add), r=[bk, self.h], wp=[(self.h, ("r", tag))])
        self.dense(self.wb["w_out"][l], self.wtiles(self.wb["w_out"][l].h, 0, 1024, 8), 8, lambda k: (mixed, mixed[:, k, 0:nt]), nt,
                   ev_res)
        self.rmsnorm(hsrc, 8, 1024, lambda c: self.vcol(f"nmlp{l}", c), usrc, nt, usrc)
        abuf = [self.big[0], self.big[1], self.big[2], self.sz[0]]

        def ev_up(tag, bk, ps):
            t1 = self.nt_()
            P.act(lambda e, ps=ps, t1=t1: e.activation(t1[:, 0:nt], ps, AF.Relu), r=[bk], w=[t1])
            at = abuf[tag // 8]
            P.pool(lambda e, t1=t1, at=at, tag=tag: e.tensor_tensor(out=at[:, tag % 8, 0:nt], in0=t1[:, 0:nt],
                                                                   in1=t1[:, 0:nt], op=ALU.mult),
                   r=[t1], wp=[(at, ("a", tag % 8))])
        self.dense(self.wb["w_up"][l], self.wtiles(self.wb["w_up"][l].h, 0, 4096, 8), 8, usrc, nt, ev_up)
        self.dense(self.wb["w_dn"][l], self.wtiles(self.wb["w_dn"][l].h, 0, 1024, 32), 32,
                   lambda k: (abuf[k // 8], abuf[k // 8][:, k % 8, 0:nt]), nt, ev_res)
        if l < self.depth - 1:
            P.dma(self.h1, self.h1[:, :, t0:t0 + nt], self.h, self.h[:, :, 0:nt])
        elif t0 > 0:
            self.rmsnorm(hsrc, 8, 1024, lambda c: self.vcol("fnw", c), hsrc, nt, usrc)
            for j in range(4):
                ot = self.tok[j % 2]
                for half in range(2):
                    bk = self.nb()
                    for cc in range(4):
                        c = half * 4 + cc
                        P.pe(lambda e, bk=bk, c=c, cc=cc, j=j:
                             e.transpose(bk[:, cc * 128:(cc + 1) * 128], self.h[:, c, j * 128:(j + 1) * 128],
                                         self.ident_f[:]), r=[self.h, self.ident_f], w=[bk])
                    P.act(lambda e, bk=bk, ot=ot, half=half: e.copy(ot[:, half * 512:(half + 1) * 512], bk[:, :]),
                          r=[bk], wp=[(ot, ("o", half))])
                a = t0 - NMETA + j * 128
                P.dma(self.out, self.out[s, a:a + 128, :], ot, ot[:, :], partial=(s, a))

    def meta_save(self, l):
        P = self.P
        P.dma(self.mk[l], self.mk[l][:], self.kc, self.kc[:, :, 0:NMETA])
        P.dma(self.mv[l], self.mv[l][:], self.vc, self.vc[0:NMETA, 0, :])
        P.dma(self.mp[l], self.mp[l][:], self.kpe, self.kpe[0:64, 0:NMETA])
        P.dma(self.ms[l], self.ms[l][:], self.state_f, self.state_f[:, :])
        P.dma(self.mh[l], self.mh[l][:], list(self.halo_c), self.halo[:, :, :].rearrange("p c k -> p (c k)"))
        P.dma(self.mm[l], self.mm[l][:], self.kmax2, self.kmax2[:, :])

    def meta_restore(self, l):
        P = self.P
        P.dma(self.kc, self.kc[:, :, 0:NMETA], self.mk[l], self.mk[l][:], partial=("k", 0, "meta"))
        P.dma(self.vc, self.vc[0:NMETA, 0, :], self.mv[l], self.mv[l][:], partial=("v", 0, "meta"))
        P.dma(self.kpe, self.kpe[0:64, 0:NMETA], self.mp[l], self.mp[l][:], partial=("kpe", 0))
        P.dma(self.state_f, self.state_f[:, :], self.ms[l], self.ms[l][:])
        P.act(lambda e: e.copy(self.state_b[:, :], self.state_f[:, :]), r=[self.state_f], w=[self.state_b])
        stg = self.dta[:, :, :].rearrange("p a b -> p (a b)")
        P.dma(self.dta, stg[:, 0:72], self.mh[l], self.mh[l][:])
        P.dve(lambda e: e.tensor_copy(self.halo[:, :, :].rearrange("p c k -> p (c k)"), stg[:, 0:72]),
              r=[self.dta], w=list(self.halo_c))
        P.dma(self.kmax2, self.kmax2[:, :], self.mm[l], self.mm[l][:])

    def build(self):
        self.consts()
        for s in range(self.nseq):
            for l in range(self.depth):
                if s == 0 and l + 1 < self.depth:
                    self.cast_queue(l + 1, ["w_in", "w_kr2", "w_bs", "w_uk", "w_uv", "w_uq2", "w_bm", "w_out", "w_up", "w_dn"])
                if s == 0:
                    self.block(l, s, 0, 0, NMETA)
                    if self.nseq > 1:
                        self.meta_save(l)
                else:
                    self.meta_restore(l)
                for b in range(1, self.nblk + 1):
                    self.block(l, s, b, NMETA + 512 * (b - 1), 512)
                self.cast_some(len(self.castq))
        self.P.finish(final_waits=[self.out] + self.dbg_outs)
        return self.nc


def prep_inputs(inp, depth, nblk, seqs):
    f = lambda a: np.ascontiguousarray(np.asarray(a, dtype=np.float32))
    Lt = NMETA + 512 * nblk
    voff, NV = vec_layout(depth)
    vecs = np.zeros((128, NV), np.float32)

    def put(name, arr):
        arr = np.asarray(arr, np.float32)
        n = arr.shape[0] // 128
        vecs[:, voff[name]:voff[name] + n] = arr.reshape(n, 128).T
    for l in range(depth):
        put(f"nmix{l}", inp["norm_mix_w"][l])
        put(f"nmlp{l}", inp["norm_mlp_w"][l])
        for k in range(4):
            cw = np.asarray(inp["conv_w"][l][k], np.float32)
            vecs[:, voff[f"convw{l}"] + k * 24:voff[f"convw{l}"] + (k + 1) * 24] = cw.reshape(24, 128).T
        put(f"convb{l}", inp["conv_b"][l])
        put(f"ssmw{l}", inp["ssm_norm_w"][l])
        put(f"dskip{l}", np.repeat(np.asarray(inp["d_skip"][l], np.float32), 64))
        put(f"qnw{l}", inp["q_norm_w"][l])
        put(f"kvnw{l}", inp["kv_norm_w"][l])
        vecs[0:32, voff[f"dtb{l}"]] = np.asarray(inp["dt_bias"][l], np.float32)
        vecs[0:32, voff[f"alog{l}"]] = np.asarray(inp["a_log"][l], np.float32)
    put("fnw", inp["final_norm_w"])
    vecs[:, voff["eps"]] = EPS
    vecs[:, voff["one"]] = 1.0
    w_in = f(inp["w_in"])[:depth]
    kr = w_in[:, :, C_KR:C_KR + 64]
    w_kr2 = f(np.concatenate([kr, kr[:, :, 32:64], kr[:, :, 0:32]], axis=2))
    wuq = f(inp["w_uq"])[:depth].reshape(depth, 512, 8, 192)
    nope, pe = wuq[..., :128], wuq[..., 128:]
    pesw = np.concatenate([pe[..., 32:], pe[..., :32]], axis=-1)
    w_uq2 = f(np.concatenate([nope, pe, pesw], axis=-1).reshape(depth, 512, 2048))
    wukv = f(inp["w_ukv"])[:depth].reshape(depth, 256, 8, 256)
    w_uk = f(wukv[..., :128].reshape(depth, 256, 1024))
    w_uv = f(wukv[..., 128:].reshape(depth, 256, 1024))
    inv = (10000.0 ** (-np.arange(0, 64, 2, dtype=np.float32) / np.float32(64))).astype(np.float32)
    ang = (np.arange(Lt, dtype=np.float32)[:, None] * inv[None, :]).astype(np.float32)
    cos, sin = np.cos(ang).astype(np.float32), np.sin(ang).astype(np.float32)
    cos2 = f(np.concatenate([cos, cos], axis=1).T)
    sin2 = f(np.concatenate([-sin, sin], axis=1).T)
    common = {
        "meta": f(inp["meta_tokens"]), "w_in": w_in, "w_kr2": w_kr2, "w_uq2": w_uq2, "w_uk": w_uk, "w_uv": w_uv,
        "w_bs": f(inp["w_branch_ssm"])[:depth], "w_bm": f(inp["w_branch_mla"])[:depth], "w_out": f(inp["w_out"])[:depth],
        "w_up": f(inp["w_mlp_up"])[:depth], "w_dn": f(inp["w_mlp_down"])[:depth], "vecs": vecs, "cos2": cos2, "sin2": sin2,
    }
    x = f(inp["x"])
    maps = []
    for sl in seqs:
        m = dict(common)
        m["x"] = np.ascontiguousarray(x[sl, :512 * nblk, :])
        maps.append(m)
    return maps


_CACHE = {}


def kernel(**inputs):
    nseq, depth, nblk, ncores = 2, 2, 4, 8
    mk = MK(nseq, depth, nblk)
    nc = mk.build()
    seqs = [slice(i * nseq, (i + 1) * nseq) for i in range(ncores)]
    maps = prep_inputs(inputs, depth, nblk, seqs)
    res = run_bass_kernel_spmd(nc, maps, core_ids=list(range(ncores)))
    out = np.concatenate([np.asarray(r["out"], dtype=np.float32) for r in res.results], axis=0)
    return out
```

```python
import contextlib
import numpy as np
import concourse.bass as bass
import concourse.mybir as mybir
from concourse.bass_utils import run_bass_kernel_spmd

F32 = mybir.dt.float32
BF16 = mybir.dt.bfloat16
AF = mybir.ActivationFunctionType
ALU = mybir.AluOpType
AX = mybir.AxisListType

COMPUTE = ("pe", "act", "dve", "pool")
SAME_ENGINE_SYNC = True


class T:
    def __init__(self, name, handle):
        self.name = name
        self.h = handle
        self.writers = []
        self.readers = []
        self.dsem = None
        self.dcount = 0

    def __getitem__(self, k):
        return self.h[k]


class TG:
    def __init__(self, name, handle, n):
        self.name = name
        self.h = handle
        self.subs = [T(f"{name}_s{j}", handle) for j in range(n)]

    def __getitem__(self, k):
        return self.h[k]

    def sub(self, j):
        return self.subs[j]


def _flat(lst):
    out = []
    for t in lst:
        if isinstance(t, TG):
            out += t.subs
        elif isinstance(t, (list, tuple)):
            out += _flat(t)
        else:
            out.append(t)
    return out


def _flatp(lst):
    out = []
    for (t, k) in lst:
        if isinstance(t, TG):
            out += [(x, k) for x in t.subs]
        else:
            out.append((t, k))
    return out


class Op:
    __slots__ = ("eng", "fn", "deps", "idx", "signal", "dma", "dtile", "dval", "waits", "semval")

    def __init__(self, eng, fn):
        self.eng = eng
        self.fn = fn
        self.deps = []
        self.idx = -1
        self.signal = False
        self.dma = False
        self.dtile = None
        self.dval = 0
        self.waits = []
        self.semval = 0


class Prog:
    def __init__(self, nc):
        self.nc = nc
        self.es = contextlib.ExitStack()
        self.streams = {e: [] for e in ("pe", "act", "dve", "pool", "sp")}
        self.known = {e: {} for e in self.streams}
        self.sems = {}

    def sb(self, name, shape, dt):
        h = self.es.enter_context(self.nc.sbuf_tensor(name, list(shape), dt))
        return T(name, h)

    def sbg(self, name, shape, dt, n):
        h = self.es.enter_context(self.nc.sbuf_tensor(name, list(shape), dt))
        return TG(name, h, n)

    def ps(self, name, shape, dt=F32):
        h = self.es.enter_context(self.nc.psum_tensor(name, list(shape), dt))
        return T(name, h)

    def dram(self, name, shape, dt, kind="Internal"):
        h = self.nc.dram_tensor(name, list(shape), dt, kind=kind)
        return T(name, h.ap())

    def sem(self, name):
        if name not in self.sems:
            self.sems[name] = self.es.enter_context(self.nc.semaphore(name))
        return self.sems[name]

    def _add(self, eng, fn, r=(), w=(), wp=(), dma=False):
        op = Op(eng, fn)
        op.dma = dma
        r, w, wp = _flat(r), _flat(w), _flatp(wp)
        deps = []
        raw = set()
        for t in r:
            deps += [o for (o, k) in t.writers]
            raw.update(id(o) for (o, k) in t.writers)
        for t in w:
            deps += [o for (o, k) in t.writers]
            deps += t.readers
        for (t, key) in wp:
            deps += t.readers
            deps += [o for (o, k) in t.writers if k is None or k == key]
        for t in r:
            t.readers.append(op)
        for t in w:
            t.writers = [(op, None)]
            t.readers = []
        for (t, key) in wp:
            if t.readers:
                t.writers = [(op, key)]
                t.readers = []
            else:
                t.writers = [(o, k) for (o, k) in t.writers if k != key] + [(op, key)]
        if dma:
            dts = [t for t in w] + [t for (t, k) in wp]
            assert len(dts) == 1
            t = dts[0]
            if t.dsem is None:
                t.dsem = self.sem("d_" + t.name)
            t.dcount += 16
            op.dtile = t
            op.dval = t.dcount
        op.idx = len(self.streams[eng])
        kn = self.known[eng]
        best = {}
        for d in deps:
            if d is op:
                continue
            if d.dma:
                key = ("dma", d.dtile.name)
                val = d.dval
            else:
                if d.eng == eng and (eng == "pe" or not SAME_ENGINE_SYNC or id(d) not in raw):
                    continue
                key = d.eng
                val = d.idx
            if kn.get(key, -1) >= val:
                continue
            if key not in best or best[key][0] < val:
                best[key] = (val, d)
        for key, (val, d) in best.items():
            kn[key] = val
            if not d.dma:
                d.signal = True
            op.deps.append(d)
        self.streams[eng].append(op)
        return op

    def pe(self, fn, r=(), w=(), wp=()):
        return self._add("pe", fn, r, w, wp)

    def act(self, fn, r=(), w=(), wp=()):
        return self._add("act", fn, r, w, wp)

    def dve(self, fn, r=(), w=(), wp=()):
        return self._add("dve", fn, r, w, wp)

    def pool(self, fn, r=(), w=(), wp=()):
        return self._add("pool", fn, r, w, wp)

    def dma(self, out_t, out_ap, in_t, in_ap, q="sp", partial=None, **kw):
        fn = lambda e: e.dma_start(out=out_ap, in_=in_ap, **kw)
        r = [in_t] if in_t is not None else []
        if partial is None:
            return self._add(q, fn, r=r, w=[out_t], dma=True)
        return self._add(q, fn, r=r, wp=[(out_t, partial)], dma=True)

    def finish(self, final_waits=()):
        nc = self.nc
        esem = {e: self.sem("e_" + e) for e in COMPUTE}
        self.maxsem = 0
        for e in COMPUTE:
            c = 0
            for op in self.streams[e]:
                if op.signal:
                    c += 1
                    op.semval = c
            self.maxsem = max(self.maxsem, c)
        for e, ops in self.streams.items():
            for op in ops:
                for d in op.deps:
                    if d.dma:
                        op.waits.append((d.dtile.dsem, d.dval))
                    else:
                        op.waits.append((esem[d.eng], d.semval))
        engmap = {"pe": "tensor", "act": "scalar", "dve": "vector", "pool": "gpsimd", "sp": "sync"}
        with nc.Block() as block:
            for e, ops in self.streams.items():
                def body(eng, e=e, ops=ops):
                    for op in ops:
                        for (s, v) in op.waits:
                            eng.wait_ge(s, v)
                        ins = op.fn(eng)
                        if op.dma:
                            ins.then_inc(op.dtile.dsem, 16)
                        elif op.signal:
                            ins.then_inc(esem[e], 1)
                    if e == "sp":
                        for t in final_waits:
                            eng.wait_ge(t.dsem, t.dcount)
                getattr(block, engmap[e])(body)
        self.es.close()


D = 1024
NMETA = 16
EPS = 1e-6
SCALE = 192 ** -0.5
WSLOT = 4352
NEGBIG = -30000.0

C_Z, C_XBC, C_DT, C_CQ, C_CKV, C_KR, C_GS, C_GM = 0, 2048, 5120, 5152, 5664, 5920, 5984, 7008


def vec_layout(depth):
    off = {}
    n = 0

    def add(name, w):
        nonlocal n
        off[name] = n
        n += w
    for l in range(depth):
        for nm, w in (("nmix", 8), ("nmlp", 8), ("convw", 96), ("convb", 24), ("ssmw", 16), ("dskip", 16),
                      ("qnw", 4), ("kvnw", 2), ("dtb", 1), ("alog", 1)):
            add(f"{nm}{l}", w)
    add("fnw", 8)
    add("eps", 1)
    add("one", 1)
    return off, n


class MK:
    def __init__(self, nseq=2, depth=2, nblk=4):
        self.nseq, self.depth, self.nblk = nseq, depth, nblk
        self.Lt = NMETA + 512 * nblk
        self.ntile = 1 + 4 * nblk
        nc = bass.Bass("TRN2", target_bir_lowering=False)
        self.nc = nc
        P = Prog(nc)
        self.P = P
        Lt = self.Lt
        self.voff, self.NV = vec_layout(depth)
        di = lambda n, s: P.dram(n, s, F32, kind="ExternalInput")
        self.x = di("x", [nseq, 512 * nblk, D])
        self.meta = di("meta", [NMETA, D])
        self.w_in = di("w_in", [depth, D, 8032])
        self.w_kr2 = di("w_kr2", [depth, D, 128])
        self.w_uq2 = di("w_uq2", [depth, 512, 2048])
        self.w_uk = di("w_uk", [depth, 256, 1024])
        self.w_uv = di("w_uv", [depth, 256, 1024])
        self.w_bs = di("w_bs", [depth, 2048, D])
        self.w_bm = di("w_bm", [depth, D, D])
        self.w_out = di("w_out", [depth, D, D])
        self.w_up = di("w_up", [depth, D, 4096])
        self.w_dn = di("w_dn", [depth, 4096, D])
        self.vecs_d = di("vecs", [128, self.NV])
        self.cos_d = di("cos2", [64, Lt])
        self.sin_d = di("sin2", [64, Lt])
        self.wsrc = {"w_in": self.w_in, "w_kr2": self.w_kr2, "w_uq2": self.w_uq2, "w_uk": self.w_uk, "w_uv": self.w_uv,
                     "w_bs": self.w_bs, "w_bm": self.w_bm, "w_out": self.w_out, "w_up": self.w_up, "w_dn": self.w_dn}
        self.wb = {}
        for nm, t in self.wsrc.items():
            shp = list(t.h.shape)
            self.wb[nm] = [P.dram(f"{nm}_b{l}", shp[1:], BF16, kind="Internal") for l in range(depth)]
        self.out = P.dram("out", [nseq, 512 * nblk, D], F32, kind="ExternalOutput")
        self.h1 = P.dram("h1s", [128, 8, Lt], F32, kind="Internal")
        self.mk = [P.dram(f"mk{l}", [128, 8, NMETA], BF16) for l in range(depth)]
        self.mv = [P.dram(f"mv{l}", [NMETA, 1024], BF16) for l in range(depth)]
        self.mp = [P.dram(f"mp{l}", [64, NMETA], BF16) for l in range(depth)]
        self.ms = [P.dram(f"ms{l}", [128, 2048], F32) for l in range(depth)]
        self.mh = [P.dram(f"mh{l}", [128, 72], F32) for l in range(depth)]
        self.mm = [P.dram(f"mm{l}", [65, 1], F32) for l in range(depth)]

        sb = P.sb
        self.ident_f = sb("ident_f", [128, 128], F32)
        self.ident_b = sb("ident_b", [128, 128], BF16)
        self.ones_b = sb("ones_b", [128, 128], BF16)
        self.ones_f = sb("ones_f", [128, 128], F32)
        self.tri_b = sb("tri_b", [128, 128], BF16)
        self.tri_f = sb("tri_f", [128, 128], F32)
        self.triU_f = sb("triU_f", [128, 128], F32)
        self.neg_f = sb("neg_f", [128, 128], F32)
        self.vecs = sb("vecs_s", [128, self.NV], F32)
        self.h = sb("h", [128, 8, 512], F32)
        self.u = P.sbg("u", [128, 8, 512], BF16, 8)
        self.ws = [sb(f"ws{i}", [128, WSLOT], BF16) for i in range(3)]
        self.wi = 0
        self.kc = sb("kc", [128, 8, Lt], BF16)
        self.vc = sb("vc", [128, self.ntile, 1024], BF16)
        self.kpe = sb("kpe", [65, Lt], BF16)
        self.state_f = sb("state_f", [128, 2048], F32)
        self.state_b = sb("state_b", [128, 2048], BF16)
        self.kmax2 = sb("kmax2", [65, 1], F32)
        self.kmtmp = sb("kmtmp", [65, 1], F32)
        self.big = [P.sbg(f"big{i}", [128, 8, 512], BF16, 4) for i in range(3)]
        self.sz = [sb(f"sz{i}", [128, 8, 512], BF16) for i in range(2)]
        self.tok = [sb(f"tok{i}", [128, 1024], F32) for i in range(2)]
        self.sm = [sb(f"sm{i}", [128, 4, 128], BF16) for i in range(6)]
        self.tmp = [sb(f"tmp{i}", [128, 512], F32) for i in range(2)]
        self.ti = 0
        self.dta = sb("dta", [128, 4, 64], F32)
        self.acsE2 = [sb(f"acsE{i}", [128, 96], F32) for i in range(2)]
        self.cbT2 = [self.sm[4], sb("cbTb", [128, 4, 128], BF16)]
        self.negA = sb("negA", [32, 1], F32)
        self.a42 = [sb(f"a4{i}", [128, 4, 32], BF16) for i in range(2)]
        self.neg_b = sb("neg_b", [128, 128], BF16)
        self.xpre = sb("xpre", [128, 515], F32)
        self.acc = sb("acc", [128, 512], F32)
        self.halo = sb("halo", [128, 24, 3], F32)
        self.halo_c = [T(f"halo_c{c}", self.halo.h) for c in range(24)]
        self.PT = [sb(f"PT{i}", [128, 512], BF16) for i in range(4)]
        self.pti = 0
        self.bank = [P.ps(f"bk{i}", [128, 512], F32) for i in range(8)]
        self.bi = 0
        self.dbg_outs = []
        self.chunk_ctr = 0
        self.castq = []

    def dump(self, name, t, ap, dt):
        if not getattr(self, "debug", False):
            return
        shape = list(ap.shape)
        d = self.P.dram("dbg_" + name, shape, dt, kind="ExternalOutput")
        self.P.dma(d, d[:], t, ap)
        self.dbg_outs.append(d)

    def nb(self):
        b = self.bank[self.bi % 4]
        self.bi += 1
        return b

    def nt_(self):
        t = self.tmp[self.ti % 2]
        self.ti += 1
        return t

    def vcol(self, name, c=0, rows=128):
        o = self.voff[name] + c
        return self.vecs[0:rows, o:o + 1]

    def consts(self):
        P = self
        Pr = self.P
        Pr.dma(self.vecs, self.vecs[:], self.vecs_d, self.vecs_d[:])
        for t, dt in ((self.ident_f, F32), (self.ident_b, BF16)):
            Pr.pool(lambda e, t=t: e.memset(t[:], 1.0), w=[t])
            Pr.pool(lambda e, t=t: e.affine_select(out=t[:], in_=t[:], pattern=[[-1, 128]], compare_op=ALU.is_equal,
                                                   fill=0.0, base=0, channel_multiplier=1), r=[t], w=[t])
        for t in (self.ones_b, self.ones_f):
            Pr.pool(lambda e, t=t: e.memset(t[:], 1.0), w=[t])
        for t in (self.tri_b, self.tri_f):
            Pr.pool(lambda e, t=t: e.memset(t[:], 1.0), w=[t])
            Pr.pool(lambda e, t=t: e.affine_select(out=t[:], in_=t[:], pattern=[[1, 128]], compare_op=ALU.is_ge,
                                                   fill=0.0, base=0, channel_multiplier=-1), r=[t], w=[t])
        t = self.triU_f
        Pr.pool(lambda e, t=t: e.memset(t[:], 1.0), w=[t])
        Pr.pool(lambda e, t=t: e.affine_select(out=t[:], in_=t[:], pattern=[[-1, 128]], compare_op=ALU.is_gt,
                                               fill=0.0, base=0, channel_multiplier=1), r=[t], w=[t])
        t = self.neg_f
        Pr.pool(lambda e, t=t: e.memset(t[:], 0.0), w=[t])
        Pr.pool(lambda e, t=t: e.affine_select(out=t[:], in_=t[:], pattern=[[1, 128]], compare_op=ALU.is_ge,
                                               fill=NEGBIG, base=0, channel_multiplier=-1), r=[t], w=[t])
        Pr.pool(lambda e: e.tensor_copy(self.neg_b[:], self.neg_f[:]), r=[self.neg_f], w=[self.neg_b])
        self.cast_weights(0, ["w_in", "w_kr2", "w_bs", "w_uk", "w_uv", "w_uq2", "w_bm", "w_out", "w_up", "w_dn"])
        Pr.pool(lambda e: e.memset(self.kpe[64:65, :], 1.0), wp=[(self.kpe, "ones")])

    def cast_queue(self, l, names):
        for nm in names:
            src = self.wsrc[nm]
            dst = self.wb[nm][l]
            R_, C_ = dst.h.shape
            for r0 in range(0, R_, 1024):
                r1 = min(R_, r0 + 1024)
                for c0 in range(0, C_, 1024):
                    c1 = min(C_, c0 + 1024)
                    self.castq.append((dst, dst[r0:r1, c0:c1], src, src[l][r0:r1, c0:c1], (r0, c0)))

    def cast_some(self, n):
        while n > 0 and self.castq:
            dst, dap, src, sap, key = self.castq.pop(0)
            self.P.dma(dst, dap, src, sap, q="pool", partial=key)
            n -= 1

    def cast_weights(self, l, names):
        for nm in names:
            src = self.wsrc[nm]
            dst = self.wb[nm][l]
            R_, C_ = dst.h.shape
            for c0 in range(0, C_, 2048):
                c1 = min(C_, c0 + 2048)
                self.P.dma(dst, dst[:, c0:c1], src, src[l][:, c0:c1], q="pool", partial=c0)

    def wload(self, wT, wap, Kc, ncols):
        assert Kc * ncols <= WSLOT
        s = self.ws[self.wi % 3]
        self.wi += 1
        view = s[:, 0:Kc * ncols].rearrange("p (k n) -> p k n", n=ncols)
        self.P.dma(s, view, wT, wap.rearrange("(k p) n -> p k n", p=128), q="sp")
        return s, view

    def dense(self, wT, tiles, Kc, rhs, nt, evac):
        P = self.P
        loaded = [None] * len(tiles)
        loaded[0] = self.wload(wT, tiles[0][0], Kc, tiles[0][1])
        deferred = []
        for i, (wap, ncols, chunks) in enumerate(tiles):
            if i + 1 < len(tiles):
                loaded[i + 1] = self.wload(wT, tiles[i + 1][0], Kc, tiles[i + 1][1])
            s, view = loaded[i]
            for (off, width, tag) in chunks:
                bk = self.nb()
                for k in range(Kc):
                    rt, rap = rhs(k)
                    P.pe(lambda e, bk=bk, view=view, k=k, off=off, width=width, rap=rap:
                         e.matmul(bk[0:width, 0:nt], lhsT=view[:, k, off:off + width], rhs=rap,
                                  start=(k == 0), stop=(k == Kc - 1)),
                         r=[s, rt], w=[bk])
                while len(deferred) > 1:
                    deferred.pop(0)()
                d = evac(tag, bk, bk[0:width, 0:nt])
                if d is not None:
                    deferred.append(d)
        while deferred:
            deferred.pop(0)()

    def wtiles(self, wbase, c0, ncols_total, Kc, per=None, width=128, tag0=0):
        if per is None:
            per = (WSLOT // Kc) // width * width
            per = min(per, 512)
        tiles = []
        c = 0
        tag = tag0
        while c < ncols_total:
            n = min(per, ncols_total - c)
            chunks = []
            for o in range(0, n, width):
                chunks.append((o, min(width, n - o), tag))
                tag += 1
            tiles.append((wbase[:, c0 + c:c0 + c + n], n, chunks))
            c += n
        return tiles

    def rmsnorm(self, src_fn, nch, nfeat, wcol, out_fn, nt, sq_fn):
        P = self.P
        for c in range(nch):
            st, sap = src_fn(c)
            qt, qap = sq_fn(c)
            P.act(lambda e, sap=sap, qap=qap: e.activation(qap, sap, AF.Square), r=[st], wp=[(qt, ("sq", c))])
        bk = self.nb()
        for c in range(nch):
            qt, qap = sq_fn(c)
            P.pe(lambda e, bk=bk, qap=qap, c=c: e.matmul(bk[:, 0:nt], lhsT=self.ones_b[:], rhs=qap,
                                                        start=(c == 0), stop=(c == nch - 1)),
                 r=[qt, self.ones_b], w=[bk])
        rstd = self.nt_()
        P.act(lambda e, bk=bk, rstd=rstd: e.activation(rstd[:, 0:nt], bk[:, 0:nt], AF.Sqrt,
                                                       bias=self.vcol("eps"), scale=1.0 / nfeat),
              r=[bk, self.vecs], w=[rstd])
        P.dve(lambda e, rstd=rstd: e.reciprocal(rstd[:, 0:nt], rstd[:, 0:nt]), r=[rstd], w=[rstd])
        for c in range(nch):
            st, sap = src_fn(c)
            ot, oap = out_fn(c)
            P.dve(lambda e, sap=sap, oap=oap, c=c, rstd=rstd:
                  e.scalar_tensor_tensor(out=oap, in0=sap, scalar=wcol(c), in1=rstd[:, 0:nt],
                                         op0=ALU.mult, op1=ALU.mult),
                  r=[st, rstd, self.vecs], wp=[(ot, ("n", c))])

    def load_block(self, l, s, t0, nt):
        P = self.P
        if l > 0:
            P.dma(self.h, self.h[:, :, 0:nt], self.h1, self.h1[:, :, t0:t0 + nt])
            return
        nch = max(1, nt // 128)
        for j in range(nch):
            q = min(128, nt)
            xin = self.tok[j % 2]
            if t0 == 0:
                src_t, src = self.meta, self.meta[:, :]
            else:
                a = t0 - NMETA + j * 128
                src_t, src = self.x, self.x[s, a:a + q, :]
            P.dma(xin, xin[0:q, :], src_t, src)
            for half in range(2):
                bk = self.nb()
                for cc in range(4):
                    c = half * 4 + cc
                    P.pe(lambda e, bk=bk, xin=xin, c=c, cc=cc, q=q:
                         e.transpose(bk[:, cc * 128:cc * 128 + q], xin[0:q, c * 128:(c + 1) * 128],
                                     self.ident_f[0:q, 0:q]),
                         r=[xin, self.ident_f], w=[bk])
                P.act(lambda e, bk=bk, half=half, j=j, q=q:
                      e.copy(self.h[:, half * 4:half * 4 + 4, j * 128:j * 128 + q],
                             bk[:, :].rearrange("p (c t) -> p c t", t=128)[:, :, 0:q]),
                      r=[bk], wp=[(self.h, ("ld", j, half))])

    def xa(self, c):
        return self.big[c // 8], c % 8

    def ssd_inproj(self, l, nt, first):
        P = self.P
        W = self.wb["w_in"][l].h
        rhs = lambda k: (self.u.sub(k), self.u[:, k, 0:nt])
        def ev_z(tag, bk, ps):
            t = self.sz[tag // 8]
            P.act(lambda e, t=t, tag=tag, ps=ps: e.activation(t[:, tag % 8, 0:nt], ps, AF.Silu),
                  r=[bk], wp=[(t, ("z", tag))])
        self.dense(self.wb["w_in"][l], self.wtiles(W, C_Z, 2048, 8), 8, rhs, nt, ev_z)
        if first:
            P.dve(lambda e: e.memset(self.halo[:], 0.0), w=list(self.halo_c))

        xsets = [self.xpre, self.tok[0], self.tok[1]]
        asets = [self.acc, self.tmp[0], self.tmp[1]]

        pendx = []

        def emit_x(items):
            info = []
            for (c, bk, ps) in items:
                si = c % 3
                info.append((c, bk, ps, xsets[si], asets[si], self.halo_c[c]))
            cw = lambda c, k: self.vcol(f"convw{l}", k * 24 + c)
            for (c, bk, ps, xp, acc, hc) in info:
                P.act(lambda e, ps=ps, xp=xp: e.copy(xp[:, 3:3 + nt], ps), r=[bk], wp=[(xp, "main")])
            for (c, bk, ps, xp, acc, hc) in info:
                P.dve(lambda e, c=c, xp=xp: e.tensor_copy(xp[:, 0:3], self.halo[:, c, :]), r=[hc], wp=[(xp, "halo")])
            for (c, bk, ps, xp, acc, hc) in info:
                P.dve(lambda e, c=c, xp=xp, acc=acc: e.tensor_scalar(out=acc[:, 0:nt], in0=xp[:, 0:nt], scalar1=cw(c, 0),
                                                                     scalar2=None, op0=ALU.mult),
                      r=[xp, self.vecs], w=[acc])
            for k in (1, 2, 3):
                for (c, bk, ps, xp, acc, hc) in info:
                    P.dve(lambda e, k=k, c=c, xp=xp, acc=acc:
                          e.scalar_tensor_tensor(out=acc[:, 0:nt], in0=xp[:, k:k + nt], scalar=cw(c, k), in1=acc[:, 0:nt],
                                                 op0=ALU.mult, op1=ALU.add), r=[xp, acc, self.vecs], w=[acc])
            for (c, bk, ps, xp, acc, hc) in info:
                P.dve(lambda e, c=c, xp=xp: e.tensor_copy(self.halo[:, c, :], xp[:, nt:nt + 3]), r=[xp], w=[hc])
            for (c, bk, ps, xp, acc, hc) in info:
                bt, bc = self.xa(c)
                P.act(lambda e, bt=bt, bc=bc, c=c, acc=acc: e.activation(bt[:, bc, 0:nt], acc[:, 0:nt], AF.Silu,
                                                                        bias=self.vcol(f"convb{l}", c)),
                      r=[acc, self.vecs], wp=[(bt, ("x", bc))])

        def ev_x(tag, bk, ps):
            pendx.append((tag, bk, ps))
            if len(pendx) == 2:
                emit_x(list(pendx))
                pendx.clear()
        self.dense(self.wb["w_in"][l], self.wtiles(W, C_XBC, 3072, 8), 8, rhs, nt, ev_x)

        def ev_dt(tag, bk, ps):
            t1 = self.nt_()
            P.act(lambda e, ps=ps, t1=t1: e.activation(t1[0:32, 0:nt], ps, AF.Exp, bias=self.vcol(f"dtb{l}", 0, 32)),
                  r=[bk, self.vecs], w=[t1])
            P.act(lambda e, t1=t1: e.activation(t1[0:32, 0:nt], t1[0:32, 0:nt], AF.Ln, bias=self.vcol("one", 0, 32)),
                  r=[t1, self.vecs], w=[t1])
            t2 = self.nt_()
            P.dve(lambda e, t1=t1, t2=t2: e.tensor_scalar(out=t2[0:32, 0:nt], in0=t1[0:32, 0:nt],
                                                          scalar1=self.negA[:, 0:1], scalar2=None, op0=ALU.mult),
                  r=[t1, self.negA], w=[t2])
            nch = max(1, nt // 128)
            q = min(128, nt)
            bk2 = self.nb()
            for j in range(nch):
                for (ii, tt) in ((0, t1), (1, t2)):
                    P.pe(lambda e, j=j, ii=ii, tt=tt, bk2=bk2:
                         e.transpose(bk2[0:q, j * 64 + ii * 32:j * 64 + ii * 32 + 32],
                                     tt[0:32, j * 128:j * 128 + q], self.ident_f[0:32, 0:32]),
                         r=[tt, self.ident_f], w=[bk2])
            P.act(lambda e, bk2=bk2: e.copy(self.dta[0:q, 0:nch, :],
                                           bk2[0:q, 0:nch * 64].rearrange("p (j x) -> p j x", x=64)),
                  r=[bk2], w=[self.dta])
        self.dense(self.wb["w_in"][l], [(W[:, C_DT:C_DT + 32], 32, [(0, 32, 0)])], 8, rhs, nt, ev_dt)

    def ssd_chunk(self, l, j, q, c0, nt):
        P = self.P
        LT, MT, Btok = self.sm[0:2], self.sm[2:4], self.sm[5]
        par = self.chunk_ctr % 2
        self.chunk_ctr += 1
        cbT = self.cbT2[par]
        acsE = self.acsE2[par]
        bsub = [self.big[i].sub(j) for i in range(3)]
        tokA, tokB = self.tok
        tokA_b = tokA[:].bitcast(BF16)
        tokB_b = tokB[:].bitcast(BF16)
        bfv = lambda bk: bk[:].bitcast(BF16)
        dt_tok = self.dta[0:q, j, 0:32]
        a_tok = self.dta[0:q, j, 32:64]
        for half in range(2):
            bk = self.nb()
            for cc in range(8):
                c = half * 8 + cc
                bt, bc = self.xa(c)
                P.pe(lambda e, bk=bk, bt=bt, bc=bc, cc=cc:
                     e.transpose(bfv(bk)[0:q, cc * 128:(cc + 1) * 128], bt[:, bc, c0:c0 + q], self.ident_b[:]),
                     r=[bsub[half], self.ident_b], w=[bk])
            P.dve(lambda e, bk=bk, half=half:
                  e.tensor_tensor(out=tokA_b[0:q, half * 1024:(half + 1) * 1024].rearrange("p (h x) -> p h x", x=64),
                                  in0=bfv(bk)[0:q, :].rearrange("p (h x) -> p h x", x=64),
                                  in1=dt_tok[:, half * 16:(half + 1) * 16].unsqueeze(2).to_broadcast([q, 16, 64]),
                                  op=ALU.mult),
                  r=[bk, self.dta], wp=[(tokA, half)])
        bk = self.nb()
        bt = self.big[2]
        for g in range(4):
            P.pe(lambda e, bk=bk, g=g: e.transpose(bfv(bk)[0:q, g * 128:(g + 1) * 128], bt[:, g, c0:c0 + q],
                                                  self.ident_b[:]), r=[bsub[2], self.ident_b], w=[bk])
        P.act(lambda e, bk=bk: e.copy(Btok[0:q, :, :], bfv(bk)[0:q, 0:512].rearrange("p (g x) -> p g x", x=128)),
              r=[bk], w=[Btok])
        bk = self.nb()
        P.pe(lambda e, bk=bk: e.matmul(bk[0:q, 0:32], lhsT=self.tri_f[0:q, 0:q], rhs=a_tok, start=True, stop=True),
             r=[self.tri_f, self.dta], w=[bk])
        P.pe(lambda e, bk=bk: e.matmul(bk[0:q, 32:64], lhsT=self.triU_f[0:q, 0:q], rhs=a_tok, start=True, stop=True),
             r=[self.triU_f, self.dta], w=[bk])
        P.pe(lambda e, bk=bk: e.matmul(bk[0:128, 64:96], lhsT=self.ones_f[0:q, 0:128], rhs=a_tok, start=True, stop=True),
             r=[self.ones_f, self.dta], w=[bk])
        P.act(lambda e, bk=bk: e.activation(acsE[0:q, 0:64], bk[0:q, 0:64], AF.Exp), r=[bk], wp=[(acsE, 0)])
        P.act(lambda e, bk=bk: e.activation(acsE[:, 64:96], bk[:, 64:96], AF.Exp), r=[bk], wp=[(acsE, 1)])
        bk = self.nb()
        for g in range(4):
            P.pe(lambda e, bk=bk, g=g: e.matmul(bk[0:q, g * 128:g * 128 + q], lhsT=bt[:, g, c0:c0 + q],
                                               rhs=bt[:, 4 + g, c0:c0 + q], start=True, stop=True),
                 r=[bsub[2]], w=[bk])
        P.act(lambda e, bk=bk: e.copy(cbT[0:q, :, 0:q], bk[0:q, :].rearrange("p (g x) -> p g x", x=128)[:, :, 0:q]),
              r=[bk], w=[cbT])
        a4 = self.a42[par]
        P.dve(lambda e: e.tensor_copy(a4[0:q, 0, :], a_tok), r=[self.dta], wp=[(a4, 0)])
        P.dve(lambda e: e.tensor_tensor(out=a4[0:q, 1, :], in0=a_tok, in1=a4[0:q, 0, :], op=ALU.subtract),
              r=[self.dta, a4], wp=[(a4, 1)])
        P.dve(lambda e: e.tensor_scalar(out=a4[0:q, 2:4, :], in0=a4[0:q, 0:2, :], scalar1=-1.0, scalar2=None,
                                        op0=ALU.mult), r=[a4], wp=[(a4, 2)])
        v3 = lambda ap: ap.rearrange("p (h x) -> p h x", x=128)[:, :, 0:q]

        def emit_decay(g, hb):
            h0 = g * 8 + hb * 4
            bkd = self.nb()
            for i, part in enumerate((2, 3)):
                P.pe(lambda e, part=part, i=i:
                     e.matmul(v3(bkd[0:q, :]), lhsT=self.tri_b[0:q, 0:q],
                              rhs=a4[0:q, part, h0:h0 + 4].unsqueeze(2).to_broadcast([q, 4, q]),
                              start=(i == 0), stop=False), r=[self.tri_b, a4], w=[bkd])
            P.pe(lambda e: e.matmul(v3(bkd[0:q, :]), lhsT=self.ident_b[0:q, 0:q],
                                    rhs=self.neg_b[0:q, 0:q].unsqueeze(1).to_broadcast([q, 4, q]),
                                    start=False, stop=False), r=[self.ident_b, self.neg_b], w=[bkd])
            for hh in range(4):
                for part in (0, 1):
                    P.pe(lambda e, hh=hh, part=part:
                         e.matmul(bkd[0:q, hh * 128:hh * 128 + q],
                                  lhsT=a4[0:q, part, h0 + hh:h0 + hh + 1].to_broadcast([q, q]),
                                  rhs=self.tri_b[0:q, 0:q], start=False, stop=(hh == 3 and part == 1)),
                         r=[self.tri_b, a4], w=[bkd])
            lt = LT[hb]
            mt = MT[hb]
            P.act(lambda e: e.activation(lt[0:q, :, 0:q], v3(bkd[0:q, :]), AF.Exp), r=[bkd], w=[lt])
            P.dve(lambda e: e.tensor_tensor(out=mt[0:q, :, 0:q], in0=lt[0:q, :, 0:q],
                                            in1=cbT[0:q, g:g + 1, 0:q].to_broadcast([q, 4, q]), op=ALU.mult),
                  r=[lt, cbT], w=[mt])
            return mt

        def emit_yoff(g):
            bko = self.bank[6 + g % 2]
            P.pe(lambda e: e.matmul(bko[0:q, :], lhsT=bt[:, 4 + g, c0:c0 + q],
                                    rhs=self.state_b[:, g * 512:(g + 1) * 512], start=True, stop=True),
                 r=[bsub[2], self.state_b], w=[bko])
            ty = self.nt_()
            P.dve(lambda e: e.tensor_tensor(out=ty[0:q, :].rearrange("p (h x) -> p h x", x=64),
                                            in0=bko[0:q, :].rearrange("p (h x) -> p h x", x=64),
                                            in1=acsE[0:q, g * 8:(g + 1) * 8].unsqueeze(2).to_broadcast([q, 8, 64]),
                                            op=ALU.mult), r=[bko, acsE], w=[ty])
            return ty

        def emit_ydiag(g, hb, mt, ty):
            bky = self.bank[4 + g % 2]
            h0 = g * 8 + hb * 4
            for hh in range(4):
                hd = h0 + hh
                P.pe(lambda e, hh=hh, hd=hd:
                     e.matmul(bky[0:q, (hb * 4 + hh) * 64:(hb * 4 + hh + 1) * 64], lhsT=mt[0:q, hh, 0:q],
                              rhs=tokA_b[0:q, hd * 64:(hd + 1) * 64], start=True, stop=True),
                     r=[mt, tokA], w=[bky])
            if hb == 1:
                P.dve(lambda e: e.tensor_tensor(out=tokB_b[0:q, g * 512:(g + 1) * 512], in0=bky[0:q, :], in1=ty[0:q, :],
                                                op=ALU.add), r=[bky, ty], wp=[(tokB, g)])

        batches = [(g, hb) for g in range(4) for hb in range(2)]
        tys = {}
        tys[0] = emit_yoff(0)
        mts = {0: emit_decay(*batches[0])}
        for bi, (g, hb) in enumerate(batches):
            if bi + 1 < len(batches):
                g2, hb2 = batches[bi + 1]
                if hb2 == 0:
                    tys[g2] = emit_yoff(g2)
                mts[bi + 1] = emit_decay(g2, hb2)
            emit_ydiag(g, hb, mts.pop(bi), tys[g])
        if j == 1 and nt == 512:
            self.dump("ytok", tokB, tokB_b[0:q, :], BF16)
            self.dump("xdt", tokA, tokA_b[0:q, :], BF16)
            self.dump("cbT", cbT, cbT[0:q, :, :], BF16)
            self.dump("acsE", acsE, acsE[:, :], F32)
            self.dump("mt", MT[1], MT[1][0:q, :, :], BF16)
            self.dump("lt", LT[1], LT[1][0:q, :, :], BF16)
            self.dump("dta", self.dta, self.dta[:, :, :], F32)
        for half in range(2):
            bk = self.nb()
            for cc in range(8):
                c = half * 8 + cc
                P.pe(lambda e, bk=bk, c=c, cc=cc:
                     e.transpose(bfv(bk)[:, cc * 128:cc * 128 + q], tokB_b[0:q, c * 128:(c + 1) * 128],
                                 self.ident_b[0:q, 0:q]),
                     r=[tokB, self.ident_b], w=[bk])
            xt = self.big[half]
            for cc in range(8):
                c = half * 8 + cc
                P.dve(lambda e, bk=bk, xt=xt, c=c, cc=cc:
                      e.scalar_tensor_tensor(out=xt[:, cc, c0:c0 + q], in0=xt[:, cc, c0:c0 + q],
                                             scalar=self.vcol(f"dskip{l}", c), in1=bfv(bk)[:, cc * 128:cc * 128 + q],
                                             op0=ALU.mult, op1=ALU.add),
                      r=[bk, bsub[half], self.vecs], wp=[(bsub[half], ("y", cc))])
        P.pool(lambda e: e.tensor_tensor(out=tokB_b[0:q, :].rearrange("p (h x) -> p h x", x=64),
                                         in0=tokA_b[0:q, :].rearrange("p (h x) -> p h x", x=64),
                                         in1=acsE[0:q, 32:64].unsqueeze(2).to_broadcast([q, 32, 64]), op=ALU.mult),
               r=[tokA, acsE], w=[tokB])
        P.pool(lambda e: e.tensor_tensor(out=self.state_f[:, :].rearrange("p (h x) -> p h x", x=64),
                                         in0=self.state_f[:, :].rearrange("p (h x) -> p h x", x=64),
                                         in1=acsE[:, 64:96].unsqueeze(2).to_broadcast([128, 32, 64]), op=ALU.mult),
               r=[self.state_f, acsE], w=[self.state_f])
        for g in range(4):
            bk = self.nb()
            P.pe(lambda e, bk=bk, g=g: e.matmul(bk[:, :], lhsT=Btok[0:q, g, :], rhs=tokB_b[0:q, g * 512:(g + 1) * 512],
                                               start=True, stop=True), r=[Btok, tokB], w=[bk])
            P.dve(lambda e, bk=bk, g=g: e.tensor_tensor(out=self.state_f[:, g * 512:(g + 1) * 512],
                                                       in0=self.state_f[:, g * 512:(g + 1) * 512], in1=bk[:, :],
                                                       op=ALU.add),
                  r=[bk, self.state_f], wp=[(self.state_f, g)])
        P.act(lambda e: e.copy(self.state_b[:, :], self.state_f[:, :]), r=[self.state_f], w=[self.state_b])

    def ssd_finish(self, l, nt):
        P = self.P
        for g in range(4):
            i, o = g // 2, (g % 2) * 4
            eng = P.pool if g % 2 == 0 else P.dve
            eng(lambda e, i=i, o=o: e.tensor_tensor(out=self.sz[i][:, o:o + 4, 0:nt], in0=self.sz[i][:, o:o + 4, 0:nt],
                                                    in1=self.big[i][:, o:o + 4, 0:nt], op=ALU.mult),
                r=[self.sz[i], self.big[i]], wp=[(self.sz[i], ("gate", g))])
        tokB = self.tok[1]
        sqv = tokB[:].bitcast(BF16).rearrange("p (c t) -> p c t", t=512)
        for g in range(4):
            st = self.sz[g // 2]
            ot = self.big[g // 2]
            self.rmsnorm(lambda c, g=g, st=st: (st, st[:, (g % 2) * 4 + c, 0:nt]), 4, 512,
                         lambda c, g=g: self.vcol(f"ssmw{l}", g * 4 + c),
                         lambda c, g=g, ot=ot: (ot, ot[:, (g % 2) * 4 + c, 0:nt]), nt,
                         lambda c: (tokB, sqv[:, c, 0:nt]))

    def mla(self, l, b, t0, nt, chunks, first):
        P = self.P
        W = self.wb["w_in"][l].h
        rhs_u = lambda k: (self.u.sub(k), self.u[:, k, 0:nt])
        tokA, tokB = self.tok
        cqf = self.sz[0][:].bitcast(F32).rearrange("p (a b) c -> p a (b c)", b=2)
        ckvf = tokA[:].rearrange("p (a t) -> p a t", t=512)
        sqv = tokB[:].bitcast(BF16).rearrange("p (c t) -> p c t", t=512)
        cqn = [self.sm[i][:].rearrange("p a b -> p (a b)") for i in range(4)]
        ckvn = [self.sm[4 + i][:].rearrange("p a b -> p (a b)") for i in range(2)]
        cos_t, sin_t = self.acc, self.xpre
        P.dma(cos_t, cos_t[0:64, 0:nt], self.cos_d, self.cos_d[:, t0:t0 + nt])
        P.dma(sin_t, sin_t[0:64, 0:nt], self.sin_d, self.sin_d[:, t0:t0 + nt])
        if first:
            P.dve(lambda e: e.memset(self.kmax2[:], 0.0), w=[self.kmax2])

        def ev_cq(tag, bk, ps):
            P.act(lambda e, ps=ps, tag=tag: e.copy(cqf[:, tag, 0:nt], ps), r=[bk], wp=[(self.sz[0], ("cq", tag))])
        self.dense(self.wb["w_in"][l], self.wtiles(W, C_CQ, 512, 8), 8, rhs_u, nt, ev_cq)

        def ev_ckv(tag, bk, ps):
            P.act(lambda e, ps=ps, tag=tag: e.copy(ckvf[:, tag, 0:nt], ps), r=[bk], wp=[(tokA, ("ckv", tag))])
        self.dense(self.wb["w_in"][l], self.wtiles(W, C_CKV, 256, 8), 8, rhs_u, nt, ev_ckv)
        self.rmsnorm(lambda c: (self.sz[0], cqf[:, c, 0:nt]), 4, 512, lambda c: self.vcol(f"qnw{l}", c),
                     lambda c: (self.sm[c], cqn[c][:, 0:nt]), nt, lambda c: (tokB, sqv[:, c, 0:nt]))
        self.rmsnorm(lambda c: (tokA, ckvf[:, c, 0:nt]), 2, 256, lambda c: self.vcol(f"kvnw{l}", c),
                     lambda c: (self.sm[4 + c], ckvn[c][:, 0:nt]), nt, lambda c: (tokB, sqv[:, c, 0:nt]))

        ropebuf = {}

        def rope_ev(kind, bk, ps, out_t, out_ap, key):
            if kind == "pe":
                t1 = self.nt_()
                P.dve(lambda e, t1=t1, ps=ps: e.tensor_tensor(out=t1[0:64, 0:nt], in0=ps, in1=cos_t[0:64, 0:nt],
                                                              op=ALU.mult), r=[bk, cos_t], w=[t1])
                ropebuf["t1"] = t1
            else:
                t1 = ropebuf["t1"]
                t2 = self.nt_()
                P.dve(lambda e, t2=t2, ps=ps: e.tensor_tensor(out=t2[0:64, 0:nt], in0=ps, in1=sin_t[0:64, 0:nt],
                                                              op=ALU.mult), r=[bk, sin_t], w=[t2])
                P.pool(lambda e, t1=t1, t2=t2: e.tensor_tensor(out=out_ap, in0=t1[0:64, 0:nt], in1=t2[0:64, 0:nt],
                                                               op=ALU.add), r=[t1, t2], wp=[(out_t, key)])

        def ev_kr(tag, bk, ps):
            rope_ev("pe" if tag == 0 else "sw", bk, ps, self.kpe, self.kpe[0:64, t0:t0 + nt], ("kpe", b))
        self.dense(self.wb["w_kr2"][l], [(self.wb["w_kr2"][l].h, 128, [(0, 64, 0), (64, 64, 1)])], 8, rhs_u, nt, ev_kr)
        bkk = self.bank[4]

        def ev_k(tag, bk, ps):
            P.act(lambda e, ps=ps, tag=tag: e.copy(self.kc[:, tag, t0:t0 + nt], ps), r=[bk],
                  wp=[(self.kc, ("k", b, tag))])
            P.pool(lambda e, tag=tag: e.tensor_tensor(out=sqv[:, tag % 4, 0:nt], in0=self.kc[:, tag, t0:t0 + nt],
                                                      in1=self.kc[:, tag, t0:t0 + nt], op=ALU.mult),
                   r=[self.kc], wp=[(tokB, ("ksq", tag % 4))])
            def later(tag=tag):
                P.pe(lambda e: e.matmul(bkk[0:65, 0:nt], lhsT=self.ones_b[:, 0:65], rhs=sqv[:, tag % 4, 0:nt],
                                        start=(tag == 0), stop=False), r=[tokB, self.ones_b], w=[bkk])
            return later
        self.dense(self.wb["w_uk"][l], self.wtiles(self.wb["w_uk"][l].h, 0, 1024, 2), 2, lambda k: (self.sm[4 + k], ckvn[k][:, 0:nt]),
                   nt, ev_k)
        P.pool(lambda e: e.tensor_tensor(out=sqv[0:64, 0, 0:nt], in0=self.kpe[0:64, t0:t0 + nt],
                                         in1=self.kpe[0:64, t0:t0 + nt], op=ALU.mult),
               r=[self.kpe], wp=[(tokB, ("ksq", 0))])
        P.pe(lambda e: e.matmul(bkk[0:65, 0:nt], lhsT=self.ones_b[0:64, 0:65], rhs=sqv[0:64, 0, 0:nt],
                                start=False, stop=True), r=[tokB, self.ones_b], w=[bkk])
        P.dve(lambda e: e.tensor_reduce(out=self.kmtmp[:, 0:1], in_=bkk[0:65, 0:nt], axis=AX.X, op=ALU.max),
              r=[bkk], w=[self.kmtmp])
        P.dve(lambda e: e.tensor_tensor(out=self.kmax2[:, :], in0=self.kmax2[:, :], in1=self.kmtmp[:, :], op=ALU.max),
              r=[self.kmtmp, self.kmax2], w=[self.kmax2])
        ws, wview = self.wload(self.wb["w_uv"][l], self.wb["w_uv"][l].h, 2, 1024)
        for (j, q, c0, ti) in chunks:
            for half in range(2):
                bk = self.nb()
                for k in range(2):
                    P.pe(lambda e, bk=bk, k=k, half=half, q=q, c0=c0:
                         e.matmul(bk[0:q, :], lhsT=ckvn[k][:, c0:c0 + q], rhs=wview[:, k, half * 512:(half + 1) * 512],
                                  start=(k == 0), stop=(k == 1)), r=[self.sm[4 + k], ws], w=[bk])
                P.act(lambda e, bk=bk, half=half, q=q, ti=ti: e.copy(self.vc[0:q, ti, half * 512:(half + 1) * 512],
                                                                    bk[0:q, :]),
                      r=[bk], wp=[(self.vc, ("v", ti, half))])
        qn = self.big[0]
        qpe = self.big[1]
        bkq = self.bank[5]
        tiles = []
        for hp in range(4):
            ch = []
            for hh in range(2):
                h = hp * 2 + hh
                ch += [(hh * 256, 128, ("n", h)), (hh * 256 + 128, 64, ("pe", h)), (hh * 256 + 192, 64, ("sw", h))]
            tiles.append((self.wb["w_uq2"][l].h[:, hp * 512:(hp + 1) * 512], 512, ch))

        def ev_q(tag, bk, ps):
            kind, h = tag
            if kind == "n":
                P.act(lambda e, ps=ps, h=h: e.copy(qn[:, h, 0:nt], ps), r=[bk], wp=[(qn, ("q", h))])
                P.pool(lambda e, h=h: e.tensor_tensor(out=sqv[:, h % 4, 0:nt], in0=qn[:, h, 0:nt], in1=qn[:, h, 0:nt],
                                                      op=ALU.mult), r=[qn], wp=[(tokB, ("qsq", h % 4))])
                def later(h=h):
                    P.pe(lambda e: e.matmul(bkq[0:65, 0:nt], lhsT=self.ones_b[:, 0:65], rhs=sqv[:, h % 4, 0:nt],
                                            start=True, stop=False), r=[tokB, self.ones_b], w=[bkq])
                return later
            else:
                rope_ev(kind, bk, ps, qpe, qpe[0:64, h, 0:nt], ("qpe", h))
                if kind == "sw":
                    def later2(h=h):
                        P.pool(lambda e: e.tensor_tensor(out=sqv[0:64, h % 4, 0:nt], in0=qpe[0:64, h, 0:nt],
                                                         in1=qpe[0:64, h, 0:nt], op=ALU.mult),
                               r=[qpe], wp=[(tokB, ("qsq", h % 4))])
                        P.pe(lambda e: e.matmul(bkq[0:65, 0:nt], lhsT=self.ones_b[0:64, 0:65], rhs=sqv[0:64, h % 4, 0:nt],
                                                start=False, stop=True), r=[tokB, self.ones_b], w=[bkq])
                        tq = self.nt_()
                        P.act(lambda e: e.activation(tq[64:65, 0:nt], bkq[64:65, 0:nt], AF.Sqrt,
                                                     scale=self.kmax2[64:65, 0:1]), r=[bkq, self.kmax2], w=[tq])
                        P.dve(lambda e: e.tensor_scalar(out=qpe[64:65, h, 0:nt], in0=tq[64:65, 0:nt], scalar1=-1.0,
                                                        scalar2=None, op0=ALU.mult), r=[tq], wp=[(qpe, ("qb", h))])
                    return later2
            return None
        self.dense(self.wb["w_uq2"][l], tiles, 4, lambda k: (self.sm[k], cqn[k][:, 0:nt]), nt, ev_q)

        ymla = self.big[2]
        ktiles = [(0, 0, NMETA)] + [(i, NMETA + 128 * (i - 1), 128) for i in range(1, self.ntile)]
        ktiles = [kt for kt in ktiles if kt[1] < t0 + nt]
        items = [(h, idx) for h in range(8) for idx in range(len(ktiles))]
        nk = len(ktiles)

        def emit_S(h, idx):
            ti, k0, kn = ktiles[idx]
            qs = max(0, k0 - t0)
            ncol = nt - qs
            bks = self.nb()
            P.pe(lambda e: e.matmul(bks[0:kn, 0:ncol], lhsT=self.kc[:, h, k0:k0 + kn], rhs=qn[:, h, qs:nt],
                                    start=True, stop=False), r=[self.kc, qn], w=[bks])
            P.pe(lambda e: e.matmul(bks[0:kn, 0:ncol], lhsT=self.kpe[0:65, k0:k0 + kn], rhs=qpe[0:65, h, qs:nt],
                                    start=False, stop=True), r=[self.kpe, qpe], w=[bks])
            pt = self.PT[self.pti % len(self.PT)]
            self.pti += 1
            P.act(lambda e: e.activation(pt[0:kn, 0:ncol], bks[0:kn, 0:ncol], AF.Exp, scale=SCALE), r=[bks], w=[pt])
            if k0 >= t0:
                P.pool(lambda e: e.tensor_tensor(out=pt[0:kn, 0:kn], in0=pt[0:kn, 0:kn], in1=self.tri_b[0:kn, 0:kn],
                                                 op=ALU.mult), r=[pt, self.tri_b], w=[pt])
            return pt

        def emit_PV(h, idx, pt):
            ti, k0, kn = ktiles[idx]
            qs = max(0, k0 - t0)
            ncol = nt - qs
            bo = self.bank[4 + 2 * (h % 2)]
            bd = self.bank[5 + 2 * (h % 2)]
            last = idx == nk - 1
            P.pe(lambda e: e.matmul(bo[:, qs:nt], lhsT=self.vc[0:kn, ti, h * 128:(h + 1) * 128], rhs=pt[0:kn, 0:ncol],
                                    start=(idx == 0), stop=last), r=[self.vc, pt], w=[bo])
            P.pe(lambda e: e.matmul(bd[:, qs:nt], lhsT=self.ones_b[0:kn, :], rhs=pt[0:kn, 0:ncol],
                                    start=(idx == 0), stop=last), r=[self.ones_b, pt], w=[bd])
            if last:
                rd = self.nt_()
                P.dve(lambda e: e.reciprocal(rd[:, 0:nt], bd[:, 0:nt]), r=[bd], w=[rd])
                P.dve(lambda e: e.tensor_tensor(out=ymla[:, h, 0:nt], in0=bo[:, 0:nt], in1=rd[:, 0:nt], op=ALU.mult),
                      r=[bo, rd], wp=[(ymla, ("y", h))])

        pend = []
        for (h, idx) in items:
            pend.append((h, idx, emit_S(h, idx)))
            if len(pend) > 2:
                emit_PV(*pend.pop(0))
        while pend:
            emit_PV(*pend.pop(0))

    def gate(self, l, c0, nt, dst):
        P = self.P
        rhs_u = lambda k: (self.u.sub(k), self.u[:, k, 0:nt])

        def ev(tag, bk, ps):
            P.act(lambda e, ps=ps, tag=tag: e.activation(dst[:, tag, 0:nt], ps, AF.Sigmoid), r=[bk],
                  wp=[(dst, ("g", tag))])
        self.dense(self.wb["w_in"][l], self.wtiles(self.wb["w_in"][l].h, c0, 1024, 8), 8, rhs_u, nt, ev)

    def block(self, l, s, b, t0, nt):
        P = self.P
        first = (b == 0)
        if t0 == 0:
            chunks = [(0, NMETA, 0, 0)]
        else:
            chunks = [(j, 128, 128 * j, 1 + 4 * (b - 1) + j) for j in range(4)]
        self.load_block(l, s, t0, nt)
        hsrc = lambda c: (self.h, self.h[:, c, 0:nt])
        usrc = lambda c: (self.u.sub(c), self.u[:, c, 0:nt])
        self.rmsnorm(hsrc, 8, 1024, lambda c: self.vcol(f"nmix{l}", c), usrc, nt, usrc)
        if first:
            P.dve(lambda e: e.memset(self.state_f[:], 0.0), w=[self.state_f])
            P.pool(lambda e: e.memset(self.state_b[:], 0.0), w=[self.state_b])
        if first or b == 1:
            t1 = self.nt_()
            P.act(lambda e, t1=t1: e.activation(t1[0:32, 0:1], self.vcol(f"alog{l}", 0, 32), AF.Exp),
                  r=[self.vecs], w=[t1])
            P.dve(lambda e, t1=t1: e.tensor_scalar(out=self.negA[:, 0:1], in0=t1[0:32, 0:1], scalar1=-1.0, scalar2=None,
                                                   op0=ALU.mult), r=[t1], w=[self.negA])
        self.ssd_inproj(l, nt, first)
        for (j, q, c0, ti) in chunks:
            if nt == 512:
                self.cast_some(2)
            self.ssd_chunk(l, j, q, c0, nt)
        self.ssd_finish(l, nt)
        gs = self.sz[0]
        mixed = self.sz[1]
        self.gate(l, C_GS, nt, gs)

        def ev_bs(tag, bk, ps):
            P.dve(lambda e, ps=ps, tag=tag: e.tensor_tensor(out=mixed[:, tag, 0:nt], in0=ps, in1=gs[:, tag, 0:nt],
                                                           op=ALU.mult), r=[bk, gs], wp=[(mixed, ("m", tag))])
        self.dense(self.wb["w_bs"][l], self.wtiles(self.wb["w_bs"][l].h, 0, 1024, 16), 16,
                   lambda k: (self.big[k // 8], self.big[k // 8][:, k % 8, 0:nt]), nt, ev_bs)
        self.mla(l, b, t0, nt, chunks, first)
        if nt == 512:
            self.dump('ymla', self.big[2], self.big[2][:, :, :], BF16)
            self.dump('yn0', self.big[0], self.big[0][:, :, :], BF16)
        gm = self.sz[0]
        self.gate(l, C_GM, nt, gm)
        ymla = self.big[2]

        def ev_bm(tag, bk, ps):
            t1 = self.nt_()
            P.dve(lambda e, ps=ps, tag=tag, t1=t1: e.tensor_tensor(out=t1[:, 0:nt], in0=ps, in1=gm[:, tag, 0:nt],
                                                                  op=ALU.mult), r=[bk, gm], w=[t1])
            P.pool(lambda e, tag=tag, t1=t1: e.tensor_tensor(out=mixed[:, tag, 0:nt], in0=mixed[:, tag, 0:nt],
                                                            in1=t1[:, 0:nt], op=ALU.add),
                   r=[t1, mixed], wp=[(mixed, ("m2", tag))])
        self.dense(self.wb["w_bm"][l], self.wtiles(self.wb["w_bm"][l].h, 0, 1024, 8), 8, lambda k: (ymla, ymla[:, k, 0:nt]), nt, ev_bm)

        def ev_res(tag, bk, ps):
            P.dve(lambda e, ps=ps, tag=tag: e.tensor_tensor(out=self.h[:, tag, 0:nt], in0=self.h[:, tag, 0:nt], in1=ps,
                                                           op=ALU.add), r=[bk, self.h], wp=[(self.h, ("r", tag))])
        self.dense(self.wb["w_out"][l], self.wtiles(self.wb["w_out"][l].h, 0, 1024, 8), 8, lambda k: (mixed, mixed[:, k, 0:nt]), nt,
                   ev_res)
        self.rmsnorm(hsrc, 8, 1024, lambda c: self.vcol(f"nmlp{l}", c), usrc, nt, usrc)
        abuf = [self.big[0], self.big[1], self.big[2], self.sz[0]]

        def ev_up(tag, bk, ps):
            t1 = self.nt_()
            P.act(lambda e, ps=ps, t1=t1: e.activation(t1[:, 0:nt], ps, AF.Relu), r=[bk], w=[t1])
            at = abuf[tag // 8]
            P.pool(lambda e, t1=t1, at=at, tag=tag: e.tensor_tensor(out=at[:, tag % 8, 0:nt], in0=t1[:, 0:nt],
                                                                   in1=t1[:, 0:nt], op=ALU.mult),
                   r=[t1], wp=[(at, ("a", tag % 8))])
        self.dense(self.wb["w_up"][l], self.wtiles(self.wb["w_up"][l].h, 0, 4096, 8), 8, usrc, nt, ev_up)
        self.dense(self.wb["w_dn"][l], self.wtiles(self.wb["w_dn"][l].h, 0, 1024, 32), 32,
                   lambda k: (abuf[k // 8], abuf[k // 8][:, k % 8, 0:nt]), nt, ev_res)
        if l < self.depth - 1:
            P.dma(self.h1, self.h1[:, :, t0:t0 + nt], self.h, self.h[:, :, 0:nt])
        elif t0 > 0:
            self.rmsnorm(hsrc, 8, 1024, lambda c: self.vcol("fnw", c), hsrc, nt, usrc)
            for j in range(4):
                ot = self.tok[j % 2]
                for half in range(2):
                    bk = self.nb()
                    for cc in range(4):
                        c = half * 4 + cc
                        P.pe(lambda e, bk=bk, c=c, cc=cc, j=j:
                             e.transpose(bk[:, cc * 128:(cc + 1) * 128], self.h[:, c, j * 128:(j + 1) * 128],
                                         self.ident_f[:]), r=[self.h, self.ident_f], w=[bk])
                    P.act(lambda e, bk=bk, ot=ot, half=half: e.copy(ot[:, half * 512:(half + 1) * 512], bk[:, :]),
                          r=[bk], wp=[(ot, ("o", half))])
                a = t0 - NMETA + j * 128
                P.dma(self.out, self.out[s, a:a + 128, :], ot, ot[:, :], partial=(s, a))

    def meta_save(self, l):
        P = self.P
        P.dma(self.mk[l], self.mk[l][:], self.kc, self.kc[:, :, 0:NMETA])
        P.dma(self.mv[l], self.mv[l][:], self.vc, self.vc[0:NMETA, 0, :])
        P.dma(self.mp[l], self.mp[l][:], self.kpe, self.kpe[0:64, 0:NMETA])
        P.dma(self.ms[l], self.ms[l][:], self.state_f, self.state_f[:, :])
        P.dma(self.mh[l], self.mh[l][:], list(self.halo_c), self.halo[:, :, :].rearrange("p c k -> p (c k)"))
        P.dma(self.mm[l], self.mm[l][:], self.kmax2, self.kmax2[:, :])

    def meta_restore(self, l):
        P = self.P
        P.dma(self.kc, self.kc[:, :, 0:NMETA], self.mk[l], self.mk[l][:], partial=("k", 0, "meta"))
        P.dma(self.vc, self.vc[0:NMETA, 0, :], self.mv[l], self.mv[l][:], partial=("v", 0, "meta"))
        P.dma(self.kpe, self.kpe[0:64, 0:NMETA], self.mp[l], self.mp[l][:], partial=("kpe", 0))
        P.dma(self.state_f, self.state_f[:, :], self.ms[l], self.ms[l][:])
        P.act(lambda e: e.copy(self.state_b[:, :], self.state_f[:, :]), r=[self.state_f], w=[self.state_b])
        stg = self.dta[:, :, :].rearrange("p a b -> p (a b)")
        P.dma(self.dta, stg[:, 0:72], self.mh[l], self.mh[l][:])
        P.dve(lambda e: e.tensor_copy(self.halo[:, :, :].rearrange("p c k -> p (c k)"), stg[:, 0:72]),
              r=[self.dta], w=list(self.halo_c))
        P.dma(self.kmax2, self.kmax2[:, :], self.mm[l], self.mm[l][:])

    def build(self):
        self.consts()
        for s in range(self.nseq):
            for l in range(self.depth):
                if s == 0 and l + 1 < self.depth:
                    self.cast_queue(l + 1, ["w_in", "w_kr2", "w_bs", "w_uk", "w_uv", "w_uq2", "w_bm", "w_out", "w_up", "w_dn"])
                if s == 0:
                    self.block(l, s, 0, 0, NMETA)
                    if self.nseq > 1:
                        self.meta_save(l)
                else:
                    self.meta_restore(l)
                for b in range(1, self.nblk + 1):
                    self.block(l, s, b, NMETA + 512 * (b - 1), 512)
                self.cast_some(len(self.castq))
        self.P.finish(final_waits=[self.out] + self.dbg_outs)
        return self.nc


def prep_inputs(inp, depth, nblk, seqs):
    f = lambda a: np.ascontiguousarray(np.asarray(a, dtype=np.float32))
    Lt = NMETA + 512 * nblk
    voff, NV = vec_layout(depth)
    vecs = np.zeros((128, NV), np.float32)

    def put(name, arr):
        arr = np.asarray(arr, np.float32)
        n = arr.shape[0] // 128
        vecs[:, voff[name]:voff[name] + n] = arr.reshape(n, 128).T
    for l in range(depth):
        put(f"nmix{l}", inp["norm_mix_w"][l])
        put(f"nmlp{l}", inp["norm_mlp_w"][l])
        for k in range(4):
            cw = np.asarray(inp["conv_w"][l][k], np.float32)
            vecs[:, voff[f"convw{l}"] + k * 24:voff[f"convw{l}"] + (k + 1) * 24] = cw.reshape(24, 128).T
        put(f"convb{l}", inp["conv_b"][l])
        put(f"ssmw{l}", inp["ssm_norm_w"][l])
        put(f"dskip{l}", np.repeat(np.asarray(inp["d_skip"][l], np.float32), 64))
        put(f"qnw{l}", inp["q_norm_w"][l])
        put(f"kvnw{l}", inp["kv_norm_w"][l])
        vecs[0:32, voff[f"dtb{l}"]] = np.asarray(inp["dt_bias"][l], np.float32)
        vecs[0:32, voff[f"alog{l}"]] = np.asarray(inp["a_log"][l], np.float32)
    put("fnw", inp["final_norm_w"])
    vecs[:, voff["eps"]] = EPS
    vecs[:, voff["one"]] = 1.0
    w_in = f(inp["w_in"])[:depth]
    kr = w_in[:, :, C_KR:C_KR + 64]
    w_kr2 = f(np.concatenate([kr, kr[:, :, 32:64], kr[:, :, 0:32]], axis=2))
    wuq = f(inp["w_uq"])[:depth].reshape(depth, 512, 8, 192)
    nope, pe = wuq[..., :128], wuq[..., 128:]
    pesw = np.concatenate([pe[..., 32:], pe[..., :32]], axis=-1)
    w_uq2 = f(np.concatenate([nope, pe, pesw], axis=-1).reshape(depth, 512, 2048))
    wukv = f(inp["w_ukv"])[:depth].reshape(depth, 256, 8, 256)
    w_uk = f(wukv[..., :128].reshape(depth, 256, 1024))
    w_uv = f(wukv[..., 128:].reshape(depth, 256, 1024))
    inv = (10000.0 ** (-np.arange(0, 64, 2, dtype=np.float32) / np.float32(64))).astype(np.float32)
    ang = (np.arange(Lt, dtype=np.float32)[:, None] * inv[None, :]).astype(np.float32)
    cos, sin = np.cos(ang).astype(np.float32), np.sin(ang).astype(np.float32)
    cos2 = f(np.concatenate([cos, cos], axis=1).T)
    sin2 = f(np.concatenate([-sin, sin], axis=1).T)
    common = {
        "meta": f(inp["meta_tokens"]), "w_in": w_in, "w_kr2": w_kr2, "w_uq2": w_uq2, "w_uk": w_uk, "w_uv": w_uv,
        "w_bs": f(inp["w_branch_ssm"])[:depth], "w_bm": f(inp["w_branch_mla"])[:depth], "w_out": f(inp["w_out"])[:depth],
        "w_up": f(inp["w_mlp_up"])[:depth], "w_dn": f(inp["w_mlp_down"])[:depth], "vecs": vecs, "cos2": cos2, "sin2": sin2,
    }
    x = f(inp["x"])
    maps = []
    for sl in seqs:
        m = dict(common)
        m["x"] = np.ascontiguousarray(x[sl, :512 * nblk, :])
        maps.append(m)
    return maps


_CACHE = {}


def kernel(**inputs):
    nseq, depth, nblk, ncores = 2, 2, 4, 8
    mk = MK(nseq, depth, nblk)
    nc = mk.build()
    seqs = [slice(i * nseq, (i + 1) * nseq) for i in range(ncores)]
    maps = prep_inputs(inputs, depth, nblk, seqs)
    res = run_bass_kernel_spmd(nc, maps, core_ids=list(range(ncores)))
    out = np.concatenate([np.asarray(r["out"], dtype=np.float32) for r in res.results], axis=0)
    return out
```

```python
import contextlib
import numpy as np
import concourse.bass as bass
import concourse.mybir as mybir
from concourse.bass_utils import run_bass_kernel_spmd

F32 = mybir.dt.float32
BF16 = mybir.dt.bfloat16
AF = mybir.ActivationFunctionType
ALU = mybir.AluOpType
AX = mybir.AxisListType

COMPUTE = ("pe", "act", "dve", "pool")
SAME_ENGINE_SYNC = True


class T:
    def __init__(self, name, handle):
        self.name = name
        self.h = handle
        self.writers = []
        self.readers = []
        self.dsem = None
        self.dcount = 0

    def __getitem__(self, k):
        return self.h[k]


class TG:
    def __init__(self, name, handle, n):
        self.name = name
        self.h = handle
        self.subs = [T(f"{name}_s{j}", handle) for j in range(n)]

    def __getitem__(self, k):
        return self.h[k]

    def sub(self, j):
        return self.subs[j]


def _flat(lst):
    out = []
    for t in lst:
        if isinstance(t, TG):
            out += t.subs
        elif isinstance(t, (list, tuple)):
            out += _flat(t)
        else:
            out.append(t)
    return out


def _flatp(lst):
    out = []
    for (t, k) in lst:
        if isinstance(t, TG):
            out += [(x, k) for x in t.subs]
        else:
            out.append((t, k))
    return out


class Op:
    __slots__ = ("eng", "fn", "deps", "idx", "signal", "dma", "dtile", "dval", "waits", "semval")

    def __init__(self, eng, fn):
        self.eng = eng
        self.fn = fn
        self.deps = []
        self.idx = -1
        self.signal = False
        self.dma = False
        self.dtile = None
        self.dval = 0
        self.waits = []
        self.semval = 0


class Prog:
    def __init__(self, nc):
        self.nc = nc
        self.es = contextlib.ExitStack()
        self.streams = {e: [] for e in ("pe", "act", "dve", "pool", "sp")}
        self.known = {e: {} for e in self.streams}
        self.sems = {}

    def sb(self, name, shape, dt):
        h = self.es.enter_context(self.nc.sbuf_tensor(name, list(shape), dt))
        return T(name, h)

    def sbg(self, name, shape, dt, n):
        h = self.es.enter_context(self.nc.sbuf_tensor(name, list(shape), dt))
        return TG(name, h, n)

    def ps(self, name, shape, dt=F32):
        h = self.es.enter_context(self.nc.psum_tensor(name, list(shape), dt))
        return T(name, h)

    def dram(self, name, shape, dt, kind="Internal"):
        h = self.nc.dram_tensor(name, list(shape), dt, kind=kind)
        return T(name, h.ap())

    def sem(self, name):
        if name not in self.sems:
            self.sems[name] = self.es.enter_context(self.nc.semaphore(name))
        return self.sems[name]

    def _add(self, eng, fn, r=(), w=(), wp=(), dma=False):
        op = Op(eng, fn)
        op.dma = dma
        r, w, wp = _flat(r), _flat(w), _flatp(wp)
        deps = []
        raw = set()
        for t in r:
            deps += [o for (o, k) in t.writers]
            raw.update(id(o) for (o, k) in t.writers)
        for t in w:
            deps += [o for (o, k) in t.writers]
            deps += t.readers
        for (t, key) in wp:
            deps += t.readers
            deps += [o for (o, k) in t.writers if k is None or k == key]
        for t in r:
            t.readers.append(op)
        for t in w:
            t.writers = [(op, None)]
            t.readers = []
        for (t, key) in wp:
            if t.readers:
                t.writers = [(op, key)]
                t.readers = []
            else:
                t.writers = [(o, k) for (o, k) in t.writers if k != key] + [(op, key)]
        if dma:
            dts = [t for t in w] + [t for (t, k) in wp]
            assert len(dts) == 1
            t = dts[0]
            if t.dsem is None:
                t.dsem = self.sem("d_" + t.name)
            t.dcount += 16
            op.dtile = t
            op.dval = t.dcount
        op.idx = len(self.streams[eng])
        kn = self.known[eng]
        best = {}
        for d in deps:
            if d is op:
                continue
            if d.dma:
                key = ("dma", d.dtile.name)
                val = d.dval
            else:
                if d.eng == eng and (eng == "pe" or not SAME_ENGINE_SYNC):
                    continue
                key = d.eng
                val = d.idx
            if kn.get(key, -1) >= val:
                continue
            if key not in best or best[key][0] < val:
                best[key] = (val, d)
        for key, (val, d) in best.items():
            kn[key] = val
            if not d.dma:
                d.signal = True
            op.deps.append(d)
        self.streams[eng].append(op)
        return op

    def pe(self, fn, r=(), w=(), wp=()):
        return self._add("pe", fn, r, w, wp)

    def act(self, fn, r=(), w=(), wp=()):
        return self._add("act", fn, r, w, wp)

    def dve(self, fn, r=(), w=(), wp=()):
        return self._add("dve", fn, r, w, wp)

    def pool(self, fn, r=(), w=(), wp=()):
        return self._add("pool", fn, r, w, wp)

    def dma(self, out_t, out_ap, in_t, in_ap, q="sp", partial=None, **kw):
        fn = lambda e: e.dma_start(out=out_ap, in_=in_ap, **kw)
        r = [in_t] if in_t is not None else []
        if partial is None:
            return self._add(q, fn, r=r, w=[out_t], dma=True)
        return self._add(q, fn, r=r, wp=[(out_t, partial)], dma=True)

    def finish(self, final_waits=()):
        nc = self.nc
        esem = {e: self.sem("e_" + e) for e in COMPUTE}
        self.maxsem = 0
        for e in COMPUTE:
            c = 0
            for op in self.streams[e]:
                if op.signal:
                    c += 1
                    op.semval = c
            self.maxsem = max(self.maxsem, c)
        for e, ops in self.streams.items():
            for op in ops:
                for d in op.deps:
                    if d.dma:
                        op.waits.append((d.dtile.dsem, d.dval))
                    else:
                        op.waits.append((esem[d.eng], d.semval))
        engmap = {"pe": "tensor", "act": "scalar", "dve": "vector", "pool": "gpsimd", "sp": "sync"}
        with nc.Block() as block:
            for e, ops in self.streams.items():
                def body(eng, e=e, ops=ops):
                    for op in ops:
                        for (s, v) in op.waits:
                            eng.wait_ge(s, v)
                        ins = op.fn(eng)
                        if op.dma:
                            ins.then_inc(op.dtile.dsem, 16)
                        elif op.signal:
                            ins.then_inc(esem[e], 1)
                    if e == "sp":
                        for t in final_waits:
                            eng.wait_ge(t.dsem, t.dcount)
                getattr(block, engmap[e])(body)
        self.es.close()


D = 1024
NMETA = 16
EPS = 1e-6
SCALE = 192 ** -0.5
WSLOT = 4352
NEGBIG = -30000.0

C_Z, C_XBC, C_DT, C_CQ, C_CKV, C_KR, C_GS, C_GM = 0, 2048, 5120, 5152, 5664, 5920, 5984, 7008


def vec_layout(depth):
    off = {}
    n = 0

    def add(name, w):
        nonlocal n
        off[name] = n
        n += w
    for l in range(depth):
        for nm, w in (("nmix", 8), ("nmlp", 8), ("convw", 96), ("convb", 24), ("ssmw", 16), ("dskip", 16),
                      ("qnw", 4), ("kvnw", 2), ("dtb", 1), ("alog", 1)):
            add(f"{nm}{l}", w)
    add("fnw", 8)
    add("eps", 1)
    add("one", 1)
    return off, n


class MK:
    def __init__(self, nseq=2, depth=2, nblk=4):
        self.nseq, self.depth, self.nblk = nseq, depth, nblk
        self.Lt = NMETA + 512 * nblk
        self.ntile = 1 + 4 * nblk
        nc = bass.Bass("TRN2", target_bir_lowering=False)
        self.nc = nc
        P = Prog(nc)
        self.P = P
        Lt = self.Lt
        self.voff, self.NV = vec_layout(depth)
        di = lambda n, s: P.dram(n, s, F32, kind="ExternalInput")
        self.x = di("x", [nseq, 512 * nblk, D])
        self.meta = di("meta", [NMETA, D])
        self.w_in = di("w_in", [depth, D, 8032])
        self.w_kr2 = di("w_kr2", [depth, D, 128])
        self.w_uq2 = di("w_uq2", [depth, 512, 2048])
        self.w_uk = di("w_uk", [depth, 256, 1024])
        self.w_uv = di("w_uv", [depth, 256, 1024])
        self.w_bs = di("w_bs", [depth, 2048, D])
        self.w_bm = di("w_bm", [depth, D, D])
        self.w_out = di("w_out", [depth, D, D])
        self.w_up = di("w_up", [depth, D, 4096])
        self.w_dn = di("w_dn", [depth, 4096, D])
        self.vecs_d = di("vecs", [128, self.NV])
        self.cos_d = di("cos2", [64, Lt])
        self.sin_d = di("sin2", [64, Lt])
        self.wsrc = {"w_in": self.w_in, "w_kr2": self.w_kr2, "w_uq2": self.w_uq2, "w_uk": self.w_uk, "w_uv": self.w_uv,
                     "w_bs": self.w_bs, "w_bm": self.w_bm, "w_out": self.w_out, "w_up": self.w_up, "w_dn": self.w_dn}
        self.wb = {}
        for nm, t in self.wsrc.items():
            shp = list(t.h.shape)
            self.wb[nm] = [P.dram(f"{nm}_b{l}", shp[1:], BF16, kind="Internal") for l in range(depth)]
        self.out = P.dram("out", [nseq, 512 * nblk, D], F32, kind="ExternalOutput")
        self.h1 = P.dram("h1s", [128, 8, Lt], F32, kind="Internal")
        self.mk = [P.dram(f"mk{l}", [128, 8, NMETA], BF16) for l in range(depth)]
        self.mv = [P.dram(f"mv{l}", [NMETA, 1024], BF16) for l in range(depth)]
        self.mp = [P.dram(f"mp{l}", [64, NMETA], BF16) for l in range(depth)]
        self.ms = [P.dram(f"ms{l}", [128, 2048], F32) for l in range(depth)]
        self.mh = [P.dram(f"mh{l}", [128, 72], F32) for l in range(depth)]
        self.mm = [P.dram(f"mm{l}", [65, 1], F32) for l in range(depth)]

        sb = P.sb
        self.ident_f = sb("ident_f", [128, 128], F32)
        self.ident_b = sb("ident_b", [128, 128], BF16)
        self.ones_b = sb("ones_b", [128, 128], BF16)
        self.ones_f = sb("ones_f", [128, 128], F32)
        self.tri_b = sb("tri_b", [128, 128], BF16)
        self.tri_f = sb("tri_f", [128, 128], F32)
        self.triU_f = sb("triU_f", [128, 128], F32)
        self.neg_f = sb("neg_f", [128, 128], F32)
        self.vecs = sb("vecs_s", [128, self.NV], F32)
        self.h = sb("h", [128, 8, 512], F32)
        self.u = P.sbg("u", [128, 8, 512], BF16, 8)
        self.ws = [sb(f"ws{i}", [128, WSLOT], BF16) for i in range(3)]
        self.wi = 0
        self.kc = sb("kc", [128, 8, Lt], BF16)
        self.vc = sb("vc", [128, self.ntile, 1024], BF16)
        self.kpe = sb("kpe", [65, Lt], BF16)
        self.state_f = sb("state_f", [128, 2048], F32)
        self.state_b = sb("state_b", [128, 2048], BF16)
        self.kmax2 = sb("kmax2", [65, 1], F32)
        self.kmtmp = sb("kmtmp", [65, 1], F32)
        self.big = [P.sbg(f"big{i}", [128, 8, 512], BF16, 4) for i in range(3)]
        self.sz = [sb(f"sz{i}", [128, 8, 512], BF16) for i in range(2)]
        self.tok = [sb(f"tok{i}", [128, 1024], F32) for i in range(2)]
        self.sm = [sb(f"sm{i}", [128, 4, 128], BF16) for i in range(6)]
        self.tmp = [sb(f"tmp{i}", [128, 512], F32) for i in range(2)]
        self.ti = 0
        self.dta = sb("dta", [128, 4, 64], F32)
        self.acsE2 = [sb(f"acsE{i}", [128, 96], F32) for i in range(2)]
        self.cbT2 = [self.sm[4], sb("cbTb", [128, 4, 128], BF16)]
        self.negA = sb("negA", [32, 1], F32)
        self.a42 = [sb(f"a4{i}", [128, 4, 32], BF16) for i in range(2)]
        self.neg_b = sb("neg_b", [128, 128], BF16)
        self.xpre = sb("xpre", [128, 515], F32)
        self.acc = sb("acc", [128, 512], F32)
        self.halo = sb("halo", [128, 24, 3], F32)
        self.halo_c = [T(f"halo_c{c}", self.halo.h) for c in range(24)]
        self.PT = [sb(f"PT{i}", [128, 512], BF16) for i in range(4)]
        self.pti = 0
        self.bank = [P.ps(f"bk{i}", [128, 512], F32) for i in range(8)]
        self.bi = 0
        self.dbg_outs = []
        self.chunk_ctr = 0
        self.castq = []

    def dump(self, name, t, ap, dt):
        if not getattr(self, "debug", False):
            return
        shape = list(ap.shape)
        d = self.P.dram("dbg_" + name, shape, dt, kind="ExternalOutput")
        self.P.dma(d, d[:], t, ap)
        self.dbg_outs.append(d)

    def nb(self):
        b = self.bank[self.bi % 4]
        self.bi += 1
        return b

    def nt_(self):
        t = self.tmp[self.ti % 2]
        self.ti += 1
        return t

    def vcol(self, name, c=0, rows=128):
        o = self.voff[name] + c
        return self.vecs[0:rows, o:o + 1]

    def consts(self):
        P = self
        Pr = self.P
        Pr.dma(self.vecs, self.vecs[:], self.vecs_d, self.vecs_d[:])
        for t, dt in ((self.ident_f, F32), (self.ident_b, BF16)):
            Pr.pool(lambda e, t=t: e.memset(t[:], 1.0), w=[t])
            Pr.pool(lambda e, t=t: e.affine_select(out=t[:], in_=t[:], pattern=[[-1, 128]], compare_op=ALU.is_equal,
                                                   fill=0.0, base=0, channel_multiplier=1), r=[t], w=[t])
        for t in (self.ones_b, self.ones_f):
            Pr.pool(lambda e, t=t: e.memset(t[:], 1.0), w=[t])
        for t in (self.tri_b, self.tri_f):
            Pr.pool(lambda e, t=t: e.memset(t[:], 1.0), w=[t])
            Pr.pool(lambda e, t=t: e.affine_select(out=t[:], in_=t[:], pattern=[[1, 128]], compare_op=ALU.is_ge,
                                                   fill=0.0, base=0, channel_multiplier=-1), r=[t], w=[t])
        t = self.triU_f
        Pr.pool(lambda e, t=t: e.memset(t[:], 1.0), w=[t])
        Pr.pool(lambda e, t=t: e.affine_select(out=t[:], in_=t[:], pattern=[[-1, 128]], compare_op=ALU.is_gt,
                                               fill=0.0, base=0, channel_multiplier=1), r=[t], w=[t])
        t = self.neg_f
        Pr.pool(lambda e, t=t: e.memset(t[:], 0.0), w=[t])
        Pr.pool(lambda e, t=t: e.affine_select(out=t[:], in_=t[:], pattern=[[1, 128]], compare_op=ALU.is_ge,
                                               fill=NEGBIG, base=0, channel_multiplier=-1), r=[t], w=[t])
        Pr.pool(lambda e: e.tensor_copy(self.neg_b[:], self.neg_f[:]), r=[self.neg_f], w=[self.neg_b])
        self.cast_weights(0, ["w_in", "w_kr2", "w_bs", "w_uk", "w_uv", "w_uq2", "w_bm", "w_out", "w_up", "w_dn"])
        Pr.pool(lambda e: e.memset(self.kpe[64:65, :], 1.0), wp=[(self.kpe, "ones")])

    def cast_queue(self, l, names):
        for nm in names:
            src = self.wsrc[nm]
            dst = self.wb[nm][l]
            R_, C_ = dst.h.shape
            for r0 in range(0, R_, 1024):
                r1 = min(R_, r0 + 1024)
                for c0 in range(0, C_, 1024):
                    c1 = min(C_, c0 + 1024)
                    self.castq.append((dst, dst[r0:r1, c0:c1], src, src[l][r0:r1, c0:c1], (r0, c0)))

    def cast_some(self, n):
        while n > 0 and self.castq:
            dst, dap, src, sap, key = self.castq.pop(0)
            self.P.dma(dst, dap, src, sap, q="pool", partial=key)
            n -= 1

    def cast_weights(self, l, names):
        for nm in names:
            src = self.wsrc[nm]
            dst = self.wb[nm][l]
            R_, C_ = dst.h.shape
            for c0 in range(0, C_, 2048):
                c1 = min(C_, c0 + 2048)
                self.P.dma(dst, dst[:, c0:c1], src, src[l][:, c0:c1], q="pool", partial=c0)

    def wload(self, wT, wap, Kc, ncols):
        assert Kc * ncols <= WSLOT
        s = self.ws[self.wi % 3]
        self.wi += 1
        view = s[:, 0:Kc * ncols].rearrange("p (k n) -> p k n", n=ncols)
        self.P.dma(s, view, wT, wap.rearrange("(k p) n -> p k n", p=128), q="sp")
        return s, view

    def dense(self, wT, tiles, Kc, rhs, nt, evac):
        P = self.P
        loaded = [None] * len(tiles)
        loaded[0] = self.wload(wT, tiles[0][0], Kc, tiles[0][1])
        deferred = []
        for i, (wap, ncols, chunks) in enumerate(tiles):
            if i + 1 < len(tiles):
                loaded[i + 1] = self.wload(wT, tiles[i + 1][0], Kc, tiles[i + 1][1])
            s, view = loaded[i]
            for (off, width, tag) in chunks:
                bk = self.nb()
                for k in range(Kc):
                    rt, rap = rhs(k)
                    P.pe(lambda e, bk=bk, view=view, k=k, off=off, width=width, rap=rap:
                         e.matmul(bk[0:width, 0:nt], lhsT=view[:, k, off:off + width], rhs=rap,
                                  start=(k == 0), stop=(k == Kc - 1)),
                         r=[s, rt], w=[bk])
                while len(deferred) > 1:
                    deferred.pop(0)()
                d = evac(tag, bk, bk[0:width, 0:nt])
                if d is not None:
                    deferred.append(d)
        while deferred:
            deferred.pop(0)()

    def wtiles(self, wbase, c0, ncols_total, Kc, per=None, width=128, tag0=0):
        if per is None:
            per = (WSLOT // Kc) // width * width
            per = min(per, 512)
        tiles = []
        c = 0
        tag = tag0
        while c < ncols_total:
            n = min(per, ncols_total - c)
            chunks = []
            for o in range(0, n, width):
                chunks.append((o, min(width, n - o), tag))
                tag += 1
            tiles.append((wbase[:, c0 + c:c0 + c + n], n, chunks))
            c += n
        return tiles

    def rmsnorm(self, src_fn, nch, nfeat, wcol, out_fn, nt, sq_fn):
        P = self.P
        for c in range(nch):
            st, sap = src_fn(c)
            qt, qap = sq_fn(c)
            P.act(lambda e, sap=sap, qap=qap: e.activation(qap, sap, AF.Square), r=[st], wp=[(qt, ("sq", c))])
        bk = self.nb()
        for c in range(nch):
            qt, qap = sq_fn(c)
            P.pe(lambda e, bk=bk, qap=qap, c=c: e.matmul(bk[:, 0:nt], lhsT=self.ones_b[:], rhs=qap,
                                                        start=(c == 0), stop=(c == nch - 1)),
                 r=[qt, self.ones_b], w=[bk])
        rstd = self.nt_()
        P.act(lambda e, bk=bk, rstd=rstd: e.activation(rstd[:, 0:nt], bk[:, 0:nt], AF.Sqrt,
                                                       bias=self.vcol("eps"), scale=1.0 / nfeat),
              r=[bk, self.vecs], w=[rstd])
        P.dve(lambda e, rstd=rstd: e.reciprocal(rstd[:, 0:nt], rstd[:, 0:nt]), r=[rstd], w=[rstd])
        for c in range(nch):
            st, sap = src_fn(c)
            ot, oap = out_fn(c)
            P.dve(lambda e, sap=sap, oap=oap, c=c, rstd=rstd:
                  e.scalar_tensor_tensor(out=oap, in0=sap, scalar=wcol(c), in1=rstd[:, 0:nt],
                                         op0=ALU.mult, op1=ALU.mult),
                  r=[st, rstd, self.vecs], wp=[(ot, ("n", c))])

    def load_block(self, l, s, t0, nt):
        P = self.P
        if l > 0:
            P.dma(self.h, self.h[:, :, 0:nt], self.h1, self.h1[:, :, t0:t0 + nt])
            return
        nch = max(1, nt // 128)
        for j in range(nch):
            q = min(128, nt)
            xin = self.tok[j % 2]
            if t0 == 0:
                src_t, src = self.meta, self.meta[:, :]
            else:
                a = t0 - NMETA + j * 128
                src_t, src = self.x, self.x[s, a:a + q, :]
            P.dma(xin, xin[0:q, :], src_t, src)
            for half in range(2):
                bk = self.nb()
                for cc in range(4):
                    c = half * 4 + cc
                    P.pe(lambda e, bk=bk, xin=xin, c=c, cc=cc, q=q:
                         e.transpose(bk[:, cc * 128:cc * 128 + q], xin[0:q, c * 128:(c + 1) * 128],
                                     self.ident_f[0:q, 0:q]),
                         r=[xin, self.ident_f], w=[bk])
                P.act(lambda e, bk=bk, half=half, j=j, q=q:
                      e.copy(self.h[:, half * 4:half * 4 + 4, j * 128:j * 128 + q],
                             bk[:, :].rearrange("p (c t) -> p c t", t=128)[:, :, 0:q]),
                      r=[bk], wp=[(self.h, ("ld", j, half))])

    def xa(self, c):
        return self.big[c // 8], c % 8

    def ssd_inproj(self, l, nt, first):
        P = self.P
        W = self.wb["w_in"][l].h
        rhs = lambda k: (self.u.sub(k), self.u[:, k, 0:nt])
        def ev_z(tag, bk, ps):
            t = self.sz[tag // 8]
            P.act(lambda e, t=t, tag=tag, ps=ps: e.activation(t[:, tag % 8, 0:nt], ps, AF.Silu),
                  r=[bk], wp=[(t, ("z", tag))])
        self.dense(self.wb["w_in"][l], self.wtiles(W, C_Z, 2048, 8), 8, rhs, nt, ev_z)
        if first:
            P.dve(lambda e: e.memset(self.halo[:], 0.0), w=list(self.halo_c))

        xsets = [self.xpre, self.tok[0], self.tok[1]]
        asets = [self.acc, self.tmp[0], self.tmp[1]]

        pendx = []

        def emit_x(items):
            info = []
            for (c, bk, ps) in items:
                si = c % 3
                info.append((c, bk, ps, xsets[si], asets[si], self.halo_c[c]))
            cw = lambda c, k: self.vcol(f"convw{l}", k * 24 + c)
            for (c, bk, ps, xp, acc, hc) in info:
                P.act(lambda e, ps=ps, xp=xp: e.copy(xp[:, 3:3 + nt], ps), r=[bk], wp=[(xp, "main")])
            for (c, bk, ps, xp, acc, hc) in info:
                P.dve(lambda e, c=c, xp=xp: e.tensor_copy(xp[:, 0:3], self.halo[:, c, :]), r=[hc], wp=[(xp, "halo")])
            for (c, bk, ps, xp, acc, hc) in info:
                P.dve(lambda e, c=c, xp=xp, acc=acc: e.tensor_scalar(out=acc[:, 0:nt], in0=xp[:, 0:nt], scalar1=cw(c, 0),
                                                                     scalar2=None, op0=ALU.mult),
                      r=[xp, self.vecs], w=[acc])
            for k in (1, 2, 3):
                for (c, bk, ps, xp, acc, hc) in info:
                    P.dve(lambda e, k=k, c=c, xp=xp, acc=acc:
                          e.scalar_tensor_tensor(out=acc[:, 0:nt], in0=xp[:, k:k + nt], scalar=cw(c, k), in1=acc[:, 0:nt],
                                                 op0=ALU.mult, op1=ALU.add), r=[xp, acc, self.vecs], w=[acc])
            for (c, bk, ps, xp, acc, hc) in info:
                P.dve(lambda e, c=c, xp=xp: e.tensor_copy(self.halo[:, c, :], xp[:, nt:nt + 3]), r=[xp], w=[hc])
            for (c, bk, ps, xp, acc, hc) in info:
                bt, bc = self.xa(c)
                P.act(lambda e, bt=bt, bc=bc, c=c, acc=acc: e.activation(bt[:, bc, 0:nt], acc[:, 0:nt], AF.Silu,
                                                                        bias=self.vcol(f"convb{l}", c)),
                      r=[acc, self.vecs], wp=[(bt, ("x", bc))])

        def ev_x(tag, bk, ps):
            pendx.append((tag, bk, ps))
            if len(pendx) == 2:
                emit_x(list(pendx))
                pendx.clear()
        self.dense(self.wb["w_in"][l], self.wtiles(W, C_XBC, 3072, 8), 8, rhs, nt, ev_x)

        def ev_dt(tag, bk, ps):
            t1 = self.nt_()
            P.act(lambda e, ps=ps, t1=t1: e.activation(t1[0:32, 0:nt], ps, AF.Exp, bias=self.vcol(f"dtb{l}", 0, 32)),
                  r=[bk, self.vecs], w=[t1])
            P.act(lambda e, t1=t1: e.activation(t1[0:32, 0:nt], t1[0:32, 0:nt], AF.Ln, bias=self.vcol("one", 0, 32)),
                  r=[t1, self.vecs], w=[t1])
            t2 = self.nt_()
            P.dve(lambda e, t1=t1, t2=t2: e.tensor_scalar(out=t2[0:32, 0:nt], in0=t1[0:32, 0:nt],
                                                          scalar1=self.negA[:, 0:1], scalar2=None, op0=ALU.mult),
                  r=[t1, self.negA], w=[t2])
            nch = max(1, nt // 128)
            q = min(128, nt)
            bk2 = self.nb()
            for j in range(nch):
                for (ii, tt) in ((0, t1), (1, t2)):
                    P.pe(lambda e, j=j, ii=ii, tt=tt, bk2=bk2:
                         e.transpose(bk2[0:q, j * 64 + ii * 32:j * 64 + ii * 32 + 32],
                                     tt[0:32, j * 128:j * 128 + q], self.ident_f[0:32, 0:32]),
                         r=[tt, self.ident_f], w=[bk2])
            P.act(lambda e, bk2=bk2: e.copy(self.dta[0:q, 0:nch, :],
                                           bk2[0:q, 0:nch * 64].rearrange("p (j x) -> p j x", x=64)),
                  r=[bk2], w=[self.dta])
        self.dense(self.wb["w_in"][l], [(W[:, C_DT:C_DT + 32], 32, [(0, 32, 0)])], 8, rhs, nt, ev_dt)

    def ssd_chunk(self, l, j, q, c0, nt):
        P = self.P
        LT, MT, Btok = self.sm[0:2], self.sm[2:4], self.sm[5]
        par = self.chunk_ctr % 2
        self.chunk_ctr += 1
        cbT = self.cbT2[par]
        acsE = self.acsE2[par]
        bsub = [self.big[i].sub(j) for i in range(3)]
        tokA, tokB = self.tok
        tokA_b = tokA[:].bitcast(BF16)
        tokB_b = tokB[:].bitcast(BF16)
        bfv = lambda bk: bk[:].bitcast(BF16)
        dt_tok = self.dta[0:q, j, 0:32]
        a_tok = self.dta[0:q, j, 32:64]
        for half in range(2):
            bk = self.nb()
            for cc in range(8):
                c = half * 8 + cc
                bt, bc = self.xa(c)
                P.pe(lambda e, bk=bk, bt=bt, bc=bc, cc=cc:
                     e.transpose(bfv(bk)[0:q, cc * 128:(cc + 1) * 128], bt[:, bc, c0:c0 + q], self.ident_b[:]),
                     r=[bsub[half], self.ident_b], w=[bk])
            P.dve(lambda e, bk=bk, half=half:
                  e.tensor_tensor(out=tokA_b[0:q, half * 1024:(half + 1) * 1024].rearrange("p (h x) -> p h x", x=64),
                                  in0=bfv(bk)[0:q, :].rearrange("p (h x) -> p h x", x=64),
                                  in1=dt_tok[:, half * 16:(half + 1) * 16].unsqueeze(2).to_broadcast([q, 16, 64]),
                                  op=ALU.mult),
                  r=[bk, self.dta], wp=[(tokA, half)])
        bk = self.nb()
        bt = self.big[2]
        for g in range(4):
            P.pe(lambda e, bk=bk, g=g: e.transpose(bfv(bk)[0:q, g * 128:(g + 1) * 128], bt[:, g, c0:c0 + q],
                                                  self.ident_b[:]), r=[bsub[2], self.ident_b], w=[bk])
        P.act(lambda e, bk=bk: e.copy(Btok[0:q, :, :], bfv(bk)[0:q, 0:512].rearrange("p (g x) -> p g x", x=128)),
              r=[bk], w=[Btok])
        bk = self.nb()
        P.pe(lambda e, bk=bk: e.matmul(bk[0:q, 0:32], lhsT=self.tri_f[0:q, 0:q], rhs=a_tok, start=True, stop=True),
             r=[self.tri_f, self.dta], w=[bk])
        P.pe(lambda e, bk=bk: e.matmul(bk[0:q, 32:64], lhsT=self.triU_f[0:q, 0:q], rhs=a_tok, start=True, stop=True),
             r=[self.triU_f, self.dta], w=[bk])
        P.pe(lambda e, bk=bk: e.matmul(bk[0:128, 64:96], lhsT=self.ones_f[0:q, 0:128], rhs=a_tok, start=True, stop=True),
             r=[self.ones_f, self.dta], w=[bk])
        P.act(lambda e, bk=bk: e.activation(acsE[0:q, 0:64], bk[0:q, 0:64], AF.Exp), r=[bk], wp=[(acsE, 0)])
        P.act(lambda e, bk=bk: e.activation(acsE[:, 64:96], bk[:, 64:96], AF.Exp), r=[bk], wp=[(acsE, 1)])
        bk = self.nb()
        for g in range(4):
            P.pe(lambda e, bk=bk, g=g: e.matmul(bk[0:q, g * 128:g * 128 + q], lhsT=bt[:, g, c0:c0 + q],
                                               rhs=bt[:, 4 + g, c0:c0 + q], start=True, stop=True),
                 r=[bsub[2]], w=[bk])
        P.act(lambda e, bk=bk: e.copy(cbT[0:q, :, 0:q], bk[0:q, :].rearrange("p (g x) -> p g x", x=128)[:, :, 0:q]),
              r=[bk], w=[cbT])
        a4 = self.a42[par]
        P.dve(lambda e: e.tensor_copy(a4[0:q, 0, :], a_tok), r=[self.dta], wp=[(a4, 0)])
        P.dve(lambda e: e.tensor_tensor(out=a4[0:q, 1, :], in0=a_tok, in1=a4[0:q, 0, :], op=ALU.subtract),
              r=[self.dta, a4], wp=[(a4, 1)])
        P.dve(lambda e: e.tensor_scalar(out=a4[0:q, 2:4, :], in0=a4[0:q, 0:2, :], scalar1=-1.0, scalar2=None,
                                        op0=ALU.mult), r=[a4], wp=[(a4, 2)])
        v3 = lambda ap: ap.rearrange("p (h x) -> p h x", x=128)[:, :, 0:q]

        def emit_decay(g, hb):
            h0 = g * 8 + hb * 4
            bkd = self.nb()
            for i, part in enumerate((2, 3)):
                P.pe(lambda e, part=part, i=i:
                     e.matmul(v3(bkd[0:q, :]), lhsT=self.tri_b[0:q, 0:q],
                              rhs=a4[0:q, part, h0:h0 + 4].unsqueeze(2).to_broadcast([q, 4, q]),
                              start=(i == 0), stop=False), r=[self.tri_b, a4], w=[bkd])
            P.pe(lambda e: e.matmul(v3(bkd[0:q, :]), lhsT=self.ident_b[0:q, 0:q],
                                    rhs=self.neg_b[0:q, 0:q].unsqueeze(1).to_broadcast([q, 4, q]),
                                    start=False, stop=False), r=[self.ident_b, self.neg_b], w=[bkd])
            for hh in range(4):
                for part in (0, 1):
                    P.pe(lambda e, hh=hh, part=part:
                         e.matmul(bkd[0:q, hh * 128:hh * 128 + q],
                                  lhsT=a4[0:q, part, h0 + hh:h0 + hh + 1].to_broadcast([q, q]),
                                  rhs=self.tri_b[0:q, 0:q], start=False, stop=(hh == 3 and part == 1)),
                         r=[self.tri_b, a4], w=[bkd])
            lt = LT[hb]
            mt = MT[hb]
            P.act(lambda e: e.activation(lt[0:q, :, 0:q], v3(bkd[0:q, :]), AF.Exp), r=[bkd], w=[lt])
            P.dve(lambda e: e.tensor_tensor(out=mt[0:q, :, 0:q], in0=lt[0:q, :, 0:q],
                                            in1=cbT[0:q, g:g + 1, 0:q].to_broadcast([q, 4, q]), op=ALU.mult),
                  r=[lt, cbT], w=[mt])
            return mt

        def emit_yoff(g):
            bko = self.bank[6 + g % 2]
            P.pe(lambda e: e.matmul(bko[0:q, :], lhsT=bt[:, 4 + g, c0:c0 + q],
                                    rhs=self.state_b[:, g * 512:(g + 1) * 512], start=True, stop=True),
                 r=[bsub[2], self.state_b], w=[bko])
            ty = self.nt_()
            P.dve(lambda e: e.tensor_tensor(out=ty[0:q, :].rearrange("p (h x) -> p h x", x=64),
                                            in0=bko[0:q, :].rearrange("p (h x) -> p h x", x=64),
                                            in1=acsE[0:q, g * 8:(g + 1) * 8].unsqueeze(2).to_broadcast([q, 8, 64]),
                                            op=ALU.mult), r=[bko, acsE], w=[ty])
            return ty

        def emit_ydiag(g, hb, mt, ty):
            bky = self.bank[4 + g % 2]
            h0 = g * 8 + hb * 4
            for hh in range(4):
                hd = h0 + hh
                P.pe(lambda e, hh=hh, hd=hd:
                     e.matmul(bky[0:q, (hb * 4 + hh) * 64:(hb * 4 + hh + 1) * 64], lhsT=mt[0:q, hh, 0:q],
                              rhs=tokA_b[0:q, hd * 64:(hd + 1) * 64], start=True, stop=True),
                     r=[mt, tokA], w=[bky])
            if hb == 1:
                P.dve(lambda e: e.tensor_tensor(out=tokB_b[0:q, g * 512:(g + 1) * 512], in0=bky[0:q, :], in1=ty[0:q, :],
                                                op=ALU.add), r=[bky, ty], wp=[(tokB, g)])

        batches = [(g, hb) for g in range(4) for hb in range(2)]
        tys = {}
        tys[0] = emit_yoff(0)
        mts = {0: emit_decay(*batches[0])}
        for bi, (g, hb) in enumerate(batches):
            if bi + 1 < len(batches):
                g2, hb2 = batches[bi + 1]
                if hb2 == 0:
                    tys[g2] = emit_yoff(g2)
                mts[bi + 1] = emit_decay(g2, hb2)
            emit_ydiag(g, hb, mts.pop(bi), tys[g])
        if j == 1 and nt == 512:
            self.dump("ytok", tokB, tokB_b[0:q, :], BF16)
            self.dump("xdt", tokA, tokA_b[0:q, :], BF16)
            self.dump("cbT", cbT, cbT[0:q, :, :], BF16)
            self.dump("acsE", acsE, acsE[:, :], F32)
            self.dump("mt", MT[1], MT[1][0:q, :, :], BF16)
            self.dump("lt", LT[1], LT[1][0:q, :, :], BF16)
            self.dump("dta", self.dta, self.dta[:, :, :], F32)
        for half in range(2):
            bk = self.nb()
            for cc in range(8):
                c = half * 8 + cc
                P.pe(lambda e, bk=bk, c=c, cc=cc:
                     e.transpose(bfv(bk)[:, cc * 128:cc * 128 + q], tokB_b[0:q, c * 128:(c + 1) * 128],
                                 self.ident_b[0:q, 0:q]),
                     r=[tokB, self.ident_b], w=[bk])
            xt = self.big[half]
            for cc in range(8):
                c = half * 8 + cc
                P.dve(lambda e, bk=bk, xt=xt, c=c, cc=cc:
                      e.scalar_tensor_tensor(out=xt[:, cc, c0:c0 + q], in0=xt[:, cc, c0:c0 + q],
                                             scalar=self.vcol(f"dskip{l}", c), in1=bfv(bk)[:, cc * 128:cc * 128 + q],
                                             op0=ALU.mult, op1=ALU.add),
                      r=[bk, bsub[half], self.vecs], wp=[(bsub[half], ("y", cc))])
        P.pool(lambda e: e.tensor_tensor(out=tokB_b[0:q, :].rearrange("p (h x) -> p h x", x=64),
                                         in0=tokA_b[0:q, :].rearrange("p (h x) -> p h x", x=64),
                                         in1=acsE[0:q, 32:64].unsqueeze(2).to_broadcast([q, 32, 64]), op=ALU.mult),
               r=[tokA, acsE], w=[tokB])
        P.pool(lambda e: e.tensor_tensor(out=self.state_f[:, :].rearrange("p (h x) -> p h x", x=64),
                                         in0=self.state_f[:, :].rearrange("p (h x) -> p h x", x=64),
                                         in1=acsE[:, 64:96].unsqueeze(2).to_broadcast([128, 32, 64]), op=ALU.mult),
               r=[self.state_f, acsE], w=[self.state_f])
        for g in range(4):
            bk = self.nb()
            P.pe(lambda e, bk=bk, g=g: e.matmul(bk[:, :], lhsT=Btok[0:q, g, :], rhs=tokB_b[0:q, g * 512:(g + 1) * 512],
                                               start=True, stop=True), r=[Btok, tokB], w=[bk])
            P.dve(lambda e, bk=bk, g=g: e.tensor_tensor(out=self.state_f[:, g * 512:(g + 1) * 512],
                                                       in0=self.state_f[:, g * 512:(g + 1) * 512], in1=bk[:, :],
                                                       op=ALU.add),
                  r=[bk, self.state_f], wp=[(self.state_f, g)])
        P.act(lambda e: e.copy(self.state_b[:, :], self.state_f[:, :]), r=[self.state_f], w=[self.state_b])

    def ssd_finish(self, l, nt):
        P = self.P
        for g in range(4):
            i, o = g // 2, (g % 2) * 4
            eng = P.pool if g % 2 == 0 else P.dve
            eng(lambda e, i=i, o=o: e.tensor_tensor(out=self.sz[i][:, o:o + 4, 0:nt], in0=self.sz[i][:, o:o + 4, 0:nt],
                                                    in1=self.big[i][:, o:o + 4, 0:nt], op=ALU.mult),
                r=[self.sz[i], self.big[i]], wp=[(self.sz[i], ("gate", g))])
        tokB = self.tok[1]
        sqv = tokB[:].bitcast(BF16).rearrange("p (c t) -> p c t", t=512)
        for g in range(4):
            st = self.sz[g // 2]
            ot = self.big[g // 2]
            self.rmsnorm(lambda c, g=g, st=st: (st, st[:, (g % 2) * 4 + c, 0:nt]), 4, 512,
                         lambda c, g=g: self.vcol(f"ssmw{l}", g * 4 + c),
                         lambda c, g=g, ot=ot: (ot, ot[:, (g % 2) * 4 + c, 0:nt]), nt,
                         lambda c: (tokB, sqv[:, c, 0:nt]))

    def mla(self, l, b, t0, nt, chunks, first):
        P = self.P
        W = self.wb["w_in"][l].h
        rhs_u = lambda k: (self.u.sub(k), self.u[:, k, 0:nt])
        tokA, tokB = self.tok
        cqf = self.sz[0][:].bitcast(F32).rearrange("p (a b) c -> p a (b c)", b=2)
        ckvf = tokA[:].rearrange("p (a t) -> p a t", t=512)
        sqv = tokB[:].bitcast(BF16).rearrange("p (c t) -> p c t", t=512)
        cqn = [self.sm[i][:].rearrange("p a b -> p (a b)") for i in range(4)]
        ckvn = [self.sm[4 + i][:].rearrange("p a b -> p (a b)") for i in range(2)]
        cos_t, sin_t = self.acc, self.xpre
        P.dma(cos_t, cos_t[0:64, 0:nt], self.cos_d, self.cos_d[:, t0:t0 + nt])
        P.dma(sin_t, sin_t[0:64, 0:nt], self.sin_d, self.sin_d[:, t0:t0 + nt])
        if first:
            P.dve(lambda e: e.memset(self.kmax2[:], 0.0), w=[self.kmax2])

        def ev_cq(tag, bk, ps):
            P.act(lambda e, ps=ps, tag=tag: e.copy(cqf[:, tag, 0:nt], ps), r=[bk], wp=[(self.sz[0], ("cq", tag))])
        self.dense(self.wb["w_in"][l], self.wtiles(W, C_CQ, 512, 8), 8, rhs_u, nt, ev_cq)

        def ev_ckv(tag, bk, ps):
            P.act(lambda e, ps=ps, tag=tag: e.copy(ckvf[:, tag, 0:nt], ps), r=[bk], wp=[(tokA, ("ckv", tag))])
        self.dense(self.wb["w_in"][l], self.wtiles(W, C_CKV, 256, 8), 8, rhs_u, nt, ev_ckv)
        self.rmsnorm(lambda c: (self.sz[0], cqf[:, c, 0:nt]), 4, 512, lambda c: self.vcol(f"qnw{l}", c),
                     lambda c: (self.sm[c], cqn[c][:, 0:nt]), nt, lambda c: (tokB, sqv[:, c, 0:nt]))
        self.rmsnorm(lambda c: (tokA, ckvf[:, c, 0:nt]), 2, 256, lambda c: self.vcol(f"kvnw{l}", c),
                     lambda c: (self.sm[4 + c], ckvn[c][:, 0:nt]), nt, lambda c: (tokB, sqv[:, c, 0:nt]))

        ropebuf = {}

        def rope_ev(kind, bk, ps, out_t, out_ap, key):
            if kind == "pe":
                t1 = self.nt_()
                P.dve(lambda e, t1=t1, ps=ps: e.tensor_tensor(out=t1[0:64, 0:nt], in0=ps, in1=cos_t[0:64, 0:nt],
                                                              op=ALU.mult), r=[bk, cos_t], w=[t1])
                ropebuf["t1"] = t1
            else:
                t1 = ropebuf["t1"]
                t2 = self.nt_()
                P.dve(lambda e, t2=t2, ps=ps: e.tensor_tensor(out=t2[0:64, 0:nt], in0=ps, in1=sin_t[0:64, 0:nt],
                                                              op=ALU.mult), r=[bk, sin_t], w=[t2])
                P.pool(lambda e, t1=t1, t2=t2: e.tensor_tensor(out=out_ap, in0=t1[0:64, 0:nt], in1=t2[0:64, 0:nt],
                                                               op=ALU.add), r=[t1, t2], wp=[(out_t, key)])

        def ev_kr(tag, bk, ps):
            rope_ev("pe" if tag == 0 else "sw", bk, ps, self.kpe, self.kpe[0:64, t0:t0 + nt], ("kpe", b))
        self.dense(self.wb["w_kr2"][l], [(self.wb["w_kr2"][l].h, 128, [(0, 64, 0), (64, 64, 1)])], 8, rhs_u, nt, ev_kr)
        bkk = self.bank[4]

        def ev_k(tag, bk, ps):
            P.act(lambda e, ps=ps, tag=tag: e.copy(self.kc[:, tag, t0:t0 + nt], ps), r=[bk],
                  wp=[(self.kc, ("k", b, tag))])
            P.pool(lambda e, tag=tag: e.tensor_tensor(out=sqv[:, tag % 4, 0:nt], in0=self.kc[:, tag, t0:t0 + nt],
                                                      in1=self.kc[:, tag, t0:t0 + nt], op=ALU.mult),
                   r=[self.kc], wp=[(tokB, ("ksq", tag % 4))])
            def later(tag=tag):
                P.pe(lambda e: e.matmul(bkk[0:65, 0:nt], lhsT=self.ones_b[:, 0:65], rhs=sqv[:, tag % 4, 0:nt],
                                        start=(tag == 0), stop=False), r=[tokB, self.ones_b], w=[bkk])
            return later
        self.dense(self.wb["w_uk"][l], self.wtiles(self.wb["w_uk"][l].h, 0, 1024, 2), 2, lambda k: (self.sm[4 + k], ckvn[k][:, 0:nt]),
                   nt, ev_k)
        P.pool(lambda e: e.tensor_tensor(out=sqv[0:64, 0, 0:nt], in0=self.kpe[0:64, t0:t0 + nt],
                                         in1=self.kpe[0:64, t0:t0 + nt], op=ALU.mult),
               r=[self.kpe], wp=[(tokB, ("ksq", 0))])
        P.pe(lambda e: e.matmul(bkk[0:65, 0:nt], lhsT=self.ones_b[0:64, 0:65], rhs=sqv[0:64, 0, 0:nt],
                                start=False, stop=True), r=[tokB, self.ones_b], w=[bkk])
        P.dve(lambda e: e.tensor_reduce(out=self.kmtmp[:, 0:1], in_=bkk[0:65, 0:nt], axis=AX.X, op=ALU.max),
              r=[bkk], w=[self.kmtmp])
        P.dve(lambda e: e.tensor_tensor(out=self.kmax2[:, :], in0=self.kmax2[:, :], in1=self.kmtmp[:, :], op=ALU.max),
              r=[self.kmtmp, self.kmax2], w=[self.kmax2])
        ws, wview = self.wload(self.wb["w_uv"][l], self.wb["w_uv"][l].h, 2, 1024)
        for (j, q, c0, ti) in chunks:
            for half in range(2):
                bk = self.nb()
                for k in range(2):
                    P.pe(lambda e, bk=bk, k=k, half=half, q=q, c0=c0:
                         e.matmul(bk[0:q, :], lhsT=ckvn[k][:, c0:c0 + q], rhs=wview[:, k, half * 512:(half + 1) * 512],
                                  start=(k == 0), stop=(k == 1)), r=[self.sm[4 + k], ws], w=[bk])
                P.act(lambda e, bk=bk, half=half, q=q, ti=ti: e.copy(self.vc[0:q, ti, half * 512:(half + 1) * 512],
                                                                    bk[0:q, :]),
                      r=[bk], wp=[(self.vc, ("v", ti, half))])
        qn = self.big[0]
        qpe = self.big[1]
        bkq = self.bank[5]
        tiles = []
        for hp in range(4):
            ch = []
            for hh in range(2):
                h = hp * 2 + hh
                ch += [(hh * 256, 128, ("n", h)), (hh * 256 + 128, 64, ("pe", h)), (hh * 256 + 192, 64, ("sw", h))]
            tiles.append((self.wb["w_uq2"][l].h[:, hp * 512:(hp + 1) * 512], 512, ch))

        def ev_q(tag, bk, ps):
            kind, h = tag
            if kind == "n":
                P.act(lambda e, ps=ps, h=h: e.copy(qn[:, h, 0:nt], ps), r=[bk], wp=[(qn, ("q", h))])
                P.pool(lambda e, h=h: e.tensor_tensor(out=sqv[:, h % 4, 0:nt], in0=qn[:, h, 0:nt], in1=qn[:, h, 0:nt],
                                                      op=ALU.mult), r=[qn], wp=[(tokB, ("qsq", h % 4))])
                def later(h=h):
                    P.pe(lambda e: e.matmul(bkq[0:65, 0:nt], lhsT=self.ones_b[:, 0:65], rhs=sqv[:, h % 4, 0:nt],
                                            start=True, stop=False), r=[tokB, self.ones_b], w=[bkq])
                return later
            else:
                rope_ev(kind, bk, ps, qpe, qpe[0:64, h, 0:nt], ("qpe", h))
                if kind == "sw":
                    def later2(h=h):
                        P.pool(lambda e: e.tensor_tensor(out=sqv[0:64, h % 4, 0:nt], in0=qpe[0:64, h, 0:nt],
                                                         in1=qpe[0:64, h, 0:nt], op=ALU.mult),
                               r=[qpe], wp=[(tokB, ("qsq", h % 4))])
                        P.pe(lambda e: e.matmul(bkq[0:65, 0:nt], lhsT=self.ones_b[0:64, 0:65], rhs=sqv[0:64, h % 4, 0:nt],
                                                start=False, stop=True), r=[tokB, self.ones_b], w=[bkq])
                        tq = self.nt_()
                        P.act(lambda e: e.activation(tq[64:65, 0:nt], bkq[64:65, 0:nt], AF.Sqrt,
                                                     scale=self.kmax2[64:65, 0:1]), r=[bkq, self.kmax2], w=[tq])
                        P.dve(lambda e: e.tensor_scalar(out=qpe[64:65, h, 0:nt], in0=tq[64:65, 0:nt], scalar1=-1.0,
                                                        scalar2=None, op0=ALU.mult), r=[tq], wp=[(qpe, ("qb", h))])
                    return later2
            return None
        self.dense(self.wb["w_uq2"][l], tiles, 4, lambda k: (self.sm[k], cqn[k][:, 0:nt]), nt, ev_q)

        ymla = self.big[2]
        ktiles = [(0, 0, NMETA)] + [(i, NMETA + 128 * (i - 1), 128) for i in range(1, self.ntile)]
        ktiles = [kt for kt in ktiles if kt[1] < t0 + nt]
        items = [(h, idx) for h in range(8) for idx in range(len(ktiles))]
        nk = len(ktiles)

        def emit_S(h, idx):
            ti, k0, kn = ktiles[idx]
            qs = max(0, k0 - t0)
            ncol = nt - qs
            bks = self.nb()
            P.pe(lambda e: e.matmul(bks[0:kn, 0:ncol], lhsT=self.kc[:, h, k0:k0 + kn], rhs=qn[:, h, qs:nt],
                                    start=True, stop=False), r=[self.kc, qn], w=[bks])
            P.pe(lambda e: e.matmul(bks[0:kn, 0:ncol], lhsT=self.kpe[0:65, k0:k0 + kn], rhs=qpe[0:65, h, qs:nt],
                                    start=False, stop=True), r=[self.kpe, qpe], w=[bks])
            pt = self.PT[self.pti % len(self.PT)]
            self.pti += 1
            P.act(lambda e: e.activation(pt[0:kn, 0:ncol], bks[0:kn, 0:ncol], AF.Exp, scale=SCALE), r=[bks], w=[pt])
            if k0 >= t0:
                P.pool(lambda e: e.tensor_tensor(out=pt[0:kn, 0:kn], in0=pt[0:kn, 0:kn], in1=self.tri_b[0:kn, 0:kn],
                                                 op=ALU.mult), r=[pt, self.tri_b], w=[pt])
            return pt

        def emit_PV(h, idx, pt):
            ti, k0, kn = ktiles[idx]
            qs = max(0, k0 - t0)
            ncol = nt - qs
            bo = self.bank[4 + 2 * (h % 2)]
            bd = self.bank[5 + 2 * (h % 2)]
            last = idx == nk - 1
            P.pe(lambda e: e.matmul(bo[:, qs:nt], lhsT=self.vc[0:kn, ti, h * 128:(h + 1) * 128], rhs=pt[0:kn, 0:ncol],
                                    start=(idx == 0), stop=last), r=[self.vc, pt], w=[bo])
            P.pe(lambda e: e.matmul(bd[:, qs:nt], lhsT=self.ones_b[0:kn, :], rhs=pt[0:kn, 0:ncol],
                                    start=(idx == 0), stop=last), r=[self.ones_b, pt], w=[bd])
            if last:
                rd = self.nt_()
                P.dve(lambda e: e.reciprocal(rd[:, 0:nt], bd[:, 0:nt]), r=[bd], w=[rd])
                P.dve(lambda e: e.tensor_tensor(out=ymla[:, h, 0:nt], in0=bo[:, 0:nt], in1=rd[:, 0:nt], op=ALU.mult),
                      r=[bo, rd], wp=[(ymla, ("y", h))])

        pend = []
        for (h, idx) in items:
            pend.append((h, idx, emit_S(h, idx)))
            if len(pend) > 2:
                emit_PV(*pend.pop(0))
        while pend:
            emit_PV(*pend.pop(0))

    def gate(self, l, c0, nt, dst):
        P = self.P
        rhs_u = lambda k: (self.u.sub(k), self.u[:, k, 0:nt])

        def ev(tag, bk, ps):
            P.act(lambda e, ps=ps, tag=tag: e.activation(dst[:, tag, 0:nt], ps, AF.Sigmoid), r=[bk],
                  wp=[(dst, ("g", tag))])
        self.dense(self.wb["w_in"][l], self.wtiles(self.wb["w_in"][l].h, c0, 1024, 8), 8, rhs_u, nt, ev)

    def block(self, l, s, b, t0, nt):
        P = self.P
        first = (b == 0)
        if t0 == 0:
            chunks = [(0, NMETA, 0, 0)]
        else:
            chunks = [(j, 128, 128 * j, 1 + 4 * (b - 1) + j) for j in range(4)]
        self.load_block(l, s, t0, nt)
        hsrc = lambda c: (self.h, self.h[:, c, 0:nt])
        usrc = lambda c: (self.u.sub(c), self.u[:, c, 0:nt])
        self.rmsnorm(hsrc, 8, 1024, lambda c: self.vcol(f"nmix{l}", c), usrc, nt, usrc)
        if first:
            P.dve(lambda e: e.memset(self.state_f[:], 0.0), w=[self.state_f])
            P.pool(lambda e: e.memset(self.state_b[:], 0.0), w=[self.state_b])
        if first or b == 1:
            t1 = self.nt_()
            P.act(lambda e, t1=t1: e.activation(t1[0:32, 0:1], self.vcol(f"alog{l}", 0, 32), AF.Exp),
                  r=[self.vecs], w=[t1])
            P.dve(lambda e, t1=t1: e.tensor_scalar(out=self.negA[:, 0:1], in0=t1[0:32, 0:1], scalar1=-1.0, scalar2=None,
                                                   op0=ALU.mult), r=[t1], w=[self.negA])
        self.ssd_inproj(l, nt, first)
        for (j, q, c0, ti) in chunks:
            if nt == 512:
                self.cast_some(2)
            self.ssd_chunk(l, j, q, c0, nt)
        self.ssd_finish(l, nt)
        gs = self.sz[0]
        mixed = self.sz[1]
        self.gate(l, C_GS, nt, gs)

        def ev_bs(tag, bk, ps):
            P.dve(lambda e, ps=ps, tag=tag: e.tensor_tensor(out=mixed[:, tag, 0:nt], in0=ps, in1=gs[:, tag, 0:nt],
                                                           op=ALU.mult), r=[bk, gs], wp=[(mixed, ("m", tag))])
        self.dense(self.wb["w_bs"][l], self.wtiles(self.wb["w_bs"][l].h, 0, 1024, 16), 16,
                   lambda k: (self.big[k // 8], self.big[k // 8][:, k % 8, 0:nt]), nt, ev_bs)
        self.mla(l, b, t0, nt, chunks, first)
        if nt == 512:
            self.dump('ymla', self.big[2], self.big[2][:, :, :], BF16)
            self.dump('yn0', self.big[0], self.big[0][:, :, :], BF16)
        gm = self.sz[0]
        self.gate(l, C_GM, nt, gm)
        ymla = self.big[2]

        def ev_bm(tag, bk, ps):
            t1 = self.nt_()
            P.dve(lambda e, ps=ps, tag=tag, t1=t1: e.tensor_tensor(out=t1[:, 0:nt], in0=ps, in1=gm[:, tag, 0:nt],
                                                                  op=ALU.mult), r=[bk, gm], w=[t1])
            P.pool(lambda e, tag=tag, t1=t1: e.tensor_tensor(out=mixed[:, tag, 0:nt], in0=mixed[:, tag, 0:nt],
                                                            in1=t1[:, 0:nt], op=ALU.add),
                   r=[t1, mixed], wp=[(mixed, ("m2", tag))])
        self.dense(self.wb["w_bm"][l], self.wtiles(self.wb["w_bm"][l].h, 0, 1024, 8), 8, lambda k: (ymla, ymla[:, k, 0:nt]), nt, ev_bm)

        def ev_res(tag, bk, ps):
            P.dve(lambda e, ps=ps, tag=tag: e.tensor_tensor(out=self.h[:, tag, 0:nt], in0=self.h[:, tag, 0:nt], in1=ps,
                                                           op=ALU.add), r=[bk, self.h], wp=[(self.h, ("r", tag))])
        self.dense(self.wb["w_out"][l], self.wtiles(self.wb["w_out"][l].h, 0, 1024, 8), 8, lambda k: (mixed, mixed[:, k, 0:nt]), nt,
                   ev_res)
        self.rmsnorm(hsrc, 8, 1024, lambda c: self.vcol(f"nmlp{l}", c), usrc, nt, usrc)
        abuf = [self.big[0], self.big[1], self.big[2], self.sz[0]]

        def ev_up(tag, bk, ps):
            t1 = self.nt_()
            P.act(lambda e, ps=ps, t1=t1: e.activation(t1[:, 0:nt], ps, AF.Relu), r=[bk], w=[t1])
            at = abuf[tag // 8]
            P.pool(lambda e, t1=t1, at=at, tag=tag: e.tensor_tensor(out=at[:, tag % 8, 0:nt], in0=t1[:, 0:nt],
                                                                   in1=t1[:, 0:nt], op=ALU.mult),
                   r=[t1], wp=[(at, ("a", tag % 8))])
        self.dense(self.wb["w_up"][l], self.wtiles(self.wb["w_up"][l].h, 0, 4096, 8), 8, usrc, nt, ev_up)
        self.dense(self.wb["w_dn"][l], self.wtiles(self.wb["w_dn"][l].h, 0, 1024, 32), 32,
                   lambda k: (abuf[k // 8], abuf[k // 8][:, k % 8, 0:nt]), nt, ev_res)
        if l < self.depth - 1:
            P.dma(self.h1, self.h1[:, :, t0:t0 + nt], self.h, self.h[:, :, 0:nt])
        elif t0 > 0:
            self.rmsnorm(hsrc, 8, 1024, lambda c: self.vcol("fnw", c), hsrc, nt, usrc)
            for j in range(4):
                ot = self.tok[j % 2]
                for half in range(2):
                    bk = self.nb()
                    for cc in range(4):
                        c = half * 4 + cc
                        P.pe(lambda e, bk=bk, c=c, cc=cc, j=j:
                             e.transpose(bk[:, cc * 128:(cc + 1) * 128], self.h[:, c, j * 128:(j + 1) * 128],
                                         self.ident_f[:]), r=[self.h, self.ident_f], w=[bk])
                    P.act(lambda e, bk=bk, ot=ot, half=half: e.copy(ot[:, half * 512:(half + 1) * 512], bk[:, :]),
                          r=[bk], wp=[(ot, ("o", half))])
                a = t0 - NMETA + j * 128
                P.dma(self.out, self.out[s, a:a + 128, :], ot, ot[:, :], partial=(s, a))

    def meta_save(self, l):
        P = self.P
        P.dma(self.mk[l], self.mk[l][:], self.kc, self.kc[:, :, 0:NMETA])
        P.dma(self.mv[l], self.mv[l][:], self.vc, self.vc[0:NMETA, 0, :])
        P.dma(self.mp[l], self.mp[l][:], self.kpe, self.kpe[0:64, 0:NMETA])
        P.dma(self.ms[l], self.ms[l][:], self.state_f, self.state_f[:, :])
        P.dma(self.mh[l], self.mh[l][:], list(self.halo_c), self.halo[:, :, :].rearrange("p c k -> p (c k)"))
        P.dma(self.mm[l], self.mm[l][:], self.kmax2, self.kmax2[:, :])

    def meta_restore(self, l):
        P = self.P
        P.dma(self.kc, self.kc[:, :, 0:NMETA], self.mk[l], self.mk[l][:], partial=("k", 0, "meta"))
        P.dma(self.vc, self.vc[0:NMETA, 0, :], self.mv[l], self.mv[l][:], partial=("v", 0, "meta"))
        P.dma(self.kpe, self.kpe[0:64, 0:NMETA], self.mp[l], self.mp[l][:], partial=("kpe", 0))
        P.dma(self.state_f, self.state_f[:, :], self.ms[l], self.ms[l][:])
        P.act(lambda e: e.copy(self.state_b[:, :], self.state_f[:, :]), r=[self.state_f], w=[self.state_b])
        stg = self.dta[:, :, :].rearrange("p a b -> p (a b)")
        P.dma(self.dta, stg[:, 0:72], self.mh[l], self.mh[l][:])
        P.dve(lambda e: e.tensor_copy(self.halo[:, :, :].rearrange("p c k -> p (c k)"), stg[:, 0:72]),
              r=[self.dta], w=list(self.halo_c))
        P.dma(self.kmax2, self.kmax2[:, :], self.mm[l], self.mm[l][:])

    def build(self):
        self.consts()
        for s in range(self.nseq):
            for l in range(self.depth):
                if s == 0 and l + 1 < self.depth:
                    self.cast_queue(l + 1, ["w_in", "w_kr2", "w_bs", "w_uk", "w_uv", "w_uq2", "w_bm", "w_out", "w_up", "w_dn"])
                if s == 0:
                    self.block(l, s, 0, 0, NMETA)
                    if self.nseq > 1:
                        self.meta_save(l)
                else:
                    self.meta_restore(l)
                for b in range(1, self.nblk + 1):
                    self.block(l, s, b, NMETA + 512 * (b - 1), 512)
                self.cast_some(len(self.castq))
        self.P.finish(final_waits=[self.out] + self.dbg_outs)
        return self.nc


def prep_inputs(inp, depth, nblk, seqs):
    f = lambda a: np.ascontiguousarray(np.asarray(a, dtype=np.float32))
    Lt = NMETA + 512 * nblk
    voff, NV = vec_layout(depth)
    vecs = np.zeros((128, NV), np.float32)

    def put(name, arr):
        arr = np.asarray(arr, np.float32)
        n = arr.shape[0] // 128
        vecs[:, voff[name]:voff[name] + n] = arr.reshape(n, 128).T
    for l in range(depth):
        put(f"nmix{l}", inp["norm_mix_w"][l])
        put(f"nmlp{l}", inp["norm_mlp_w"][l])
        for k in range(4):
            cw = np.asarray(inp["conv_w"][l][k], np.float32)
            vecs[:, voff[f"convw{l}"] + k * 24:voff[f"convw{l}"] + (k + 1) * 24] = cw.reshape(24, 128).T
        put(f"convb{l}", inp["conv_b"][l])
        put(f"ssmw{l}", inp["ssm_norm_w"][l])
        put(f"dskip{l}", np.repeat(np.asarray(inp["d_skip"][l], np.float32), 64))
        put(f"qnw{l}", inp["q_norm_w"][l])
        put(f"kvnw{l}", inp["kv_norm_w"][l])
        vecs[0:32, voff[f"dtb{l}"]] = np.asarray(inp["dt_bias"][l], np.float32)
        vecs[0:32, voff[f"alog{l}"]] = np.asarray(inp["a_log"][l], np.float32)
    put("fnw", inp["final_norm_w"])
    vecs[:, voff["eps"]] = EPS
    vecs[:, voff["one"]] = 1.0
    w_in = f(inp["w_in"])[:depth]
    kr = w_in[:, :, C_KR:C_KR + 64]
    w_kr2 = f(np.concatenate([kr, kr[:, :, 32:64], kr[:, :, 0:32]], axis=2))
    wuq = f(inp["w_uq"])[:depth].reshape(depth, 512, 8, 192)
    nope, pe = wuq[..., :128], wuq[..., 128:]
    pesw = np.concatenate([pe[..., 32:], pe[..., :32]], axis=-1)
    w_uq2 = f(np.concatenate([nope, pe, pesw], axis=-1).reshape(depth, 512, 2048))
    wukv = f(inp["w_ukv"])[:depth].reshape(depth, 256, 8, 256)
    w_uk = f(wukv[..., :128].reshape(depth, 256, 1024))
    w_uv = f(wukv[..., 128:].reshape(depth, 256, 1024))
    inv = (10000.0 ** (-np.arange(0, 64, 2, dtype=np.float32) / np.float32(64))).astype(np.float32)
    ang = (np.arange(Lt, dtype=np.float32)[:, None] * inv[None, :]).astype(np.float32)
    cos, sin = np.cos(ang).astype(np.float32), np.sin(ang).astype(np.float32)
    cos2 = f(np.concatenate([cos, cos], axis=1).T)
    sin2 = f(np.concatenate([-sin, sin], axis=1).T)
    common = {
        "meta": f(inp["meta_tokens"]), "w_in": w_in, "w_kr2": w_kr2, "w_uq2": w_uq2, "w_uk": w_uk, "w_uv": w_uv,
        "w_bs": f(inp["w_branch_ssm"])[:depth], "w_bm": f(inp["w_branch_mla"])[:depth], "w_out": f(inp["w_out"])[:depth],
        "w_up": f(inp["w_mlp_up"])[:depth], "w_dn": f(inp["w_mlp_down"])[:depth], "vecs": vecs, "cos2": cos2, "sin2": sin2,
    }
    x = f(inp["x"])
    maps = []
    for sl in seqs:
        m = dict(common)
        m["x"] = np.ascontiguousarray(x[sl, :512 * nblk, :])
        maps.append(m)
    return maps


_CACHE = {}


def kernel(**inputs):
    nseq, depth, nblk, ncores = 2, 2, 4, 8
    mk = MK(nseq, depth, nblk)
    nc = mk.build()
    seqs = [slice(i * nseq, (i + 1) * nseq) for i in range(ncores)]
    maps = prep_inputs(inputs, depth, nblk, seqs)
    res = run_bass_kernel_spmd(nc, maps, core_ids=list(range(ncores)))
    out = np.concatenate([np.asarray(r["out"], dtype=np.float32) for r in res.results], axis=0)
    return out
```
